# Optimizing a Trainium2 kernel written in Bass

```python
import jax, jax.numpy as jnp
from jax import lax
import numpy as np

D_MODEL = 1024
BATCH = 32
SEQ = 256
DEPTH = 4
DEC_BATCH = 2
DEC_SEQ = 4096
PAST_LEN = 512

GRID_W = 64
HEAD_DIM = 64
NA_HEADS = 4
NA_KH = 8
NA_KW = 16
NA_QCOLS = 16
NA_KCOLS = 32
GQA_Q_HEADS = 8
GQA_KV_HEADS = 2
Q_BLOCK = 128
ROPE_THETA = 10000.0
GLA_HEADS = 4
GLA_DK = 64
GLA_DV = 64
GLA_RANK = 16
GLA_TAU = 16.0
GLA_CHUNK = 16
D_FF = 2816
CONV_W = 3
EPS = 1e-6
NEG_INF = -1e30

W_A = NA_HEADS * HEAD_DIM
W_BQ = GQA_Q_HEADS * HEAD_DIM
W_BKV = GQA_KV_HEADS * HEAD_DIM
W_CK = GLA_HEADS * GLA_DK
W_CV = GLA_HEADS * GLA_DV
IN_SIZES = (W_A, W_A, W_A, W_BQ, W_BKV, W_BKV, W_CK, W_CK, W_CV, W_CV, GLA_RANK, GLA_RANK, D_MODEL, D_MODEL, D_MODEL)
D_IN = sum(IN_SIZES)

kernel_name = 'hybrid_diffusion_trunk_step'


def rms_norm(x, g):
    xf = x.astype(jnp.float32)
    y = xf * lax.rsqrt(jnp.mean(xf * xf, axis=-1, keepdims=True) + EPS)
    return (y * g.astype(jnp.float32)).astype(x.dtype)


def axial_rope(T):
    t = np.arange(T)
    n_freq = HEAD_DIM // 4
    inv_freq = ROPE_THETA ** (-np.arange(n_freq) / n_freq)
    ang = np.concatenate([(t // GRID_W)[:, None] * inv_freq, (t % GRID_W)[:, None] * inv_freq], axis=-1)
    return jnp.asarray(np.cos(ang), jnp.float32), jnp.asarray(np.sin(ang), jnp.float32)


def apply_rope(x, cos, sin):
    half = x.shape[-1] // 2
    x1 = x[..., :half].astype(jnp.float32)
    x2 = x[..., half:].astype(jnp.float32)
    c = cos[None, :, None, :]
    s = sin[None, :, None, :]
    return jnp.concatenate([x1 * c - x2 * s, x1 * s + x2 * c], axis=-1).astype(x.dtype)


def block_attention(q, k, v):
    B, Tq, Hq, hd = q.shape
    Hkv = k.shape[2]
    G = Hq // Hkv
    nb = Tq // Q_BLOCK
    qb = jnp.moveaxis(q.reshape(B, nb, Q_BLOCK, Hkv, G, hd), 1, 0)
    scale = hd ** -0.5

    def one_block(qi):
        s = jnp.einsum('bqhgd,bkhd->bhgqk', qi, k, preferred_element_type=jnp.float32) * scale
        pr = jax.nn.softmax(s, axis=-1).astype(v.dtype)
        return jnp.einsum('bhgqk,bkhd->bqhgd', pr, v)

    o = lax.map(one_block, qb)
    return jnp.moveaxis(o, 0, 1).reshape(B, Tq, Hq * hd)


def neighborhood_attention(q, k, v, k_ctx, v_ctx, rpb):
    B, T, H, hd = q.shape
    rows = T // GRID_W
    kh = min(NA_KH, rows)
    ncb = GRID_W // NA_QCOLS
    r = np.arange(rows)
    key_rows = np.clip(r - kh // 2, 0, rows - kh)[:, None] + np.arange(kh)[None, :]
    qcols = np.arange(GRID_W).reshape(ncb, NA_QCOLS)
    key_cols = np.clip(qcols[:, 0] - NA_KW // 2, 0, GRID_W - NA_KCOLS)[:, None] + np.arange(NA_KCOLS)[None, :]
    win0 = np.clip(qcols - NA_KW // 2, 0, GRID_W - NA_KW)
    in_win = (key_cols[:, None, :] >= win0[:, :, None]) & (key_cols[:, None, :] < win0[:, :, None] + NA_KW)
    dr_idx = key_rows - r[:, None] + NA_KH - 1
    dc_idx = np.clip(key_cols[:, None, :] - qcols[:, :, None] + NA_KW - 1, 0, 2 * NA_KW - 2)
    bias = rpb[:, dr_idx[:, :, None, None, None], dc_idx[None, None]]
    bias = jnp.where(in_win[None, None, None], bias.astype(jnp.float32), NEG_INF)
    bias = bias.transpose(1, 3, 0, 4, 2, 5)

    scale = hd ** -0.5
    qg = q.reshape(B, rows, ncb, NA_QCOLS, H, hd)
    kgrid = k.reshape(B, rows, GRID_W, H, hd)
    vgrid = v.reshape(B, rows, GRID_W, H, hd)
    gr = key_rows[:, :, None, None]
    gc = key_cols[None, None]
    k_blk = kgrid[:, gr, gc]
    v_blk = vgrid[:, gr, gc]
    s_loc = jnp.einsum('brnqhd,brknchd->brnhqkc', qg, k_blk, preferred_element_type=jnp.float32) * scale + bias[None]
    s_ctx = jnp.einsum('brnqhd,blhd->brnhql', qg, k_ctx, preferred_element_type=jnp.float32) * scale
    n_loc = kh * NA_KCOLS
    s = jnp.concatenate([s_loc.reshape(s_loc.shape[:5] + (n_loc,)), s_ctx], axis=-1)
    pr = jax.nn.softmax(s, axis=-1).astype(v.dtype)
    p_loc = pr[..., :n_loc].reshape(s_loc.shape)
    p_ctx = pr[..., n_loc:]
    o = (jnp.einsum('brnhqkc,brknchd->brnqhd', p_loc, v_blk)
         + jnp.einsum('brnhql,blhd->brnqhd', p_ctx, v_ctx))
    return o.reshape(B, T, H * hd)


def gla_scan(q, k, v, log_a, s0):
    B, H, T, dk = q.shape
    dv = v.shape[-1]
    C = GLA_CHUNK
    n = T // C
    f32 = jnp.float32
    q = q.astype(f32).reshape(B, H, n, C, dk)
    k = k.astype(f32).reshape(B, H, n, C, dk)
    v = v.astype(f32).reshape(B, H, n, C, dv)
    b = jnp.cumsum(log_a.astype(f32).reshape(B, H, n, C, dk), axis=3)
    b_last = b[:, :, :, -1:, :]
    causal = np.tril(np.ones((C, C), dtype=bool))[:, :, None]
    decay = jnp.exp(jnp.where(causal, b[:, :, :, :, None, :] - b[:, :, :, None, :, :], -jnp.inf))
    att = jnp.einsum('bhntd,bhnsd,bhntsd->bhnts', q, k, decay)
    o_intra = jnp.einsum('bhnts,bhnsv->bhntv', att, v)
    kv = jnp.einsum('bhncd,bhncv->nbhdv', k * jnp.exp(b_last - b), v)
    chunk_decay = jnp.moveaxis(jnp.exp(b_last[:, :, :, 0, :]), 2, 0)

    def step(state, inp):
        dec, kv_n = inp
        return dec[..., None] * state + kv_n, state

    s_final, s_prev = lax.scan(step, s0.astype(f32), (chunk_decay, kv))
    o_inter = jnp.einsum('bhncd,nbhdv->bhncv', q * jnp.exp(b), s_prev)
    return (o_intra + o_inter).reshape(B, H, T, dv), s_final


def bidir_gla(q, k, v, la_f, la_b, s0_f, s0_b):
    o_f, s_f = gla_scan(q, k, v, la_f, s0_f)
    flip = lambda t: jnp.flip(t, axis=2)
    o_b, s_b = gla_scan(flip(q), flip(k), flip(v), flip(la_b), s0_b)
    return o_f + flip(o_b), s_f, s_b


def mixer_sublayer(h, p, ctx):
    B, T, _ = h.shape
    f32 = jnp.float32
    splits = np.cumsum(IN_SIZES)[:-1].tolist()
    (qa, ka, va, qb, kb, vb, qc, kc, vc, rc, zf, zb, ga, gb, gc) = jnp.split(h @ p['w_in'], splits, axis=-1)
    qa = rms_norm(qa.reshape(B, T, NA_HEADS, HEAD_DIM), p['na_q_norm'])
    ka = rms_norm(ka.reshape(B, T, NA_HEADS, HEAD_DIM), p['na_k_norm'])
    va = va.reshape(B, T, NA_HEADS, HEAD_DIM)
    qb = rms_norm(qb.reshape(B, T, GQA_Q_HEADS, HEAD_DIM), p['gqa_q_norm'])
    kb = rms_norm(kb.reshape(B, T, GQA_KV_HEADS, HEAD_DIM), p['gqa_k_norm'])
    vb = vb.reshape(B, T, GQA_KV_HEADS, HEAD_DIM)
    to_heads = lambda t, d: t.reshape(B, T, GLA_HEADS, d).transpose(0, 2, 1, 3)
    qc = to_heads(qc, GLA_DK) * GLA_DK ** -0.5
    kc = to_heads(kc, GLA_DK)
    vc = to_heads(vc, GLA_DV)
    la_f = to_heads(jax.nn.log_sigmoid((zf @ p['gla_wg2'][0] + p['gla_bg'][0]).astype(f32)) / GLA_TAU, GLA_DK)
    la_b = to_heads(jax.nn.log_sigmoid((zb @ p['gla_wg2'][1] + p['gla_bg'][1]).astype(f32)) / GLA_TAU, GLA_DK)
    if ctx is None:
        oa = block_attention(qa, ka, va)
        ob = block_attention(qb, kb, vb)
        s0 = jnp.zeros((B, GLA_HEADS, GLA_DK, GLA_DV), f32)
        oc, s_f, s_b = bidir_gla(qc, kc, vc, la_f, la_b, s0, s0)
        new_ctx = (ka, va, kb, vb, s_f.astype(h.dtype), s_b.astype(h.dtype))
    else:
        ka_c, va_c, kb_c, vb_c, s0_f, s0_b = ctx
        oa = neighborhood_attention(qa, ka, va, ka_c, va_c, p['na_rpb'])
        cos, sin = axial_rope(T)
        ob = block_attention(apply_rope(qb, cos, sin),
                             jnp.concatenate([apply_rope(kb, cos, sin), kb_c], axis=1),
                             jnp.concatenate([vb, vb_c], axis=1))
        oc, _, _ = bidir_gla(qc, kc, vc, la_f, la_b, s0_f, s0_b)
        new_ctx = None
    oc = rms_norm(oc.transpose(0, 2, 1, 3), p['gla_out_norm']).reshape(B, T, W_CV).astype(h.dtype) * jax.nn.silu(rc)
    merged = (jax.nn.sigmoid(ga) * (oa @ p['w_branch_a'])
              + jax.nn.sigmoid(gb) * (ob @ p['w_branch_b'])
              + jax.nn.sigmoid(gc) * (oc @ p['w_branch_c']))
    return merged @ p['w_out'], new_ctx


def conv_ffn(h, p):
    u = h @ p['ffn_w_up']
    T = u.shape[1]
    pad = CONV_W // 2
    up = jnp.pad(u, ((0, 0), (pad, pad), (0, 0)))
    w = p['ffn_conv_w']
    acc = p['ffn_conv_b'] + up[:, 0:T] * w[0]
    for j in range(1, CONV_W):
        acc = acc + up[:, j:j + T] * w[j]
    a, g = jnp.split(acc, 2, axis=-1)
    return (a * jax.nn.silu(g)) @ p['ffn_w_down']


def trunk_layer(x, cond, p, ctx):
    mod = jax.nn.silu(cond) @ p['w_mod'] + p['b_mod']
    sh1, sc1, g1, sh2, sc2, g2 = [m[:, None, :] for m in jnp.split(mod, 6, axis=-1)]
    h = rms_norm(x, p['g_attn']) * (1 + sc1) + sh1
    a, new_ctx = mixer_sublayer(h, p, ctx)
    x = x + g1 * a
    h = rms_norm(x, p['g_ffn']) * (1 + sc2) + sh2
    x = x + g2 * conv_ffn(h, p)
    return x, new_ctx


def setup_inputs(seed: int = 0) -> dict:
    key = jax.random.key(seed)
    ks = iter(jax.random.split(key, 40))
    nrm = lambda shape, s: jax.random.normal(next(ks), shape, jnp.float32) * s
    L = DEPTH
    D = D_MODEL
    return {
        'x_prompt': nrm((BATCH, SEQ, D), 1.0),
        'x_sample': nrm((DEC_BATCH, DEC_SEQ, D), 1.0),
        'cache_na_k': nrm((DEC_BATCH, L, PAST_LEN, NA_HEADS, HEAD_DIM), 1.0),
        'cache_na_v': nrm((DEC_BATCH, L, PAST_LEN, NA_HEADS, HEAD_DIM), 1.0),
        'cache_gqa_k': nrm((DEC_BATCH, L, PAST_LEN, GQA_KV_HEADS, HEAD_DIM), 1.0),
        'cache_gqa_v': nrm((DEC_BATCH, L, PAST_LEN, GQA_KV_HEADS, HEAD_DIM), 1.0),
        'state_gla_fwd': nrm((DEC_BATCH, L, GLA_HEADS, GLA_DK, GLA_DV), 1.0),
        'state_gla_bwd': nrm((DEC_BATCH, L, GLA_HEADS, GLA_DK, GLA_DV), 1.0),
        'c': nrm((DEC_BATCH, D), 1.0),
        'c_ctx': nrm((D,), 1.0),
        'w_mod': nrm((L, D, 6 * D), D ** -0.5),
        'b_mod': nrm((L, 6 * D), 0.02),
        'g_attn': 1.0 + nrm((L, D), 0.01),
        'g_ffn': 1.0 + nrm((L, D), 0.01),
        'w_in': nrm((L, D, D_IN), D ** -0.5),
        'na_q_norm': 1.0 + nrm((L, HEAD_DIM), 0.01),
        'na_k_norm': 1.0 + nrm((L, HEAD_DIM), 0.01),
        'na_rpb': nrm((L, NA_HEADS, 2 * NA_KH - 1, 2 * NA_KW - 1), 0.1),
        'gqa_q_norm': 1.0 + nrm((L, HEAD_DIM), 0.01),
        'gqa_k_norm': 1.0 + nrm((L, HEAD_DIM), 0.01),
        'gla_wg2': nrm((L, 2, GLA_RANK, W_CK), GLA_RANK ** -0.5),
        'gla_bg': nrm((L, 2, W_CK), 0.1),
        'gla_out_norm': 1.0 + nrm((L, GLA_DV), 0.01),
        'w_branch_a': nrm((L, W_A, D), W_A ** -0.5),
        'w_branch_b': nrm((L, W_BQ, D), W_BQ ** -0.5),
        'w_branch_c': nrm((L, W_CV, D), W_CV ** -0.5),
        'w_out': nrm((L, D, D), D ** -0.5),
        'ffn_w_up': nrm((L, D, 2 * D_FF), D ** -0.5),
        'ffn_conv_w': nrm((L, CONV_W, 2 * D_FF), CONV_W ** -0.5),
        'ffn_conv_b': nrm((L, 2 * D_FF), 0.01),
        'ffn_w_down': nrm((L, D_FF, D), D_FF ** -0.5),
    }


def reference(x_prompt, x_sample, cache_na_k, cache_na_v, cache_gqa_k, cache_gqa_v, state_gla_fwd, state_gla_bwd,
              c, c_ctx, w_mod, b_mod, g_attn, g_ffn, w_in, na_q_norm, na_k_norm, na_rpb, gqa_q_norm, gqa_k_norm,
              gla_wg2, gla_bg, gla_out_norm, w_branch_a, w_branch_b, w_branch_c, w_out,
              ffn_w_up, ffn_conv_w, ffn_conv_b, ffn_w_down):
    stacked = dict(w_mod=w_mod, b_mod=b_mod, g_attn=g_attn, g_ffn=g_ffn, w_in=w_in,
                   na_q_norm=na_q_norm, na_k_norm=na_k_norm, na_rpb=na_rpb,
                   gqa_q_norm=gqa_q_norm, gqa_k_norm=gqa_k_norm,
                   gla_wg2=gla_wg2, gla_bg=gla_bg, gla_out_norm=gla_out_norm,
                   w_branch_a=w_branch_a, w_branch_b=w_branch_b, w_branch_c=w_branch_c, w_out=w_out,
                   ffn_w_up=ffn_w_up, ffn_conv_w=ffn_conv_w, ffn_conv_b=ffn_conv_b, ffn_w_down=ffn_w_down)
    ctx_cond = c_ctx[None, :]
    y_prompt = x_prompt
    y_sample = x_sample
    new = ([], [], [], [], [], [])
    for l in range(DEPTH):
        p = {name: arr[l] for name, arr in stacked.items()}
        y_prompt, ctx_l = trunk_layer(y_prompt, ctx_cond, p, None)
        for lst, t in zip(new, ctx_l):
            lst.append(t)
        cache_l = (cache_na_k[:, l], cache_na_v[:, l], cache_gqa_k[:, l], cache_gqa_v[:, l],
                   state_gla_fwd[:, l], state_gla_bwd[:, l])
        y_sample, _ = trunk_layer(y_sample, c, p, cache_l)
    new_na_k, new_na_v, new_gqa_k, new_gqa_v, new_gla_fwd, new_gla_bwd = [jnp.stack(t, axis=1) for t in new]
    return (y_prompt, y_sample, new_na_k, new_na_v, new_gqa_k, new_gqa_v, new_gla_fwd, new_gla_bwd)
```

```python
from contextlib import ExitStack
import numpy as np
import concourse.bass as bass
import concourse.mybir as mybir
from concourse.bass_utils import run_bass_kernel_spmd

F32 = mybir.dt.float32
BF16 = mybir.dt.bfloat16
AF = mybir.ActivationFunctionType
ALU = mybir.AluOpType

NDMA = 24
L = 4
D = 1024
NT = 1024
SEQ = 256
DFF = 2816
NFF = 22
EPS = 1e-6
NCORES = 8


class _Rec:
    def __init__(self):
        self.calls = []

    def __getattr__(self, name):
        def f(*a, **k):
            self.calls.append((name, a, k))
        return f


class Emit:
    def __init__(self, nc, es):
        self.nc = nc
        self.engs = {'pe': nc.tensor, 'act': nc.scalar, 'dve': nc.vector, 'pool': nc.gpsimd, 'sp': nc.sync}
        self.thunks = {e: [] for e in self.engs}
        self.seq = {e: 0 for e in self.engs}
        self.sem = {e: es.enter_context(nc.semaphore("s_" + e)) for e in self.engs}
        self.dsem = [es.enter_context(nc.semaphore("d%d" % i)) for i in range(NDMA)]
        self.dcnt = [0] * NDMA
        self.dnext2 = [0, 0]
        self.waited = {}
        self.last_w = {}
        self.readers = {}

    def _deps(self, reads, writes):
        deps = {}

        def add(p):
            if p is None:
                return
            prod, val = p
            if deps.get(prod, 0) < val:
                deps[prod] = val

        for k in reads:
            add(self.last_w.get(k))
        for k in writes:
            add(self.last_w.get(k))
            for r in self.readers.get(k, ()):
                add(r)
        return deps

    def _emit_waits(self, e, deps):
        for prod, val in deps.items():
            if prod == e and e == 'pe':
                continue
            if self.waited.get((e, prod), 0) >= val:
                continue
            self.waited[(e, prod)] = val
            sem = self.sem[prod] if isinstance(prod, str) else self.dsem[prod]
            self.thunks[e].append(lambda eng, sem=sem, val=val: eng.wait_ge(sem, val))

    def _record(self, me, reads, writes):
        for k in writes:
            self.last_w[k] = me
            self.readers[k] = []
        for k in reads:
            self.readers.setdefault(k, []).append(me)

    def op(self, e, fns, reads=(), writes=()):
        if not isinstance(fns, (list, tuple)):
            fns = [fns]
        deps = self._deps(reads, writes)
        self._emit_waits(e, deps)
        self.seq[e] += 1
        val = self.seq[e]
        sem = self.sem[e]
        n = len(fns)
        for i, fn in enumerate(fns):
            rec = _Rec()
            fn(rec)
            (name, a, k), = rec.calls
            if i == n - 1:
                self.thunks[e].append(lambda eng, name=name, a=a, k=k, sem=sem: getattr(eng, name)(*a, **k).then_inc(sem, 1))
            else:
                self.thunks[e].append(lambda eng, name=name, a=a, k=k: getattr(eng, name)(*a, **k))
        self._record((e, val), reads, writes)

    def dma(self, q, out, in_, reads=(), writes=(), **kw):
        half = NDMA // 2
        qi = 0 if q == 'sp' else 1
        d = qi * half + self.dnext2[qi]
        self.dnext2[qi] = (self.dnext2[qi] + 1) % half
        deps = self._deps(reads, writes)
        if self.dcnt[d] > 0 and deps.get(d, 0) < self.dcnt[d]:
            deps[d] = self.dcnt[d]
        self._emit_waits(q, deps)
        self.dcnt[d] += 16
        val = self.dcnt[d]
        sem = self.dsem[d]
        self.thunks[q].append(
            lambda eng, out=out, in_=in_, sem=sem, kw=kw: eng.dma_start(out=out, in_=in_, **kw).then_inc(sem, 16))
        self._record((d, val), reads, writes)

    def barrier(self, soft=False):
        if soft:
            for e in ('pe', 'act', 'dve'):
                deps = {}
                for p in ('pe', 'act', 'dve', 'pool'):
                    if self.seq[p] > 0 and not (p == e and e == 'pe'):
                        deps[p] = self.seq[p]
                self._emit_waits(e, deps)
            return
        for e in self.engs:
            deps = {}
            for p in self.engs:
                if p != e and self.seq[p] > 0:
                    deps[p] = self.seq[p]
            if e != 'pe' and self.seq[e] > 0:
                deps[e] = self.seq[e]
            for d in range(NDMA):
                if self.dcnt[d] > 0:
                    deps[d] = self.dcnt[d]
            self._emit_waits(e, deps)

    def run(self):
        for d in range(NDMA):
            if self.dcnt[d] > 0:
                self.thunks['sp'].append(lambda eng, sem=self.dsem[d], val=self.dcnt[d]: eng.wait_ge(sem, val))
        with self.nc.Block() as block:
            @block.tensor
            def _(eng):
                for t in self.thunks['pe']:
                    t(eng)

            @block.scalar
            def _(eng):
                for t in self.thunks['act']:
                    t(eng)

            @block.vector
            def _(eng):
                for t in self.thunks['dve']:
                    t(eng)

            @block.gpsimd
            def _(eng):
                for t in self.thunks['pool']:
                    t(eng)

            @block.sync
            def _(eng):
                for t in self.thunks['sp']:
                    t(eng)


C_QA, C_KA, C_QB, C_QBR, C_KB, C_KBR, C_QC, C_KC, C_RC, C_Z = 0, 256, 512, 1024, 1536, 1664, 1792, 2048, 2304, 2560
W_FM = 2592
T_VA, T_VB, T_VC, T_KC = 0, 256, 384, 640
W_TM = 896
QB_PERM = [0, 4, 1, 5, 2, 6, 3, 7]
NTS = 4096
NBLK = 4
PER_L = 48 + 8 + 8 + 7 + 4 * 2 * NFF
NV = 64 + L * PER_L
NCONST = 7 * 128 + 2 * 512
ARENA = 28672


class _Stop(Exception):
    pass


def build_program(n_layers=L, do_sample=True, stop=99):
    nc = bass.Bass("TRN2", target_bir_lowering=False)
    dt_in = lambda name, shape: nc.dram_tensor(name, shape, F32, kind="ExternalInput").ap()
    dt_out = lambda name, shape: nc.dram_tensor(name, shape, F32, kind="ExternalOutput").ap()
    x_in = dt_in("xT0", [128, 8, NT])
    xs_in = dt_in("xsT0", [128, 8, NTS])
    cond_in = dt_in("cond", [128, 16])
    w_mod = dt_in("w_mod", [L, D, 6 * D])
    w_fm = dt_in("w_fm", [L, D, W_FM])
    w_tm = dt_in("w_tm", [L, D, W_TM])
    w_g = dt_in("w_g", [L, D, 3 * D])
    w_ba = dt_in("w_ba", [L, 256, D])
    w_bb = dt_in("w_bb", [L, 512, D])
    w_bc = dt_in("w_bc", [L, 256, D])
    w_out = dt_in("w_out", [L, D, D])
    w_up = dt_in("w_up", [L, D, 2 * DFF])
    w_dn = dt_in("w_dn", [L, DFF, D])
    w_z2 = dt_in("w_z2", [L, 33, 512])
    vecs_in = dt_in("vecs", [128, NV])
    consts_in = dt_in("consts", [128, NCONST])
    ropeC_in = dt_in("ropeC", [128, NTS])
    ropeS_in = dt_in("ropeS", [128, NTS])
    nab_in = dt_in("nab", [L, 5, 128, 5 * 512])
    cnak_in = dt_in("cnak", [L, 128, 2, 512])
    cnav_in = dt_in("cnav", [L, 512, 256])
    cgk_in = dt_in("cgk", [L, 128, 512])
    cgv_in = dt_in("cgv", [L, 512, 128])
    sgf_in = dt_in("sgf", [L, 128, 2, 64])
    sgb_in = dt_in("sgb", [L, 128, 2, 64])

    y_out = dt_out("yT", [128, 8, NT])
    ys_out = dt_out("ysT", [128, 8, NTS])
    o_nak = dt_out("o_nak", [L, 256, NT])
    o_nav = dt_out("o_nav", [L, NT, 256])
    o_gk = dt_out("o_gk", [L, 128, NT])
    o_gv = dt_out("o_gv", [L, NT, 128])
    o_sf = dt_out("o_sf", [L, 4, 2, 128, 64])
    o_sb = dt_out("o_sb", [L, 4, 2, 128, 64])

    scr = lambda name, shape, dt: nc.dram_tensor(name, shape, dt).ap()
    xp_scr = scr("xp_scr", [128, 8, NT], F32)
    xs_scr = scr("xs_scr", [128, 8, NTS], F32)
    xs_scr2 = scr("xs_scr2", [128, 8, NTS], F32)
    sav_arena = scr("sav_arena", [NBLK, 128, ARENA], BF16)
    sav_hT = scr("sav_hT", [NBLK, 128, 8 * NT], BF16)
    kaT_scr = scr("kaT_scr", [128, 2, NTS], BF16)
    va_scr = scr("va_scr", [NTS, 256], BF16)
    sprev_scr = scr("sprev_scr", [32, 128, 128], BF16)

    with ExitStack() as es:
        T = lambda name, shape, dt: es.enter_context(nc.sbuf_tensor("sb_" + name, shape, dt))
        xreg = T("xreg", [128, 16384], BF16)
        xT = xreg[:].bitcast(F32).rearrange("p (c t) -> p c t", c=8)
        Kslab = xreg[:, 0:4096].rearrange("p (c t) -> p c t", c=2)
        Vslab = xreg[:, 4096:8192].rearrange("p (i c) -> p i c", i=16)
        nabI = xreg[:, 8192:10752]
        nabE = xreg[:, 10752:13312]
        hT = T("hT", [128, 8, NT], BF16)
        hTh = T("hTh", [128, 8, 2], BF16)
        xh = T("xh", [128, 8, 2], F32)
        wbuf = [T("wbuf%d" % i, [128, 8, 512], BF16) for i in range(2)]
        wdn2 = [T("wdn%d" % i, [128, NFF, 128], BF16) for i in range(2)]
        brw2 = [T("brw%d" % i, [128, 8, 128], BF16) for i in range(2)]
        arena = T("arena", [128, ARENA], BF16)
        kbT_all = T("kbT_all", [128, NTS + 512], BF16)
        vaug = T("vaug", [128, 36, 2, 128], BF16)
        vecs = T("vecs", [128, NV], F32)
        cb = T("cb", [128, NCONST], BF16)
        modT = T("modT", [128, 2, 48], F32)
        scA = T("scA", [128, 2, 16], F32)
        condf = T("condf", [128, 16], F32)
        condb = T("condb", [128, 16], BF16)
        zT = T("zT", [33, NT], BF16)
        wz2 = T("wz2", [33, 512], BF16)
        tf = [T("tf%d" % i, [128, 512], F32) for i in range(6)]
        tb = [T("tb%d" % i, [128, 512], BF16) for i in range(4)]
        trC = [T("trC%d" % i, [128, 512], F32) for i in range(2)]
        trS = [T("trS%d" % i, [128, 512], F32) for i in range(2)]
        S32 = [T("S32_%d" % i, [128, 2, 64], F32) for i in range(2)]
        qtil = [[T("qtil%d%d" % (d, c), [128, 256], BF16) for c in range(2)] for d in range(2)]
        attm = [[T("attm%d%d" % (d, c), [128, 512], BF16) for c in range(2)] for d in range(2)]
        Sprev = [[T("Sprev%d%d" % (d, c), [128, 2, 64], BF16) for c in range(2)] for d in range(2)]
        dec = [T("dec%d" % d, [128, 2], F32) for d in range(2)]
        onec = T("onec", [128, 1], F32)
        gsc = T("gsc", [128, 4], F32)
        qpad = [T("qpad%d" % i, [128, 512], BF16) for i in range(2)]
        epsc = T("epsc", [128, 1], F32)
        PS = [es.enter_context(nc.psum_tensor("ps%d" % i, [128, 512], F32)) for i in range(8)]

        off = [0]

        def carve(n, shape_str=None, **kw):
            a = arena[:, off[0]:off[0] + n]
            off[0] += n
            return a.rearrange(shape_str, **kw) if shape_str else a

        qaT = carve(2 * NT, "p (c t) -> p c t", c=2)
        qbT = carve(4 * NT, "p (c t) -> p c t", c=4)
        qcT = carve(2 * NT, "p (c t) -> p c t", c=2)
        mT = arena[:, 0:8 * NT].rearrange("p (c t) -> p c t", c=8)
        kcT = carve(2 * NT, "p (c t) -> p c t", c=2)
        rcT = carve(2 * NT, "p (c t) -> p c t", c=2)
        vtok = carve(8 * 256, "p (i c) -> p i c", i=8)
        ktok = carve(8 * 256, "p (i c) -> p i c", i=8)
        latok = carve(8 * 512, "p (i c) -> p i c", i=8)
        oaT = carve(2 * NT, "p (c t) -> p c t", c=2)
        obT = carve(4 * NT, "p (c t) -> p c t", c=4)
        ocT = carve(2 * NT, "p (c t) -> p c t", c=2)
        assert off[0] <= ARENA, off[0]
        actT = arena[:, 0:NFF * NT].rearrange("p (c t) -> p c t", c=NFF)

        em = Emit(nc, es)
        ps_rr = [0]

        def phase(k):
            if stop == k:
                raise _Stop()

        em.dma('sp', vecs[:], vecs_in, writes=['vecs'])
        em.dma('pool', cb[:], consts_in, writes=['cb'])
        em.op('pool', lambda e: e.memset(onec[:], 1.0), writes=['onec'])
        em.op('pool', lambda e: e.memset(epsc[:], EPS), writes=['epsc'])
        em.op('pool', lambda e: e.memset(zT[:], 1.0), writes=['zT'])
        em.op('pool', lambda e: e.memset(vaug[:], 1.0), writes=['vb_all'])
        for i_ in range(2):
            em.op('pool', lambda e: e.memset(qpad[i_][:], 0.0), writes=['qpad%d' % i_])
        TRI_F, TRI_B, TRIX_F, TRIX_B, BLK, ONES, IDENT = [cb[:, i * 128:(i + 1) * 128] for i in range(7)]
        MASK4 = [cb[:, 896:1408], cb[:, 1408:1920]]
        em.dma('sp', condf[:], cond_in, writes=['condf'])
        em.op('act', lambda e: e.activation(condb[:], condf[:], AF.Silu), reads=['condf'], writes=['condb'])
        condb3 = condb[:].rearrange("p (k c) -> p k c", c=2)

        def vcol(base, j):
            return vecs[:, base + j:base + j + 1]

        def load_w(buf, key, src, ncols, nk=8):
            em.dma('pool', buf[:, 0:nk, 0:ncols], src.rearrange("(kc p) c -> p kc c", p=128), writes=[key])

        wpar = [0]

        def next_wbuf():
            i = wpar[0]
            wpar[0] ^= 1
            return wbuf[i], 'wbuf%d' % i

        def next_ps():
            i = (0, 1, 4, 5)[ps_rr[0]]
            ps_rr[0] = (ps_rr[0] + 1) % 4
            return PS[i], 'ps%d' % i

        def rms_cols(which, ci, xsrc, hdst, w, hkey, xkey):
            sh_base = (0 if which == 0 else 24)
            for kc in range(8):
                em.op('act', lambda e, kc=kc: e.activation(tb[0][:, 0:w], xsrc(kc), AF.Square),
                      reads=[xkey], writes=['tb0'])
                em.op('pe', lambda e, kc=kc: e.matmul(PS[2][:, 0:w], ONES, tb[0][:, 0:w], start=(kc == 0), stop=(kc == 7)),
                      reads=['tb0', 'cb'], writes=['ps2'])
            em.op('act', lambda e: e.activation(tf[0][:, 0:w], PS[2][:, 0:w], AF.Sqrt, bias=epsc[:, 0:1], scale=1.0 / D),
                  reads=['epsc'], writes=['ps2', 'tf0'])
            em.op('dve', lambda e: e.reciprocal(tf[1][:, 0:w], tf[0][:, 0:w]), reads=['tf0'], writes=['tf1'])
            for kc in range(8):
                em.op('dve', lambda e, kc=kc: e.tensor_tensor(tf[2][:, 0:w], xsrc(kc), tf[1][:, 0:w], ALU.mult),
                      reads=[xkey, 'tf1'], writes=['tf2'])
                em.op('act', lambda e, kc=kc: e.activation(hdst(kc), tf[2][:, 0:w], AF.Identity,
                                                           bias=modT[:, ci, sh_base + kc:sh_base + kc + 1],
                                                           scale=scA[:, ci, which * 8 + kc:which * 8 + kc + 1]),
                      reads=['tf2', 'modT', 'scA'], writes=[hkey])

        def rmsnorm_mod(which, ci):
            for th in range(2):
                ts = slice(th * 512, th * 512 + 512)
                rms_cols(which, ci, lambda kc, ts=ts: xT[:, kc, ts], lambda kc, ts=ts: hT[:, kc, ts], 512, ('hT', th), 'xT')

        def proj_fm(wb, wkey, c0, n, th, ps, pskey, rhsT=None, rkey='hT', nk=8):
            r = hT if rhsT is None else rhsT
            ts = slice(th * 512, th * 512 + 512)
            em.op('pe', [lambda e, kc=kc: e.matmul(ps[0:n, :], wb[:, kc, c0:c0 + n], r[:, kc, ts],
                                                  start=(kc == 0), stop=(kc == nk - 1)) for kc in range(nk)],
                  reads=[wkey, (rkey, th)], writes=[pskey])

        hn_par = [0]

        def head_norm_core(ps, pskey, gcol, dst=None, dkey='tf4'):
            dst = tf[4][:] if dst is None else dst
            sq, sqk, t0, t0k, t1, t1k, pb = tb[1], 'tb1', tf[0], 'tf0', tf[1], 'tf1', 2
            pbk = 'ps%d' % pb
            em.op('act', lambda e: e.activation(sq[:], ps[:], AF.Square), writes=[pskey, sqk])
            em.op('pe', lambda e: e.matmul(PS[pb][:], BLK, sq[:], start=True, stop=True),
                  reads=[sqk, 'cb'], writes=[pbk])
            em.op('act', lambda e: e.activation(t0[:], PS[pb][:], AF.Ln, bias=epsc[:, 0:1], scale=1.0 / 64),
                  reads=['epsc'], writes=[pbk, t0k])
            em.op('act', lambda e: e.activation(t1[:], t0[:], AF.Exp, scale=-0.5), reads=[t0k], writes=[t1k])
            em.op('dve', lambda e: e.scalar_tensor_tensor(dst, ps[:], gcol, t1[:], ALU.mult, ALU.mult),
                  reads=[t1k, 'vecs', 'gsc'], writes=[pskey, dkey])

        def mod_phase(l, V_BMOD, V_GATT, V_GFFN):
            for g in range(12):
                wb, wkey = next_wbuf()
                load_w(wb, wkey, w_mod[l][:, g * 512:(g + 1) * 512], 512)
                for j in range(4):
                    col = g * 4 + j
                    em.op('pe', [lambda e, kc=kc, j=j, col=col, wb=wb: e.matmul(
                        PS[3][:, 2 * col:2 * col + 2], wb[:, kc, j * 128:(j + 1) * 128], condb3[:, kc, :],
                        start=(kc == 0), stop=(kc == 7)) for kc in range(8)],
                        reads=[wkey, 'condb'], writes=['ps3'])
            ps3v = PS[3][:, 0:96].rearrange("p (j c) -> p j c", c=2)
            for ci in range(2):
                em.op('dve', lambda e, ci=ci: e.tensor_tensor(modT[:, ci, :], ps3v[:, :, ci], vecs[:, V_BMOD:V_BMOD + 48], ALU.add),
                      reads=['vecs'], writes=['ps3', 'modT'])
                em.op('dve', lambda e, ci=ci: e.scalar_tensor_tensor(scA[:, ci, 0:8], modT[:, ci, 8:16], 1.0,
                                                                    vecs[:, V_GATT:V_GATT + 8], ALU.add, ALU.mult),
                      reads=['modT', 'vecs'], writes=['scA'])
                em.op('dve', lambda e, ci=ci: e.scalar_tensor_tensor(scA[:, ci, 8:16], modT[:, ci, 32:40], 1.0,
                                                                    vecs[:, V_GFFN:V_GFFN + 8], ALU.add, ALU.mult),
                      reads=['modT', 'vecs'], writes=['scA'])

        def inproj(l, blk, V_NRM):
            smp = blk is not None
            t0g = 0 if not smp else blk * NT
            for k_, src_ in enumerate((0, 2, 5)):
                em.op('pool', lambda e: e.tensor_scalar_mul(gsc[:, k_:k_ + 1], vcol(V_NRM, src_), 0.125), reads=['vecs'],
                      writes=['gsc'])
            if smp:
                for th in range(2):
                    em.dma('sp', trC[th][:], ropeC_in[:, t0g + th * 512:t0g + (th + 1) * 512], writes=['trC%d' % th])
                    em.dma('sp', trS[th][:], ropeS_in[:, t0g + th * 512:t0g + (th + 1) * 512], writes=['trS%d' % th])

            def sink_bf(out_bf, out_key, scale):
                em.op('act', lambda e: e.activation(out_bf, tf[4][:], AF.Identity, scale=scale), reads=['tf4'],
                      writes=[out_key])

            def chunk(wb, wkey, cc, n, col, wb2=None, wkey2=None):
                for th in range(2):
                    ts = slice(th * 512, th * 512 + 512)
                    gts = slice(t0g + th * 512, t0g + th * 512 + 512)
                    ps, pskey = next_ps()
                    proj_fm(wb, wkey, cc * 128, n, th, ps, pskey)
                    if col < C_KA:
                        c = (col - C_QA) // 128
                        head_norm_core(ps, pskey, gsc[:, 0:1], qaT[:, c, ts], ('qaT', th))
                    elif col < C_QB:
                        c = (col - C_KA) // 128
                        if not smp:
                            head_norm_core(ps, pskey, vcol(V_NRM, 1))
                            em.dma('sp', o_nak[l][c * 128:(c + 1) * 128, ts], tf[4][:], reads=['tf4'])
                            em.op('act', lambda e: e.copy(Kslab[:, c, ts], tf[4][:]), reads=['tf4'], writes=['Kslab'])
                        else:
                            head_norm_core(ps, pskey, vcol(V_NRM, 1), tb[2][:], 'tb2')
                            em.dma('sp', kaT_scr[:, c, gts], tb[2][:], reads=['tb2'], writes=['kaT_scr'])
                    elif col < C_QBR or (C_KB <= col < C_KBR):
                        isq = col < C_QBR
                        c = (col - C_QB) // 128 if isq else 0
                        if isq:
                            dst, dkey = qbT[:, c, ts], ('qbT', th)
                            g0_, g1_ = gsc[:, 1:2], gsc[:, 2:3]
                        else:
                            dst, dkey = kbT_all[:, gts], 'kbT_all'
                            g0_, g1_ = vcol(V_NRM, 3), vcol(V_NRM, 6)
                        if not smp:
                            if isq:
                                head_norm_core(ps, pskey, g0_, dst, dkey)
                            else:
                                head_norm_core(ps, pskey, g0_)
                                em.dma('sp', o_gk[l][:, ts], tf[4][:], reads=['tf4'])
                                em.op('act', lambda e: e.copy(dst, tf[4][:]), reads=['tf4'], writes=[dkey])
                        else:
                            head_norm_core(ps, pskey, g0_)
                            em.op('dve', lambda e: e.tensor_tensor(tf[5][:], tf[4][:], trC[th][:], ALU.mult),
                                  reads=['tf4', 'trC%d' % th], writes=['tf5'])
                            ps2_, ps2key = next_ps()
                            rc0 = (cc * 128) if isq else (cc + 1) * 128
                            wbr, wkr = (wb2, wkey2) if isq else (wb, wkey)
                            proj_fm(wbr, wkr, rc0, 128, th, ps2_, ps2key)
                            head_norm_core(ps2_, ps2key, g1_)
                            em.op('dve', lambda e: e.tensor_tensor(tf[4][:], tf[4][:], trS[th][:], ALU.mult),
                                  reads=['trS%d' % th], writes=['tf4'])
                            em.op('dve', lambda e: e.tensor_tensor(dst, tf[4][:], tf[5][:], ALU.add),
                                  reads=['tf4', 'tf5'], writes=[dkey])
                    elif col < C_KC:
                        c = (col - C_QC) // 128
                        em.op('act', lambda e, ps=ps, c=c: e.activation(qcT[:, c, ts], ps[:], AF.Identity, scale=0.125),
                              writes=[pskey, ('qcT', th)])
                    elif col < C_RC:
                        c = (col - C_KC) // 128
                        em.op('act', lambda e, ps=ps, c=c: e.copy(kcT[:, c, ts], ps[:]), writes=[pskey, ('kcT', th)])
                    elif col < C_Z:
                        c = (col - C_RC) // 128
                        em.op('act', lambda e, ps=ps, c=c: e.activation(rcT[:, c, ts], ps[:], AF.Silu),
                              writes=[pskey, ('rcT', th)])
                    else:
                        em.op('act', lambda e, ps=ps: e.copy(zT[0:32, ts], ps[0:32, :]), writes=[pskey, 'zT'])

            wb, wkey = next_wbuf()
            load_w(wb, wkey, w_fm[l][:, 0:512], 512)
            for cc in range(4):
                chunk(wb, wkey, cc, 128, cc * 128)
            wb, wkey = next_wbuf()
            load_w(wb, wkey, w_fm[l][:, C_QB:C_QB + 512], 512)
            wb2 = wkey2 = None
            if smp:
                wb2, wkey2 = next_wbuf()
                load_w(wb2, wkey2, w_fm[l][:, C_QBR:C_QBR + 512], 512)
            for cc in range(4):
                chunk(wb, wkey, cc, 128, C_QB + cc * 128, wb2, wkey2)
            wb, wkey = next_wbuf()
            load_w(wb, wkey, w_fm[l][:, C_KB:C_KB + 512], 512)
            chunk(wb, wkey, 0, 128, C_KB)
            chunk(wb, wkey, 2, 128, C_QC)
            chunk(wb, wkey, 3, 128, C_QC + 128)
            wb, wkey = next_wbuf()
            load_w(wb, wkey, w_fm[l][:, C_KC:C_KC + 512], 512)
            for cc in range(4):
                chunk(wb, wkey, cc, 128, C_KC + cc * 128)
            wb, wkey = next_wbuf()
            load_w(wb, wkey, w_fm[l][:, C_Z:C_Z + 32], 32)
            chunk(wb, wkey, 0, 32, C_Z)

            wtm = [None, None]
            for gi, (g0, gn) in enumerate([(0, 512), (512, 384)]):
                wb, wkey = next_wbuf()
                load_w(wb, wkey, w_tm[l][:, g0:g0 + gn], gn)
                wtm[gi] = (wb, wkey)
            em.dma('pool', wz2[:], w_z2[l], writes=['wz2'])
            for i in range(8):
                tsl = slice(i * 128, (i + 1) * 128)
                gi_tile = (0 if not smp else blk * 8) + i
                th = i // 4
                for gi, (g0, gn) in enumerate([(0, 512), (512, 384)]):
                    wb, wkey = wtm[gi]
                    ps, pskey = next_ps()
                    em.op('pe', [lambda e, kc=kc, wb=wb, ps=ps, gn=gn: e.matmul(
                        ps[:, 0:gn], hT[:, kc, tsl], wb[:, kc, 0:gn], start=(kc == 0), stop=(kc == 7)) for kc in range(8)],
                        reads=[wkey, ('hT', th)], writes=[pskey])
                    if gi == 0:
                        if not smp:
                            em.op('act', lambda e, ps=ps: e.copy(tf[5][:, 0:512], ps[:, 0:512]), writes=[pskey, 'tf5'])
                            em.dma('sp', o_nav[l][tsl, :], tf[5][:, 0:256], reads=['tf5'])
                            em.dma('sp', o_gv[l][tsl, :], tf[5][:, 256:384], reads=['tf5'])
                            em.op('dve', lambda e: e.tensor_copy(Vslab[:, i, :], tf[5][:, 0:256]), reads=['tf5'], writes=['Vslab'])
                            em.op('dve', lambda e: e.tensor_copy(vaug[:, i, 0, 0:64], tf[5][:, 256:320]), reads=['tf5'], writes=['vb_all'])
                            em.op('dve', lambda e: e.tensor_copy(vaug[:, i, 1, 64:128], tf[5][:, 320:384]), reads=['tf5'], writes=['vb_all'])
                            em.op('dve', lambda e: e.tensor_copy(vtok[:, i, 0:128], tf[5][:, 384:512]), reads=['tf5'],
                                  writes=[('vtok', i)])
                        else:
                            em.op('act', lambda e, ps=ps: e.copy(tb[1][:], ps[:]), writes=[pskey, 'tb1'])
                            em.dma('sp', va_scr[gi_tile * 128:(gi_tile + 1) * 128, :], tb[1][:, 0:256], reads=['tb1'],
                                   writes=['va_scr'])
                            em.op('dve', lambda e: e.tensor_copy(vaug[:, gi_tile, 0, 0:64], tb[1][:, 256:320]), reads=['tb1'],
                                  writes=['vb_all'])
                            em.op('dve', lambda e: e.tensor_copy(vaug[:, gi_tile, 1, 64:128], tb[1][:, 320:384]), reads=['tb1'],
                                  writes=['vb_all'])
                            em.op('dve', lambda e: e.tensor_copy(vtok[:, i, 0:128], tb[1][:, 384:512]), reads=['tb1'],
                                  writes=[('vtok', i)])
                    else:
                        em.op('act', lambda e, ps=ps: e.copy(vtok[:, i, 128:256], ps[:, 0:128]), writes=[pskey, ('vtok', i)])
                        em.op('dve', lambda e, ps=ps: e.tensor_copy(ktok[:, i, :], ps[:, 128:384]), writes=[pskey, ('ktok', i)])
                ps, pskey = next_ps()
                em.op('pe', lambda e, ps=ps: e.matmul(ps[:], zT[:, tsl], wz2[:], start=True, stop=True),
                      reads=['zT', 'wz2'], writes=[pskey])
                em.op('act', lambda e, ps=ps: e.activation(tf[0][:], ps[:], AF.Exp, scale=-1.0), writes=[pskey, 'tf0'])
                em.op('act', lambda e: e.activation(tf[1][:], tf[0][:], AF.Ln, bias=onec[:, 0:1]), reads=['tf0', 'onec'],
                      writes=['tf1'])
                em.op('dve', lambda e: e.tensor_scalar_mul(latok[:, i, :], tf[1][:], -1.0 / 16.0), reads=['tf1'],
                      writes=[('latok', i)])

        TBK = ['tb0', 'tb1', 'tb2', 'tb3']

        def attn_generic(qT, qkey, nchunk, qsl, n, th, tiles_of, oT, okey, aug=False):
            per = 512 // n
            stages = []
            for c in range(nchunk):
                for hh in range(2):
                    tl = tiles_of(c, hh)
                    nt = len(tl)
                    for gi, g0 in enumerate(range(0, nt, per)):
                        stages.append((c, hh, gi, tl[g0:g0 + per], g0, nt))

            def acc_bank(c, hh):
                if not aug:
                    return None
                return ((6, 2) if hh == 0 else (7, 3))[c % 2]

            def slot_of(st):
                c, hh, gi, grp, g0, nt = st
                if aug:
                    k = st_index[id(st)] % 4
                    return (0, 1, 4, 5)[k], k
                return ((0, 1) if hh == 0 else (4, 5))[gi % 2], (0 if hh == 0 else 2) + gi % 2

            def emit_qk(st):
                c, hh, gi, grp, g0, nt = st
                bank, ti = slot_of(st)
                fns = []
                rk = [(qkey, th), 'cb']
                if aug and g0 == 0:
                    em.op('pool', lambda e: e.tensor_copy(qpad[hh][hh * 64:(hh + 1) * 64, 0:n], qT[hh * 64:(hh + 1) * 64, c, qsl]),
                          reads=[(qkey, th)], writes=['qpad%d' % hh])
                if aug:
                    rk.append('qpad%d' % hh)
                for j, (kap, kkeys, vap, vkeys, bias) in enumerate(grp):
                    o = PS[bank][:, j * n:(j + 1) * n]
                    if aug:
                        fns.append(lambda e, o=o, kap=kap: e.matmul(o, kap, qpad[hh][:, 0:n], start=True, stop=True))
                        rk += list(kkeys)
                        continue
                    fns.append(lambda e, o=o, kap=kap, bias=bias: e.matmul(
                        o, kap, qT[hh * 64:(hh + 1) * 64, c, qsl], start=True, stop=(bias is None)))
                    if bias is not None:
                        fns.append(lambda e, o=o, bias=bias: e.matmul(o, IDENT, bias, start=False, stop=True))
                    rk += list(kkeys)
                em.op('pe', fns, reads=rk, writes=['ps%d' % bank])
                w = len(grp) * n
                em.op('act', lambda e: e.activation(tb[ti][:, 0:w], PS[bank][:, 0:w], AF.Exp),
                      writes=['ps%d' % bank, TBK[ti]])

            def emit_pv(st):
                c, hh, gi, grp, g0, nt = st
                bank_, ti = slot_of(st)
                tbt = tb[ti]
                fns = []
                rk = [TBK[ti], 'cb']
                A = acc_bank(c, hh)
                for j, (kap, kkeys, vap, vkeys, bias) in enumerate(grp):
                    first = (g0 + j == 0)
                    last = (g0 + j == nt - 1)
                    if aug:
                        fns.append(lambda e, vap=vap, j=j, first=first, last=last: e.matmul(
                            PS[A][:, 0:n], vap, tbt[:, j * n:(j + 1) * n], start=first, stop=last))
                    else:
                        fns.append(lambda e, vap=vap, j=j, first=first, last=last: e.matmul(
                            PS[6][hh * 64:(hh + 1) * 64, 0:n], vap, tbt[:, j * n:(j + 1) * n], start=first, stop=last))
                        fns.append(lambda e, j=j, first=first, last=last: e.matmul(
                            PS[7][hh * 64:(hh + 1) * 64, 0:n], ONES[:, 0:64], tbt[:, j * n:(j + 1) * n], start=first, stop=last))
                    rk += list(vkeys)
                em.op('pe', fns, reads=rk, writes=(['ps%d' % A] if aug else ['ps6', 'ps7']))
                if g0 + len(grp) < nt:
                    return
                if aug:
                    rn = slice(hh * 64, hh * 64 + 64)
                    rd = slice((1 - hh) * 64, (1 - hh) * 64 + 64)
                    tfx = tf[4 + hh]
                    em.op('dve', lambda e: e.reciprocal(tfx[rn, 0:n], PS[A][rd, 0:n]), writes=['ps%d' % A, 'tf%d' % (4 + hh)])
                    em.op('dve', lambda e: e.tensor_tensor(oT[rn, c, qsl], tfx[rn, 0:n], PS[A][rn, 0:n], ALU.mult),
                          reads=['tf%d' % (4 + hh)], writes=['ps%d' % A, (okey, th)])
                elif hh == 1:
                    em.op('dve', lambda e: e.reciprocal(tf[0][:, 0:n], PS[7][:, 0:n]), writes=['ps7', 'tf0'])
                    em.op('dve', lambda e: e.tensor_tensor(oT[:, c, qsl], PS[6][:, 0:n], tf[0][:, 0:n], ALU.mult),
                          reads=['tf0'], writes=['ps6', (okey, th)])

            st_index = {id(st): k for k, st in enumerate(stages)}
            lag = 2 if aug else 1
            for k, st in enumerate(stages):
                emit_qk(st)
                if k - lag >= 0:
                    emit_pv(stages[k - lag])
            for k in range(max(len(stages) - lag, 0), len(stages)):
                emit_pv(stages[k])

        def attention_prompt():
            for s in range(NT // SEQ):
                th = s // 2
                qs = slice(s * SEQ, (s + 1) * SEQ)

                def tiles_a(c, hh, s=s):
                    h = 2 * c + hh
                    return [(Kslab[hh * 64:(hh + 1) * 64, c, (2 * s + kt) * 128:(2 * s + kt + 1) * 128], ['Kslab'],
                             Vslab[:, 2 * s + kt, h * 64:(h + 1) * 64], ['Vslab'], None) for kt in range(2)]

                def tiles_b(c, hh, s=s):
                    return [(kbT_all[:, (2 * s + kt) * 128:(2 * s + kt + 1) * 128], ['kbT_all'],
                             vaug[:, 2 * s + kt, hh, :], ['vb_all'], None) for kt in range(2)]

                attn_generic(qaT, 'qaT', 2, qs, 256, th, tiles_a, oaT, 'oaT')
                attn_generic(qbT, 'qbT', 4, qs, 256, th, tiles_b, obT, 'obT', aug=True)

        def attention_sample(l, blk):
            sb0 = max(8 * blk - 2, 0)
            sb1 = min(8 * blk + 10, 32)
            nsl = sb1 - sb0
            em.dma('sp', Kslab[:, :, 0:nsl * 128], kaT_scr[:, :, sb0 * 128:sb1 * 128], reads=['kaT_scr'], writes=['Kslab'])
            em.dma('pool', Kslab[:, :, 1536:2048], cnak_in[l], writes=['Kslab'])
            em.dma('sp', Vslab[:, 0:nsl, :], va_scr[sb0 * 128:sb1 * 128, :].rearrange("(i p) c -> p i c", p=128),
                   reads=['va_scr'], writes=['Vslab'])
            em.dma('pool', Vslab[:, 12:16, :], cnav_in[l].rearrange("(i p) c -> p i c", p=128), writes=['Vslab'])
            em.dma('pool', nabI, nab_in[l][0], writes=['nabI'])
            if blk == NBLK - 1:
                em.dma('pool', kbT_all[:, NTS:NTS + 512], cgk_in[l], writes=['kbT_all'])
                cgv3 = cgv_in[l].rearrange("(i p) c -> p i c", p=128)
                em.dma('pool', vaug[:, 32:36, 0, 0:64], cgv3[:, :, 0:64], writes=['vb_all'])
                em.dma('pool', vaug[:, 32:36, 1, 64:128], cgv3[:, :, 64:128], writes=['vb_all'])
            for pr in range(8):
                r = 16 * blk + 2 * pr
                t0 = min(max((r - 4) // 2, 0), 27)
                var = {0: 1, 2: 2, 60: 3, 62: 4}.get(r, 0)
                if var == 0:
                    nb, nbkey = nabI, 'nabI'
                else:
                    em.dma('pool', nabE, nab_in[l][var], writes=['nabE'])
                    nb, nbkey = nabE, 'nabE'
                qsl = slice(pr * 128, (pr + 1) * 128)
                th = pr // 4

                def tiles_a(c, hh, t0=t0, nb=nb, nbkey=nbkey):
                    h = 2 * c + hh
                    tl = []
                    for j in range(5):
                        si = t0 + j - sb0
                        tl.append((Kslab[hh * 64:(hh + 1) * 64, c, si * 128:(si + 1) * 128], ['Kslab', nbkey],
                                   Vslab[:, si, h * 64:(h + 1) * 64], ['Vslab'],
                                   nb[:, j * 512 + h * 128: j * 512 + (h + 1) * 128]))
                    for j in range(4):
                        tl.append((Kslab[hh * 64:(hh + 1) * 64, c, 1536 + j * 128:1536 + (j + 1) * 128], ['Kslab'],
                                   Vslab[:, 12 + j, h * 64:(h + 1) * 64], ['Vslab'], None))
                    return tl

                attn_generic(qaT, 'qaT', 2, qsl, 128, th, tiles_a, oaT, 'oaT')
            for th in range(2):
                qsl = slice(th * 512, (th + 1) * 512)

                def tiles_b(c, hh):
                    return [(kbT_all[:, kt * 128:(kt + 1) * 128], ['kbT_all'],
                             vaug[:, kt, hh, :], ['vb_all'], None) for kt in range(36)]

                attn_generic(qbT, 'qbT', 4, qsl, 512, th, tiles_b, obT, 'obT', aug=True)

        def gla_decay(i, th, d, slot):
            TRI = TRI_F if d == 0 else TRI_B
            tsl = slice(i * 128, (i + 1) * 128)
            em.op('pe', [lambda e, hp=hp: e.matmul(
                PS[4][:, hp * 128:(hp + 1) * 128], latok[:, i, d * 256 + hp * 128:d * 256 + (hp + 1) * 128],
                TRI, start=True, stop=True) for hp in range(2)],
                reads=[('latok', i), 'cb'], writes=['ps4'])
            em.op('act', lambda e: e.activation(tf[0][:, 0:256], PS[4][:, 0:256], AF.Exp), writes=['ps4', 'tf0'])
            em.op('act', lambda e: e.activation(tf[1][:, 0:256], PS[4][:, 0:256], AF.Exp, scale=-1.0), writes=['ps4', 'tf1'])
            qtt = qtil[d][slot]
            em.op('dve', lambda e: e.tensor_tensor(
                qtt[:].rearrange("p (c t) -> p c t", c=2), qcT[:, :, tsl],
                tf[0][:, 0:256].rearrange("p (c t) -> p c t", c=2), ALU.mult),
                reads=[('qcT', th), 'tf0'], writes=[('qtil', d, slot)])
            em.op('dve', lambda e: e.tensor_tensor(
                tb[2][:, 0:256].rearrange("p (c t) -> p c t", c=2), kcT[:, :, tsl],
                tf[1][:, 0:256].rearrange("p (c t) -> p c t", c=2), ALU.mult),
                reads=[('kcT', th), 'tf1'], writes=['tb2'])
            att = attm[d][slot]
            for hh, bank in ((0, 5), (1, 3)):
                em.op('pe', [lambda e, hp=hp: e.matmul(
                    PS[bank][:, hp * 128:(hp + 1) * 128],
                    tb[2][hh * 64:hh * 64 + 64, hp * 128:hp * 128 + 128],
                    qtt[hh * 64:hh * 64 + 64, hp * 128:hp * 128 + 128],
                    start=True, stop=True) for hp in range(2)],
                    reads=['tb2', ('qtil', d, slot)], writes=['ps%d' % bank])
                em.op('dve', lambda e: e.tensor_tensor(
                    att[:].rearrange("p (hp hh t) -> p hp hh t", hp=2, hh=2)[:, :, hh, :],
                    PS[bank][:, 0:256].rearrange("p (hp t) -> p hp t", hp=2),
                    MASK4[d][:, 0:256].rearrange("p (hp t) -> p hp t", hp=2), ALU.mult),
                    reads=['cb'], writes=['ps%d' % bank, ('attm', d, slot)])

        def gla_chain(i, d, slot, save_to=None):
            TRIX = TRIX_F if d == 0 else TRIX_B
            em.op('act', lambda e: e.copy(Sprev[d][slot][:], S32[d][:]), reads=[('S32', d)], writes=[('Sprev', d, slot)])
            if save_to is not None:
                em.dma('sp', save_to, Sprev[d][slot][:].rearrange("p a b -> p (a b)"), reads=[('Sprev', d, slot)],
                       writes=['sprev_scr'])
            em.op('pe', [lambda e, hp=hp: e.matmul(
                PS[7][:, 128 + hp:129 + hp], latok[:, i, d * 256 + hp * 128:d * 256 + (hp + 1) * 128], ONES[:, 0:1],
                start=True, stop=True) for hp in range(2)], reads=[('latok', i), 'cb'], writes=['ps7'])
            em.op('act', lambda e: e.activation(dec[d][:, 0:2], PS[7][:, 128:130], AF.Exp), writes=['ps7', ('dec', d)])
            em.op('pe', lambda e: e.matmul(PS[4][:, 0:256], TRIX, latok[:, i, d * 256:(d + 1) * 256], start=True, stop=True),
                  reads=[('latok', i), 'cb'], writes=['ps4'])
            em.op('act', lambda e: e.activation(tf[2][:, 0:256], PS[4][:, 0:256], AF.Exp), writes=['ps4', 'tf2'])
            em.op('dve', lambda e: e.tensor_tensor(tb[0][:, 0:256], ktok[:, i, :], tf[2][:, 0:256], ALU.mult),
                  reads=[('ktok', i), 'tf2'], writes=['tb0'])
            em.op('pe', [lambda e, h=h: e.matmul(
                PS[7][(h % 2) * 64:(h % 2) * 64 + 64, (h // 2) * 64:(h // 2) * 64 + 64],
                tb[0][:, h * 64:(h + 1) * 64], vtok[:, i, h * 64:(h + 1) * 64],
                start=True, stop=True) for h in range(4)],
                reads=['tb0', ('vtok', i)], writes=['ps7'])
            for hp in range(2):
                em.op('dve', lambda e, hp=hp: e.scalar_tensor_tensor(
                    S32[d][:, hp, :], S32[d][:, hp, :], dec[d][:, hp:hp + 1], PS[7][:, hp * 64:(hp + 1) * 64],
                    ALU.mult, ALU.add), reads=[('dec', d)], writes=['ps7', ('S32', d)])

        def gla_out(i, th, slot, use_inter, V_NRM):
            tsl = slice(i * 128, (i + 1) * 128)
            fns = []
            for h in range(4):
                hh, hp = h % 2, h // 2
                outp = PS[6][hh * 64:hh * 64 + 64, hp * 128:(hp + 1) * 128]
                seqm = []
                for d in range(2):
                    seqm.append((vtok[:, i, h * 64:(h + 1) * 64], attm[d][slot][:, h * 128:(h + 1) * 128]))
                    if use_inter[d]:
                        seqm.append((Sprev[d][slot][hh * 64:hh * 64 + 64, hp, :],
                                     qtil[d][slot][hh * 64:hh * 64 + 64, hp * 128:(hp + 1) * 128]))
                for k, (lt, rh) in enumerate(seqm):
                    fns.append(lambda e, outp=outp, lt=lt, rh=rh, k=k, n=len(seqm): e.matmul(
                        outp, lt, rh, start=(k == 0), stop=(k == n - 1)))
            em.op('pe', fns, reads=[('vtok', i)] + [('attm', d, slot) for d in range(2)] +
                  [('Sprev', d, slot) for d in range(2)] + [('qtil', d, slot) for d in range(2)], writes=['ps6'])
            em.op('act', lambda e: e.activation(tb[1][:, 0:256], PS[6][:, 0:256], AF.Square), writes=['ps6', 'tb1'])
            em.op('dve', lambda e: e.tensor_copy(tf[3][:, 0:256], PS[6][:, 0:256]), writes=['ps6', 'tf3'])
            em.op('pe', lambda e: e.matmul(PS[2][:, 0:256], BLK, tb[1][:, 0:256], start=True, stop=True),
                  reads=['tb1', 'cb'], writes=['ps2'])
            em.op('act', lambda e: e.activation(tf[0][:, 0:256], PS[2][:, 0:256], AF.Ln, bias=epsc[:, 0:1],
                                                scale=1.0 / 64), reads=['epsc'], writes=['ps2', 'tf0'])
            em.op('act', lambda e: e.activation(tf[1][:, 0:256], tf[0][:, 0:256], AF.Exp, scale=-0.5), reads=['tf0'],
                  writes=['tf1'])
            em.op('dve', lambda e: e.scalar_tensor_tensor(tf[4][:, 0:256], tf[3][:, 0:256], vcol(V_NRM, 4),
                                                         tf[1][:, 0:256], ALU.mult, ALU.mult),
                  reads=['tf3', 'tf1', 'vecs'], writes=['tf4'])
            em.op('dve', lambda e: e.tensor_tensor(
                ocT[:, :, tsl], tf[4][:, 0:256].rearrange("p (c t) -> p c t", c=2), rcT[:, :, tsl], ALU.mult),
                reads=['tf4', ('rcT', th)], writes=[('ocT', th)])

        def gla_prompt(l, V_NRM):
            for s in range(NT // SEQ):
                th = s // 2
                for d in range(2):
                    order = [0, 1] if d == 0 else [1, 0]
                    em.op('pool', lambda e, d=d: e.memset(S32[d][:], 0.0), writes=[('S32', d)])
                    for ci in order:
                        i = 2 * s + ci
                        gla_decay(i, th, d, ci)
                        gla_chain(i, d, ci)
                    dst = (o_sf if d == 0 else o_sb)[l][s]
                    em.dma('sp', dst.rearrange("hp p v -> p hp v"), S32[d][:], reads=[('S32', d)])
                for ci in range(2):
                    gla_out(2 * s + ci, th, ci, [ci != 0, ci != 1], V_NRM)

        mrr = [0]

        def merge_outproj(l, ci, xsrc, xdst, xkey, hook_mid=None):
            em.barrier(soft=True)
            for j in range(8):
                wb, wkey = next_wbuf()
                for bi in range(3):
                    em.dma('pool', wb[:, :, bi * 128:(bi + 1) * 128],
                           w_g[l][:, bi * 1024 + j * 128: bi * 1024 + (j + 1) * 128].rearrange("(kc p) c -> p kc c", p=128),
                           writes=[wkey])
                brw, brk = brw2[j % 2], 'brw%d' % (j % 2)
                em.dma('pool', brw[:, 0:2, :], w_ba[l][:, j * 128:(j + 1) * 128].rearrange("(kc p) c -> p kc c", p=128), writes=[brk])
                em.dma('pool', brw[:, 2:6, :], w_bb[l][:, j * 128:(j + 1) * 128].rearrange("(kc p) c -> p kc c", p=128), writes=[brk])
                em.dma('pool', brw[:, 6:8, :], w_bc[l][:, j * 128:(j + 1) * 128].rearrange("(kc p) c -> p kc c", p=128), writes=[brk])
                for th in range(2):
                    ts = slice(th * 512, th * 512 + 512)
                    for bi, (oT, okey, k0, nk) in enumerate([(oaT, 'oaT', 0, 2), (obT, 'obT', 2, 4), (ocT, 'ocT', 6, 2)]):
                        ps, pskey = next_ps()
                        proj_fm(wb, wkey, bi * 128, 128, th, ps, pskey)
                        mrr[0] ^= 1
                        sg, sgk = (tf[0], 'tf0') if mrr[0] else (tf[3], 'tf3')
                        pr_, prk = (tf[2], 'tf2') if mrr[0] else (tf[5], 'tf5')
                        bb_ = 3 if mrr[0] else 2
                        em.op('act', lambda e, ps=ps: e.activation(sg[:], ps[:], AF.Sigmoid), writes=[pskey, sgk])
                        em.op('pe', [lambda e, kc=kc, oT=oT, k0=k0, nk=nk: e.matmul(
                            PS[bb_][:], brw[:, k0 + kc, :], oT[:, kc, ts],
                            start=(kc == 0), stop=(kc == nk - 1)) for kc in range(nk)],
                            reads=[brk, (okey, th)], writes=['ps%d' % bb_])
                        if bi == 0:
                            em.op('dve', lambda e: e.tensor_tensor(tf[1][:], PS[bb_][:], sg[:], ALU.mult),
                                  reads=[sgk], writes=['ps%d' % bb_, 'tf1'])
                        else:
                            em.op('dve', lambda e: e.tensor_tensor(pr_[:], PS[bb_][:], sg[:], ALU.mult),
                                  reads=[sgk], writes=['ps%d' % bb_, prk])
                            em.op('dve', lambda e: e.tensor_tensor(tf[1][:], tf[1][:], pr_[:], ALU.add),
                                  reads=[prk], writes=['tf1'])
                    em.op('act', lambda e, j=j: e.copy(mT[:, j, ts], tf[1][:]), reads=['tf1'], writes=[('mT', th)])
            if hook_mid is not None:
                hook_mid()
            for g in range(2):
                wb, wkey = next_wbuf()
                load_w(wb, wkey, w_out[l][:, g * 512:(g + 1) * 512], 512)
                for cc in range(4):
                    j = g * 4 + cc
                    for th in range(2):
                        ts = slice(th * 512, th * 512 + 512)
                        ps, pskey = next_ps()
                        proj_fm(wb, wkey, cc * 128, 128, th, ps, pskey, rhsT=mT, rkey='mT')
                        mrr[0] ^= 1
                        xt_, xtk = (tf[3], 'tf3') if mrr[0] else (tf[4], 'tf4')
                        em.dma('sp', xt_[:], xsrc[:, j, ts], reads=[xkey, 'xs2'], writes=[xtk])
                        em.op('dve', lambda e, ps=ps, j=j: e.scalar_tensor_tensor(
                            xt_[:], ps[:], modT[:, ci, 16 + j:17 + j], xt_[:], ALU.mult, ALU.add),
                            reads=['modT'], writes=[pskey, xtk])
                        em.dma('sp', xdst[:, j, ts], xt_[:], reads=[xtk], writes=[xkey])

        def ffn(l, ci, V_CONV, smp, has_left, has_right):
            em.barrier()
            pending_tail = []
            wq = [wbuf[k // 2][:, :, (k % 2) * 256:(k % 2) * 256 + 256] for k in range(4)]
            for fg in range(11):
                nf = 2
                r_ = (fg % 2) * 2
                wa, wakey = wq[r_], 'wq%d' % r_
                em.dma('pool', wa, w_up[l][:, fg * 256:fg * 256 + 256].rearrange("(kc p) c -> p kc c", p=128), writes=[wakey])
                wg, wgkey = wq[r_ + 1], 'wq%d' % (r_ + 1)
                em.dma('pool', wg, w_up[l][:, DFF + fg * 256:DFF + fg * 256 + 256].rearrange("(kc p) c -> p kc c", p=128),
                       writes=[wgkey])
                for cc in range(nf):
                    f = fg * 2 + cc
                    if f % 2 == 0:
                        ACC, ACCK = [tf[0], tf[1], tf[2], tf[3]], ['tf0', 'tf1', 'tf2', 'tf3']
                        SIL, SILK = tf[4], 'tf4'
                    else:
                        ACC, ACCK = [trC[0], trC[1], trS[0], trS[1]], ['trC0', 'trC1', 'trS0', 'trS1']
                        SIL, SILK = tf[5], 'tf5'
                    for part, (wb, wkey) in enumerate([(wa, wakey), (wg, wgkey)]):
                        if part == 1 and len(pending_tail) > 0:
                            pending_tail.pop(0)()
                        cbase = V_CONV + part * 4 * NFF
                        w0 = vcol(cbase, f)
                        w1 = vcol(cbase + NFF, f)
                        w2 = vcol(cbase + 2 * NFF, f)
                        bb = vcol(cbase + 3 * NFF, f)
                        pss = []
                        pb = 0 if part == 0 else 4
                        hb = 2 if part == 0 else 3
                        for th in range(2):
                            ps, pskey = PS[pb + th], 'ps%d' % (pb + th)
                            proj_fm(wb, wkey, cc * 128, 128, th, ps, pskey)
                            pss.append((ps, pskey))
                        if smp and (has_left or has_right):
                            em.op('pe', [lambda e, kc=kc: e.matmul(PS[hb][:, 0:2], wb[:, kc, cc * 128:(cc + 1) * 128],
                                                                  hTh[:, kc, :], start=(kc == 0), stop=(kc == 7))
                                         for kc in range(8)], reads=[wkey, 'hTh'], writes=['ps%d' % hb])
                        for th in range(2):
                            ps, pskey = pss[th]
                            acc = ACC[part * 2 + th]
                            akey = ACCK[part * 2 + th]
                            em.op('act', lambda e, ps=ps, acc=acc: e.activation(acc[:], ps[:], AF.Identity, bias=bb, scale=w1),
                                  reads=['vecs'], writes=[pskey, akey])
                            if not smp:
                                a3 = acc[:].rearrange("p (s t) -> p s t", s=2)
                                p3 = ps[:].rearrange("p (s t) -> p s t", s=2)
                                em.op('dve', lambda e, a3=a3, p3=p3: e.scalar_tensor_tensor(
                                    a3[:, :, 1:256], p3[:, :, 0:255], w0, a3[:, :, 1:256], ALU.mult, ALU.add),
                                    reads=['vecs'], writes=[pskey, akey])
                                em.op('dve', lambda e, a3=a3, p3=p3: e.scalar_tensor_tensor(
                                    a3[:, :, 0:255], p3[:, :, 1:256], w2, a3[:, :, 0:255], ALU.mult, ALU.add),
                                    reads=['vecs'], writes=[pskey, akey])
                            else:
                                em.op('dve', lambda e, acc=acc, ps=ps: e.scalar_tensor_tensor(
                                    acc[:, 1:512], ps[:, 0:511], w0, acc[:, 1:512], ALU.mult, ALU.add),
                                    reads=['vecs'], writes=[pskey, akey])
                                em.op('dve', lambda e, acc=acc, ps=ps: e.scalar_tensor_tensor(
                                    acc[:, 0:511], ps[:, 1:512], w2, acc[:, 0:511], ALU.mult, ALU.add),
                                    reads=['vecs'], writes=[pskey, akey])
                        if smp:
                            a0, a1 = ACC[part * 2], ACC[part * 2 + 1]
                            k0, k1 = ACCK[part * 2], ACCK[part * 2 + 1]
                            em.op('dve', lambda e: e.scalar_tensor_tensor(
                                a0[:, 511:512], PS[pb + 1][:, 0:1], w2, a0[:, 511:512], ALU.mult, ALU.add),
                                reads=['vecs'], writes=['ps%d' % (pb + 1), k0])
                            em.op('dve', lambda e: e.scalar_tensor_tensor(
                                a1[:, 0:1], PS[pb][:, 511:512], w0, a1[:, 0:1], ALU.mult, ALU.add),
                                reads=['vecs'], writes=['ps%d' % pb, k1])
                            if has_left:
                                em.op('dve', lambda e: e.scalar_tensor_tensor(
                                    a0[:, 0:1], PS[hb][:, 0:1], w0, a0[:, 0:1], ALU.mult, ALU.add),
                                    reads=['vecs'], writes=['ps%d' % hb, k0])
                            if has_right:
                                em.op('dve', lambda e: e.scalar_tensor_tensor(
                                    a1[:, 511:512], PS[hb][:, 1:2], w2, a1[:, 511:512], ALU.mult, ALU.add),
                                    reads=['vecs'], writes=['ps%d' % hb, k1])
                    def tail(f=f, ACC=ACC, ACCK=ACCK):
                        for th in range(2):
                            ts = slice(th * 512, th * 512 + 512)
                            SIL, SILK = (tf[4], 'tf4') if th == 0 else (tf[5], 'tf5')
                            em.op('act', lambda e: e.activation(SIL[:], ACC[2 + th][:], AF.Silu), reads=[ACCK[2 + th]],
                                  writes=[SILK])
                            em.op('dve', lambda e: e.tensor_tensor(actT[:, f, ts], ACC[th][:], SIL[:], ALU.mult),
                                  reads=[ACCK[th], SILK], writes=[('actT', th)])
                    pending_tail.append(tail)
            while pending_tail:
                pending_tail.pop(0)()
            em.barrier(soft=True)
            for j in range(8):
                wdn, wdk = wdn2[j % 2], 'wdn%d' % (j % 2)
                em.dma('pool', wdn[:], w_dn[l][:, j * 128:(j + 1) * 128].rearrange("(kc p) c -> p kc c", p=128), writes=[wdk])
                for th in range(2):
                    ts = slice(th * 512, th * 512 + 512)
                    ps, pskey = next_ps()
                    em.op('pe', [lambda e, f=f, ps=ps: e.matmul(ps[:], wdn[:, f, :], actT[:, f, ts],
                                                               start=(f == 0), stop=(f == NFF - 1)) for f in range(NFF)],
                          reads=[wdk, ('actT', th)], writes=[pskey])
                    em.op('dve', lambda e, ps=ps, j=j: e.scalar_tensor_tensor(
                        xT[:, j, ts], ps[:], modT[:, ci, 40 + j:41 + j], xT[:, j, ts], ALU.mult, ALU.add),
                        reads=['modT'], writes=[pskey, 'xT'])

        try:
          for l in range(n_layers):
            VB = 64 + l * PER_L
            V_BMOD, V_GATT, V_GFFN, V_NRM, V_CONV = VB, VB + 48, VB + 56, VB + 64, VB + 71
            last = (l == n_layers - 1)
            mod_phase(l, V_BMOD, V_GATT, V_GFFN)
            phase(1)
            xp_src = x_in if l == 0 else xp_scr
            em.barrier()
            em.dma('sp', xT, xp_src, reads=['xp'], writes=['xT'])
            rmsnorm_mod(0, 0)
            em.barrier()
            phase(2)
            inproj(l, None, V_NRM)
            phase(4)
            attention_prompt()
            phase(5)
            gla_prompt(l, V_NRM)
            phase(6)
            merge_outproj(l, 0, xp_src, xp_scr, 'xp')
            phase(8)
            em.barrier()
            em.dma('sp', xT, xp_scr, reads=['xp'], writes=['xT'])
            rmsnorm_mod(1, 0)
            ffn(l, 0, V_CONV, False, False, False)
            em.dma('sp', y_out if last else xp_scr, xT, reads=['xT'], writes=['xp'])
            em.barrier()
            phase(9)
            if not do_sample:
                continue
            xs_src = xs_in if l == 0 else xs_scr2
            K_A = [(n_, t_) for n_ in ('kcT', 'rcT') for t_ in range(2)] + \
                  [(n_, i_) for n_ in ('vtok', 'ktok', 'latok') for i_ in range(8)]
            K_C = [(n_, t_) for n_ in ('qaT', 'qbT', 'qcT') for t_ in range(2)]
            K_H = [('hT', 0), ('hT', 1)]
            NSAV = 20480
            em.dma('sp', S32[0][:], sgf_in[l], writes=[('S32', 0)])
            for blk in range(NBLK):
                bs = slice(blk * NT, (blk + 1) * NT)
                if blk == 0:
                    em.barrier()
                em.dma('sp', xT, xs_src[:, :, bs], reads=['xs', 'xs2'], writes=['xT'])
                rmsnorm_mod(0, 1)
                inproj(l, blk, V_NRM)
                for i in range(8):
                    gla_chain(i, 0, 0, save_to=sprev_scr[blk * 8 + i])
                em.dma('sp', sav_arena[blk][:, 0:NSAV], arena[:, 0:NSAV], reads=K_A + K_C, writes=[('sav', blk)])
                em.dma('sp', sav_hT[blk], hT[:].rearrange("p c t -> p (c t)"), reads=K_H, writes=[('savh', blk)])
            phase(10)
            em.barrier()
            em.dma('sp', S32[1][:], sgb_in[l], writes=[('S32', 1)])

            def restore_a(blk):
                em.dma('sp', arena[:, 8192:NSAV], sav_arena[blk][:, 8192:NSAV], reads=[('sav', blk)], writes=K_A)

            def restore_h(blk):
                em.dma('sp', hT[:].rearrange("p c t -> p (c t)"), sav_hT[blk], reads=[('savh', blk)], writes=K_H)

            def restore_c(blk):
                em.dma('sp', arena[:, 0:8192], sav_arena[blk][:, 0:8192], reads=[('sav', blk)],
                       writes=K_C + [('mT', 0), ('mT', 1), ('actT', 0), ('actT', 1)])

            restore_a(NBLK - 1)
            restore_h(NBLK - 1)
            for blk in reversed(range(NBLK)):
                bs = slice(blk * NT, (blk + 1) * NT)
                restore_c(blk)
                attention_sample(l, blk)
                def gla_pre(i):
                    gla_decay(i, i // 4, 0, i % 2)
                    gla_decay(i, i // 4, 1, i % 2)
                    em.dma('sp', Sprev[0][i % 2][:].rearrange("p a b -> p (a b)"), sprev_scr[blk * 8 + i],
                           reads=['sprev_scr'], writes=[('Sprev', 0, i % 2)])

                gla_pre(7)
                for i in reversed(range(8)):
                    gla_chain(i, 1, i % 2)
                    if i > 0:
                        gla_pre(i - 1)
                    gla_out(i, i // 4, i % 2, [True, True], V_NRM)
                nxt = blk - 1
                if nxt >= 0:
                    restore_a(nxt)
                merge_outproj(l, 1, xs_src[:, :, bs], xs_scr[:, :, bs], 'xs',
                              hook_mid=(lambda nxt=nxt: restore_h(nxt)) if nxt >= 0 else None)
            phase(11)
            for blk in range(NBLK):
                bs = slice(blk * NT, (blk + 1) * NT)
                has_left, has_right = blk > 0, blk < NBLK - 1
                if blk == 0:
                    em.barrier()
                em.dma('sp', xT, xs_scr[:, :, bs], reads=['xs'], writes=['xT'])
                if has_left:
                    em.dma('sp', xh[:, :, 0:1], xs_scr[:, :, blk * NT - 1:blk * NT], reads=['xs'], writes=['xh'])
                else:
                    em.op('pool', lambda e: e.memset(xh[:, :, 0:1], 1.0), writes=['xh'])
                if has_right:
                    em.dma('sp', xh[:, :, 1:2], xs_scr[:, :, (blk + 1) * NT:(blk + 1) * NT + 1], reads=['xs'], writes=['xh'])
                else:
                    em.op('pool', lambda e: e.memset(xh[:, :, 1:2], 1.0), writes=['xh'])
                rmsnorm_mod(1, 1)
                rms_cols(1, 1, lambda kc: xh[:, kc, :], lambda kc: hTh[:, kc, :], 2, 'hTh', 'xh')
                ffn(l, 1, V_CONV, True, has_left, has_right)
                em.dma('sp', (ys_out if last else xs_scr2)[:, :, bs], xT, reads=['xT'], writes=['xs2'])
            em.barrier()
        except _Stop:
            pass

        with nc.allow_low_precision("bf16 matmul operands, fp32 accumulation"):
            with nc.allow_non_contiguous_dma("halo columns / small strided loads"):
                em.run()
    return nc


_PROGRAM = {}


def _get_program():
    if 'p' not in _PROGRAM:
        _PROGRAM['p'] = build_program(L, True)
    return _PROGRAM['p']


def _na_bias_tables(rpb):
    NEG = np.float32(-1e30)
    cq = np.arange(64)
    win0 = np.clip(cq - 8, 0, 48)
    ck = np.arange(64)
    in_win = (ck[:, None] >= win0[None, :]) & (ck[:, None] < win0[None, :] + 16)
    dc = np.clip(ck[:, None] - cq[None, :] + 15, 0, 30)
    out = np.full((5, 128, 5, 4, 128), NEG, np.float32)
    for vi, r in enumerate([30, 0, 2, 60, 62]):
        t0 = min(max((r - 4) // 2, 0), 27)
        for j in range(5):
            for a in range(2):
                kr = 2 * (t0 + j) + a
                for bq in range(2):
                    rq = r + bq
                    k0 = min(max(rq - 4, 0), 56)
                    if not (0 <= kr - k0 < 8):
                        continue
                    dr = kr - rq + 7
                    for h in range(4):
                        blk = np.where(in_win, rpb[h, dr][dc], NEG)
                        out[vi, a * 64:(a + 1) * 64, j, h, bq * 64:(bq + 1) * 64] = blk
    return out.reshape(5, 128, 5 * 512)


def _host_layout(inp):
    f = lambda a: np.ascontiguousarray(np.asarray(a, dtype=np.float32))
    w_in = f(inp['w_in'])
    offs = np.cumsum([0, 256, 256, 256, 512, 128, 128, 256, 256, 256, 256, 16, 16, 1024, 1024, 1024])
    (o_qa, o_ka, o_va, o_qb, o_kb, o_vb, o_qc, o_kc, o_vc, o_rc, o_zf, o_zb, o_ga, o_gb, o_gc, _) = offs
    rot = (np.arange(64) + 32) % 64
    qb_cols = np.concatenate([o_qb + h * 64 + np.arange(64) for h in QB_PERM])
    qbr_cols = np.concatenate([o_qb + h * 64 + rot for h in QB_PERM])
    kbr_cols = np.concatenate([o_kb + h * 64 + rot for h in range(2)])
    rng = lambda a, n: np.arange(a, a + n)
    fm_cols = np.concatenate([rng(o_qa, 256), rng(o_ka, 256), qb_cols, qbr_cols, rng(o_kb, 128), kbr_cols,
                              rng(o_qc, 256), rng(o_kc, 256), rng(o_rc, 256), rng(o_zf, 32)])
    assert len(fm_cols) == W_FM
    tm_cols = np.concatenate([rng(o_va, 256), rng(o_vb, 128), rng(o_vc, 256), rng(o_kc, 256)])
    shared = {
        'w_mod': f(inp['w_mod']),
        'w_fm': np.ascontiguousarray(w_in[:, :, fm_cols]),
        'w_tm': np.ascontiguousarray(w_in[:, :, tm_cols]),
        'w_g': np.ascontiguousarray(w_in[:, :, o_ga:o_ga + 3072]),
        'w_ba': f(inp['w_branch_a']),
        'w_bb': np.ascontiguousarray(f(inp['w_branch_b']).reshape(L, 8, 64, D)[:, QB_PERM].reshape(L, 512, D)),
        'w_bc': f(inp['w_branch_c']),
        'w_out': f(inp['w_out']),
        'w_up': f(inp['ffn_w_up']),
        'w_dn': f(inp['ffn_w_down']),
    }
    wg2 = f(inp['gla_wg2'])
    bg = f(inp['gla_bg'])
    wz2 = np.zeros((L, 33, 512), np.float32)
    wz2[:, 0:16, 0:256] = wg2[:, 0]
    wz2[:, 16:32, 256:512] = wg2[:, 1]
    wz2[:, 32, 0:256] = bg[:, 0]
    wz2[:, 32, 256:512] = bg[:, 1]
    shared['w_z2'] = wz2
    vecs = np.zeros((128, NV), np.float32)
    col128 = lambda v: v.reshape(-1, 128).T
    rep64 = lambda v: np.concatenate([v, v])
    for l in range(L):
        b = 64 + l * PER_L
        vecs[:, b:b + 48] = col128(f(inp['b_mod'])[l])
        vecs[:, b + 48:b + 56] = col128(f(inp['g_attn'])[l])
        vecs[:, b + 56:b + 64] = col128(f(inp['g_ffn'])[l])
        vecs[:, b + 64] = rep64(f(inp['na_q_norm'])[l])
        vecs[:, b + 65] = rep64(f(inp['na_k_norm'])[l])
        vecs[:, b + 66] = rep64(f(inp['gqa_q_norm'])[l])
        vecs[:, b + 67] = rep64(f(inp['gqa_k_norm'])[l])
        vecs[:, b + 68] = rep64(f(inp['gla_out_norm'])[l])
        vecs[:, b + 69] = rep64(f(inp['gqa_q_norm'])[l][rot])
        vecs[:, b + 70] = rep64(f(inp['gqa_k_norm'])[l][rot])
        cw = f(inp['ffn_conv_w'])[l]
        cbias = f(inp['ffn_conv_b'])[l]
        for part in range(2):
            cb0 = b + 71 + part * 4 * NFF
            sl = slice(part * DFF, (part + 1) * DFF)
            for k in range(3):
                vecs[:, cb0 + k * NFF:cb0 + (k + 1) * NFF] = col128(cw[k, sl])
            vecs[:, cb0 + 3 * NFF:cb0 + 4 * NFF] = col128(cbias[sl])
    shared['vecs'] = vecs
    s_idx = np.arange(128)[:, None]
    t_idx = np.arange(128)[None, :]
    blk = ((s_idx // 64) == (t_idx // 64))
    mf = (s_idx <= t_idx)
    mb = (s_idx >= t_idx)
    consts = np.concatenate([mf, mb, (s_idx > t_idx), (s_idx < t_idx), blk, np.ones((128, 128), bool), (s_idx == t_idx),
                             mf, mf, mf, mf, mb, mb, mb, mb], axis=1).astype(np.float32)
    assert consts.shape[1] == NCONST
    shared['consts'] = consts
    t = np.arange(NTS)
    n_freq = 16
    inv_freq = 10000.0 ** (-np.arange(n_freq) / n_freq)
    ang = np.concatenate([(t // 64)[:, None] * inv_freq, (t % 64)[:, None] * inv_freq], axis=-1)
    cosT = np.cos(ang).astype(np.float32).T
    sinT = np.sin(ang).astype(np.float32).T
    c64 = np.concatenate([cosT, cosT], 0)
    s64 = np.concatenate([-sinT, sinT], 0)
    shared['ropeC'] = np.ascontiguousarray(np.concatenate([c64, c64], 0))
    shared['ropeS'] = np.ascontiguousarray(np.concatenate([s64, s64], 0))
    rpb = f(inp['na_rpb'])
    shared['nab'] = np.stack([_na_bias_tables(rpb[l]) for l in range(L)], 0)
    return shared


def kernel(**inputs):
    shared = _host_layout(inputs)
    f = lambda a: np.asarray(a, dtype=np.float32)
    xp = f(inputs['x_prompt'])
    xsm = f(inputs['x_sample'])
    c_ctx = f(inputs['c_ctx'])
    cc = f(inputs['c'])
    per_b = []
    for b in range(2):
        d = {}
        d['xsT0'] = np.ascontiguousarray(xsm[b].T.reshape(8, 128, NTS).transpose(1, 0, 2))
        cond = np.stack([c_ctx.reshape(8, 128).T, cc[b].reshape(8, 128).T], axis=-1)
        d['cond'] = np.ascontiguousarray(cond.reshape(128, 16))
        nk = f(inputs['cache_na_k'])[b]
        d['cnak'] = np.ascontiguousarray(nk.reshape(L, 512, 2, 128).transpose(0, 3, 2, 1))
        d['cnav'] = np.ascontiguousarray(f(inputs['cache_na_v'])[b].reshape(L, 512, 256))
        d['cgk'] = np.ascontiguousarray(f(inputs['cache_gqa_k'])[b].reshape(L, 512, 128).transpose(0, 2, 1))
        d['cgv'] = np.ascontiguousarray(f(inputs['cache_gqa_v'])[b].reshape(L, 512, 128))
        for nm, key in (('sgf', 'state_gla_fwd'), ('sgb', 'state_gla_bwd')):
            st = f(inputs[key])[b]
            d[nm] = np.ascontiguousarray(st.reshape(L, 2, 2, 64, 64).transpose(0, 2, 3, 1, 4).reshape(L, 128, 2, 64))
        per_b.append(d)
    in_maps = []
    for c in range(NCORES):
        xs = xp[4 * c:4 * c + 4].reshape(NT, D)
        m = dict(shared)
        m.update(per_b[c // 4])
        m['xT0'] = np.ascontiguousarray(xs.T.reshape(8, 128, NT).transpose(1, 0, 2))
        in_maps.append(m)
    nc = _get_program()
    res = run_bass_kernel_spmd(nc, in_maps, core_ids=list(range(NCORES)))
    R = res.results
    B, S = 32, SEQ
    y_prompt = np.zeros((B, S, D), np.float32)
    y_sample = np.zeros((2, NTS, D), np.float32)
    na_k = np.zeros((B, L, S, 4, 64), np.float32)
    na_v = np.zeros((B, L, S, 4, 64), np.float32)
    gq_k = np.zeros((B, L, S, 2, 64), np.float32)
    gq_v = np.zeros((B, L, S, 2, 64), np.float32)
    s_f = np.zeros((B, L, 4, 64, 64), np.float32)
    s_b = np.zeros((B, L, 4, 64, 64), np.float32)
    for c in range(NCORES):
        r = R[c]
        yT = np.asarray(r['yT'])
        y_prompt[4 * c:4 * c + 4] = yT.transpose(2, 1, 0).reshape(4, S, D)
        if c % 4 == 0:
            y_sample[c // 4] = np.asarray(r['ysT']).transpose(2, 1, 0).reshape(NTS, D)
        nak = np.asarray(r['o_nak'])
        na_k[4 * c:4 * c + 4] = nak.reshape(L, 4, 64, 4, S).transpose(3, 0, 4, 1, 2)
        nav = np.asarray(r['o_nav'])
        na_v[4 * c:4 * c + 4] = nav.reshape(L, 4, S, 4, 64).transpose(1, 0, 2, 3, 4)
        gk = np.asarray(r['o_gk'])
        gq_k[4 * c:4 * c + 4] = gk.reshape(L, 2, 64, 4, S).transpose(3, 0, 4, 1, 2)
        gv = np.asarray(r['o_gv'])
        gq_v[4 * c:4 * c + 4] = gv.reshape(L, 4, S, 2, 64).transpose(1, 0, 2, 3, 4)
        sf = np.asarray(r['o_sf'])
        s_f[4 * c:4 * c + 4] = sf.reshape(L, 4, 4, 64, 64).transpose(1, 0, 2, 3, 4)
        sb = np.asarray(r['o_sb'])
        s_b[4 * c:4 * c + 4] = sb.reshape(L, 4, 4, 64, 64).transpose(1, 0, 2, 3, 4)
    return (y_prompt, y_sample, na_k, na_v, gq_k, gq_v, s_f, s_b)
```

```python
from contextlib import ExitStack
import numpy as np
import concourse.bass as bass
import concourse.mybir as mybir
from concourse.bass_utils import run_bass_kernel_spmd

F32 = mybir.dt.float32
BF16 = mybir.dt.bfloat16
AF = mybir.ActivationFunctionType
ALU = mybir.AluOpType

NDMA = 24
L = 4
D = 1024
NT = 1024
SEQ = 256
DFF = 2816
NFF = 22
EPS = 1e-6
NCORES = 8


class _Rec:
    def __init__(self):
        self.calls = []

    def __getattr__(self, name):
        def f(*a, **k):
            self.calls.append((name, a, k))
        return f


class Emit:
    def __init__(self, nc, es):
        self.nc = nc
        self.engs = {'pe': nc.tensor, 'act': nc.scalar, 'dve': nc.vector, 'pool': nc.gpsimd, 'sp': nc.sync}
        self.thunks = {e: [] for e in self.engs}
        self.seq = {e: 0 for e in self.engs}
        self.sem = {e: es.enter_context(nc.semaphore("s_" + e)) for e in self.engs}
        self.dsem = [es.enter_context(nc.semaphore("d%d" % i)) for i in range(NDMA)]
        self.dcnt = [0] * NDMA
        self.dnext2 = [0, 0]
        self.waited = {}
        self.last_w = {}
        self.readers = {}

    def _deps(self, reads, writes):
        deps = {}

        def add(p):
            if p is None:
                return
            prod, val = p
            if deps.get(prod, 0) < val:
                deps[prod] = val

        for k in reads:
            add(self.last_w.get(k))
        for k in writes:
            add(self.last_w.get(k))
            for r in self.readers.get(k, ()):
                add(r)
        return deps

    def _emit_waits(self, e, deps):
        for prod, val in deps.items():
            if prod == e and e == 'pe':
                continue
            if self.waited.get((e, prod), 0) >= val:
                continue
            self.waited[(e, prod)] = val
            sem = self.sem[prod] if isinstance(prod, str) else self.dsem[prod]
            self.thunks[e].append(lambda eng, sem=sem, val=val: eng.wait_ge(sem, val))

    def _record(self, me, reads, writes):
        for k in writes:
            self.last_w[k] = me
            self.readers[k] = []
        for k in reads:
            self.readers.setdefault(k, []).append(me)

    def op(self, e, fns, reads=(), writes=()):
        if not isinstance(fns, (list, tuple)):
            fns = [fns]
        deps = self._deps(reads, writes)
        self._emit_waits(e, deps)
        self.seq[e] += 1
        val = self.seq[e]
        sem = self.sem[e]
        n = len(fns)
        for i, fn in enumerate(fns):
            rec = _Rec()
            fn(rec)
            (name, a, k), = rec.calls
            if i == n - 1:
                self.thunks[e].append(lambda eng, name=name, a=a, k=k, sem=sem: getattr(eng, name)(*a, **k).then_inc(sem, 1))
            else:
                self.thunks[e].append(lambda eng, name=name, a=a, k=k: getattr(eng, name)(*a, **k))
        self._record((e, val), reads, writes)

    def dma(self, q, out, in_, reads=(), writes=(), **kw):
        half = NDMA // 2
        qi = 0 if q == 'sp' else 1
        d = qi * half + self.dnext2[qi]
        self.dnext2[qi] = (self.dnext2[qi] + 1) % half
        deps = self._deps(reads, writes)
        if self.dcnt[d] > 0 and deps.get(d, 0) < self.dcnt[d]:
            deps[d] = self.dcnt[d]
        self._emit_waits(q, deps)
        self.dcnt[d] += 16
        val = self.dcnt[d]
        sem = self.dsem[d]
        self.thunks[q].append(
            lambda eng, out=out, in_=in_, sem=sem, kw=kw: eng.dma_start(out=out, in_=in_, **kw).then_inc(sem, 16))
        self._record((d, val), reads, writes)

    def barrier(self, soft=False):
        if soft:
            for e in ('pe', 'act', 'dve'):
                deps = {}
                for p in ('pe', 'act', 'dve', 'pool'):
                    if self.seq[p] > 0 and not (p == e and e == 'pe'):
                        deps[p] = self.seq[p]
                self._emit_waits(e, deps)
            return
        for e in self.engs:
            deps = {}
            for p in self.engs:
                if p != e and self.seq[p] > 0:
                    deps[p] = self.seq[p]
            if e != 'pe' and self.seq[e] > 0:
                deps[e] = self.seq[e]
            for d in range(NDMA):
                if self.dcnt[d] > 0:
                    deps[d] = self.dcnt[d]
            self._emit_waits(e, deps)

    def run(self):
        for d in range(NDMA):
            if self.dcnt[d] > 0:
                self.thunks['sp'].append(lambda eng, sem=self.dsem[d], val=self.dcnt[d]: eng.wait_ge(sem, val))
        with self.nc.Block() as block:
            @block.tensor
            def _(eng):
                for t in self.thunks['pe']:
                    t(eng)

            @block.scalar
            def _(eng):
                for t in self.thunks['act']:
                    t(eng)

            @block.vector
            def _(eng):
                for t in self.thunks['dve']:
                    t(eng)

            @block.gpsimd
            def _(eng):
                for t in self.thunks['pool']:
                    t(eng)

            @block.sync
            def _(eng):
                for t in self.thunks['sp']:
                    t(eng)


C_QA, C_KA, C_QB, C_QBR, C_KB, C_KBR, C_QC, C_KC, C_RC, C_Z = 0, 256, 512, 1024, 1536, 1664, 1792, 2048, 2304, 2560
W_FM = 2592
T_VA, T_VB, T_VC, T_KC = 0, 256, 384, 640
W_TM = 896
QB_PERM = [0, 4, 1, 5, 2, 6, 3, 7]
NTS = 4096
NBLK = 4
PER_L = 48 + 8 + 8 + 7 + 4 * 2 * NFF
NV = 64 + L * PER_L
NCONST = 7 * 128 + 2 * 512
ARENA = 28672


class _Stop(Exception):
    pass


def build_program(n_layers=L, do_sample=True, stop=99):
    nc = bass.Bass("TRN2", target_bir_lowering=False)
    dt_in = lambda name, shape: nc.dram_tensor(name, shape, F32, kind="ExternalInput").ap()
    dt_out = lambda name, shape: nc.dram_tensor(name, shape, F32, kind="ExternalOutput").ap()
    x_in = dt_in("xT0", [128, 8, NT])
    xs_in = dt_in("xsT0", [128, 8, NTS])
    cond_in = dt_in("cond", [128, 16])
    w_mod = dt_in("w_mod", [L, D, 6 * D])
    w_fm = dt_in("w_fm", [L, D, W_FM])
    w_tm = dt_in("w_tm", [L, D, W_TM])
    w_g = dt_in("w_g", [L, D, 3 * D])
    w_ba = dt_in("w_ba", [L, 256, D])
    w_bb = dt_in("w_bb", [L, 512, D])
    w_bc = dt_in("w_bc", [L, 256, D])
    w_out = dt_in("w_out", [L, D, D])
    w_up = dt_in("w_up", [L, D, 2 * DFF])
    w_dn = dt_in("w_dn", [L, DFF, D])
    w_z2 = dt_in("w_z2", [L, 33, 512])
    vecs_in = dt_in("vecs", [128, NV])
    consts_in = dt_in("consts", [128, NCONST])
    ropeC_in = dt_in("ropeC", [128, NTS])
    ropeS_in = dt_in("ropeS", [128, NTS])
    nab_in = dt_in("nab", [L, 5, 128, 5 * 512])
    cnak_in = dt_in("cnak", [L, 128, 2, 512])
    cnav_in = dt_in("cnav", [L, 512, 256])
    cgk_in = dt_in("cgk", [L, 128, 512])
    cgv_in = dt_in("cgv", [L, 512, 128])
    sgf_in = dt_in("sgf", [L, 128, 2, 64])
    sgb_in = dt_in("sgb", [L, 128, 2, 64])

    y_out = dt_out("yT", [128, 8, NT])
    ys_out = dt_out("ysT", [128, 8, NTS])
    o_nak = dt_out("o_nak", [L, 256, NT])
    o_nav = dt_out("o_nav", [L, NT, 256])
    o_gk = dt_out("o_gk", [L, 128, NT])
    o_gv = dt_out("o_gv", [L, NT, 128])
    o_sf = dt_out("o_sf", [L, 4, 2, 128, 64])
    o_sb = dt_out("o_sb", [L, 4, 2, 128, 64])

    scr = lambda name, shape, dt: nc.dram_tensor(name, shape, dt).ap()
    xp_scr = scr("xp_scr", [128, 8, NT], F32)
    xs_scr = scr("xs_scr", [128, 8, NTS], F32)
    xs_scr2 = scr("xs_scr2", [128, 8, NTS], F32)
    sav_arena = scr("sav_arena", [NBLK, 128, ARENA], BF16)
    sav_hT = scr("sav_hT", [NBLK, 128, 8 * NT], BF16)
    kaT_scr = scr("kaT_scr", [128, 2, NTS], BF16)
    va_scr = scr("va_scr", [NTS, 256], BF16)
    sprev_scr = scr("sprev_scr", [32, 128, 128], BF16)

    with ExitStack() as es:
        T = lambda name, shape, dt: es.enter_context(nc.sbuf_tensor("sb_" + name, shape, dt))
        xreg = T("xreg", [128, 16384], BF16)
        xT = xreg[:].bitcast(F32).rearrange("p (c t) -> p c t", c=8)
        Kslab = xreg[:, 0:4096].rearrange("p (c t) -> p c t", c=2)
        Vslab = xreg[:, 4096:8192].rearrange("p (i c) -> p i c", i=16)
        nabI = xreg[:, 8192:10752]
        nabE = xreg[:, 10752:13312]
        hT = T("hT", [128, 8, NT], BF16)
        hTh = T("hTh", [128, 8, 2], BF16)
        xh = T("xh", [128, 8, 2], F32)
        wbuf = [T("wbuf%d" % i, [128, 8, 512], BF16) for i in range(2)]
        wdn2 = [T("wdn%d" % i, [128, NFF, 128], BF16) for i in range(2)]
        brw2 = [T("brw%d" % i, [128, 8, 128], BF16) for i in range(2)]
        arena = T("arena", [128, ARENA], BF16)
        kbT_all = T("kbT_all", [128, NTS + 512], BF16)
        vaug = T("vaug", [128, 36, 2, 128], BF16)
        vecs = T("vecs", [128, NV], F32)
        cb = T("cb", [128, NCONST], BF16)
        modT = T("modT", [128, 2, 48], F32)
        scA = T("scA", [128, 2, 16], F32)
        condf = T("condf", [128, 16], F32)
        condb = T("condb", [128, 16], BF16)
        zT = T("zT", [33, NT], BF16)
        wz2 = T("wz2", [33, 512], BF16)
        tf = [T("tf%d" % i, [128, 512], F32) for i in range(6)]
        tb = [T("tb%d" % i, [128, 512], BF16) for i in range(4)]
        trC = [T("trC%d" % i, [128, 512], F32) for i in range(2)]
        trS = [T("trS%d" % i, [128, 512], F32) for i in range(2)]
        S32 = [T("S32_%d" % i, [128, 2, 64], F32) for i in range(2)]
        qtil = [[T("qtil%d%d" % (d, c), [128, 256], BF16) for c in range(2)] for d in range(2)]
        attm = [[T("attm%d%d" % (d, c), [128, 512], BF16) for c in range(2)] for d in range(2)]
        Sprev = [[T("Sprev%d%d" % (d, c), [128, 2, 64], BF16) for c in range(2)] for d in range(2)]
        dec = [T("dec%d" % d, [128, 2], F32) for d in range(2)]
        onec = T("onec", [128, 1], F32)
        gsc = T("gsc", [128, 4], F32)
        qpad = [T("qpad%d" % i, [128, 512], BF16) for i in range(2)]
        epsc = T("epsc", [128, 1], F32)
        PS = [es.enter_context(nc.psum_tensor("ps%d" % i, [128, 512], F32)) for i in range(8)]

        off = [0]

        def carve(n, shape_str=None, **kw):
            a = arena[:, off[0]:off[0] + n]
            off[0] += n
            return a.rearrange(shape_str, **kw) if shape_str else a

        qaT = carve(2 * NT, "p (c t) -> p c t", c=2)
        qbT = carve(4 * NT, "p (c t) -> p c t", c=4)
        qcT = carve(2 * NT, "p (c t) -> p c t", c=2)
        mT = arena[:, 0:8 * NT].rearrange("p (c t) -> p c t", c=8)
        kcT = carve(2 * NT, "p (c t) -> p c t", c=2)
        rcT = carve(2 * NT, "p (c t) -> p c t", c=2)
        vtok = carve(8 * 256, "p (i c) -> p i c", i=8)
        ktok = carve(8 * 256, "p (i c) -> p i c", i=8)
        latok = carve(8 * 512, "p (i c) -> p i c", i=8)
        oaT = carve(2 * NT, "p (c t) -> p c t", c=2)
        obT = carve(4 * NT, "p (c t) -> p c t", c=4)
        ocT = carve(2 * NT, "p (c t) -> p c t", c=2)
        assert off[0] <= ARENA, off[0]
        actT = arena[:, 0:NFF * NT].rearrange("p (c t) -> p c t", c=NFF)

        em = Emit(nc, es)
        ps_rr = [0]

        def phase(k):
            if stop == k:
                raise _Stop()

        em.dma('sp', vecs[:], vecs_in, writes=['vecs'])
        em.dma('pool', cb[:], consts_in, writes=['cb'])
        em.op('pool', lambda e: e.memset(onec[:], 1.0), writes=['onec'])
        em.op('pool', lambda e: e.memset(epsc[:], EPS), writes=['epsc'])
        em.op('pool', lambda e: e.memset(zT[:], 1.0), writes=['zT'])
        em.op('pool', lambda e: e.memset(vaug[:], 1.0), writes=['vb_all'])
        for i_ in range(2):
            em.op('pool', lambda e: e.memset(qpad[i_][:], 0.0), writes=['qpad%d' % i_])
        TRI_F, TRI_B, TRIX_F, TRIX_B, BLK, ONES, IDENT = [cb[:, i * 128:(i + 1) * 128] for i in range(7)]
        MASK4 = [cb[:, 896:1408], cb[:, 1408:1920]]
        em.dma('sp', condf[:], cond_in, writes=['condf'])
        em.op('act', lambda e: e.activation(condb[:], condf[:], AF.Silu), reads=['condf'], writes=['condb'])
        condb3 = condb[:].rearrange("p (k c) -> p k c", c=2)

        def vcol(base, j):
            return vecs[:, base + j:base + j + 1]

        def load_w(buf, key, src, ncols, nk=8):
            em.dma('pool', buf[:, 0:nk, 0:ncols], src.rearrange("(kc p) c -> p kc c", p=128), writes=[key])

        wpar = [0]

        def next_wbuf():
            i = wpar[0]
            wpar[0] ^= 1
            return wbuf[i], 'wbuf%d' % i

        def next_ps():
            i = (0, 1, 4, 5)[ps_rr[0]]
            ps_rr[0] = (ps_rr[0] + 1) % 4
            return PS[i], 'ps%d' % i

        def rms_cols(which, ci, xsrc, hdst, w, hkey, xkey):
            sh_base = (0 if which == 0 else 24)
            for kc in range(8):
                em.op('act', lambda e, kc=kc: e.activation(tb[0][:, 0:w], xsrc(kc), AF.Square),
                      reads=[xkey], writes=['tb0'])
                em.op('pe', lambda e, kc=kc: e.matmul(PS[2][:, 0:w], ONES, tb[0][:, 0:w], start=(kc == 0), stop=(kc == 7)),
                      reads=['tb0', 'cb'], writes=['ps2'])
            em.op('act', lambda e: e.activation(tf[0][:, 0:w], PS[2][:, 0:w], AF.Sqrt, bias=epsc[:, 0:1], scale=1.0 / D),
                  reads=['epsc'], writes=['ps2', 'tf0'])
            em.op('dve', lambda e: e.reciprocal(tf[1][:, 0:w], tf[0][:, 0:w]), reads=['tf0'], writes=['tf1'])
            for kc in range(8):
                em.op('dve', lambda e, kc=kc: e.tensor_tensor(tf[2][:, 0:w], xsrc(kc), tf[1][:, 0:w], ALU.mult),
                      reads=[xkey, 'tf1'], writes=['tf2'])
                em.op('act', lambda e, kc=kc: e.activation(hdst(kc), tf[2][:, 0:w], AF.Identity,
                                                           bias=modT[:, ci, sh_base + kc:sh_base + kc + 1],
                                                           scale=scA[:, ci, which * 8 + kc:which * 8 + kc + 1]),
                      reads=['tf2', 'modT', 'scA'], writes=[hkey])

        def rmsnorm_mod(which, ci):
            for th in range(2):
                ts = slice(th * 512, th * 512 + 512)
                rms_cols(which, ci, lambda kc, ts=ts: xT[:, kc, ts], lambda kc, ts=ts: hT[:, kc, ts], 512, ('hT', th), 'xT')

        def proj_fm(wb, wkey, c0, n, th, ps, pskey, rhsT=None, rkey='hT', nk=8):
            r = hT if rhsT is None else rhsT
            ts = slice(th * 512, th * 512 + 512)
            em.op('pe', [lambda e, kc=kc: e.matmul(ps[0:n, :], wb[:, kc, c0:c0 + n], r[:, kc, ts],
                                                  start=(kc == 0), stop=(kc == nk - 1)) for kc in range(nk)],
                  reads=[wkey, (rkey, th)], writes=[pskey])

        hn_par = [0]

        def head_norm_core(ps, pskey, gcol, dst=None, dkey='tf4'):
            dst = tf[4][:] if dst is None else dst
            sq, sqk, t0, t0k, t1, t1k, pb = tb[1], 'tb1', tf[0], 'tf0', tf[1], 'tf1', 2
            pbk = 'ps%d' % pb
            em.op('act', lambda e: e.activation(sq[:], ps[:], AF.Square), writes=[pskey, sqk])
            em.op('pe', lambda e: e.matmul(PS[pb][:], BLK, sq[:], start=True, stop=True),
                  reads=[sqk, 'cb'], writes=[pbk])
            em.op('act', lambda e: e.activation(t0[:], PS[pb][:], AF.Ln, bias=epsc[:, 0:1], scale=1.0 / 64),
                  reads=['epsc'], writes=[pbk, t0k])
            em.op('act', lambda e: e.activation(t1[:], t0[:], AF.Exp, scale=-0.5), reads=[t0k], writes=[t1k])
            em.op('dve', lambda e: e.scalar_tensor_tensor(dst, ps[:], gcol, t1[:], ALU.mult, ALU.mult),
                  reads=[t1k, 'vecs', 'gsc'], writes=[pskey, dkey])

        def mod_phase(l, V_BMOD, V_GATT, V_GFFN):
            for g in range(12):
                wb, wkey = next_wbuf()
                load_w(wb, wkey, w_mod[l][:, g * 512:(g + 1) * 512], 512)
                for j in range(4):
                    col = g * 4 + j
                    em.op('pe', [lambda e, kc=kc, j=j, col=col, wb=wb: e.matmul(
                        PS[3][:, 2 * col:2 * col + 2], wb[:, kc, j * 128:(j + 1) * 128], condb3[:, kc, :],
                        start=(kc == 0), stop=(kc == 7)) for kc in range(8)],
                        reads=[wkey, 'condb'], writes=['ps3'])
            ps3v = PS[3][:, 0:96].rearrange("p (j c) -> p j c", c=2)
            for ci in range(2):
                em.op('dve', lambda e, ci=ci: e.tensor_tensor(modT[:, ci, :], ps3v[:, :, ci], vecs[:, V_BMOD:V_BMOD + 48], ALU.add),
                      reads=['vecs'], writes=['ps3', 'modT'])
                em.op('dve', lambda e, ci=ci: e.scalar_tensor_tensor(scA[:, ci, 0:8], modT[:, ci, 8:16], 1.0,
                                                                    vecs[:, V_GATT:V_GATT + 8], ALU.add, ALU.mult),
                      reads=['modT', 'vecs'], writes=['scA'])
                em.op('dve', lambda e, ci=ci: e.scalar_tensor_tensor(scA[:, ci, 8:16], modT[:, ci, 32:40], 1.0,
                                                                    vecs[:, V_GFFN:V_GFFN + 8], ALU.add, ALU.mult),
                      reads=['modT', 'vecs'], writes=['scA'])

        def inproj(l, blk, V_NRM):
            smp = blk is not None
            t0g = 0 if not smp else blk * NT
            for k_, src_ in enumerate((0, 2, 5)):
                em.op('pool', lambda e: e.tensor_scalar_mul(gsc[:, k_:k_ + 1], vcol(V_NRM, src_), 0.125), reads=['vecs'],
                      writes=['gsc'])
            if smp:
                for th in range(2):
                    em.dma('sp', trC[th][:], ropeC_in[:, t0g + th * 512:t0g + (th + 1) * 512], writes=['trC%d' % th])
                    em.dma('sp', trS[th][:], ropeS_in[:, t0g + th * 512:t0g + (th + 1) * 512], writes=['trS%d' % th])

            def sink_bf(out_bf, out_key, scale):
                em.op('act', lambda e: e.activation(out_bf, tf[4][:], AF.Identity, scale=scale), reads=['tf4'],
                      writes=[out_key])

            def chunk(wb, wkey, cc, n, col, wb2=None, wkey2=None):
                for th in range(2):
                    ts = slice(th * 512, th * 512 + 512)
                    gts = slice(t0g + th * 512, t0g + th * 512 + 512)
                    ps, pskey = next_ps()
                    proj_fm(wb, wkey, cc * 128, n, th, ps, pskey)
                    if col < C_KA:
                        c = (col - C_QA) // 128
                        head_norm_core(ps, pskey, gsc[:, 0:1], qaT[:, c, ts], ('qaT', th))
                    elif col < C_QB:
                        c = (col - C_KA) // 128
                        if not smp:
                            head_norm_core(ps, pskey, vcol(V_NRM, 1))
                            em.dma('sp', o_nak[l][c * 128:(c + 1) * 128, ts], tf[4][:], reads=['tf4'])
                            em.op('act', lambda e: e.copy(Kslab[:, c, ts], tf[4][:]), reads=['tf4'], writes=['Kslab'])
                        else:
                            head_norm_core(ps, pskey, vcol(V_NRM, 1), tb[2][:], 'tb2')
                            em.dma('sp', kaT_scr[:, c, gts], tb[2][:], reads=['tb2'], writes=['kaT_scr'])
                    elif col < C_QBR or (C_KB <= col < C_KBR):
                        isq = col < C_QBR
                        c = (col - C_QB) // 128 if isq else 0
                        if isq:
                            dst, dkey = qbT[:, c, ts], ('qbT', th)
                            g0_, g1_ = gsc[:, 1:2], gsc[:, 2:3]
                        else:
                            dst, dkey = kbT_all[:, gts], 'kbT_all'
                            g0_, g1_ = vcol(V_NRM, 3), vcol(V_NRM, 6)
                        if not smp:
                            if isq:
                                head_norm_core(ps, pskey, g0_, dst, dkey)
                            else:
                                head_norm_core(ps, pskey, g0_)
                                em.dma('sp', o_gk[l][:, ts], tf[4][:], reads=['tf4'])
                                em.op('act', lambda e: e.copy(dst, tf[4][:]), reads=['tf4'], writes=[dkey])
                        else:
                            head_norm_core(ps, pskey, g0_)
                            em.op('dve', lambda e: e.tensor_tensor(tf[5][:], tf[4][:], trC[th][:], ALU.mult),
                                  reads=['tf4', 'trC%d' % th], writes=['tf5'])
                            ps2_, ps2key = next_ps()
                            rc0 = (cc * 128) if isq else (cc + 1) * 128
                            wbr, wkr = (wb2, wkey2) if isq else (wb, wkey)
                            proj_fm(wbr, wkr, rc0, 128, th, ps2_, ps2key)
                            head_norm_core(ps2_, ps2key, g1_)
                            em.op('dve', lambda e: e.tensor_tensor(tf[4][:], tf[4][:], trS[th][:], ALU.mult),
                                  reads=['trS%d' % th], writes=['tf4'])
                            em.op('dve', lambda e: e.tensor_tensor(dst, tf[4][:], tf[5][:], ALU.add),
                                  reads=['tf4', 'tf5'], writes=[dkey])
                    elif col < C_KC:
                        c = (col - C_QC) // 128
                        em.op('act', lambda e, ps=ps, c=c: e.activation(qcT[:, c, ts], ps[:], AF.Identity, scale=0.125),
                              writes=[pskey, ('qcT', th)])
                    elif col < C_RC:
                        c = (col - C_KC) // 128
                        em.op('act', lambda e, ps=ps, c=c: e.copy(kcT[:, c, ts], ps[:]), writes=[pskey, ('kcT', th)])
                    elif col < C_Z:
                        c = (col - C_RC) // 128
                        em.op('act', lambda e, ps=ps, c=c: e.activation(rcT[:, c, ts], ps[:], AF.Silu),
                              writes=[pskey, ('rcT', th)])
                    else:
                        em.op('act', lambda e, ps=ps: e.copy(zT[0:32, ts], ps[0:32, :]), writes=[pskey, 'zT'])

            wb, wkey = next_wbuf()
            load_w(wb, wkey, w_fm[l][:, 0:512], 512)
            for cc in range(4):
                chunk(wb, wkey, cc, 128, cc * 128)
            wb, wkey = next_wbuf()
            load_w(wb, wkey, w_fm[l][:, C_QB:C_QB + 512], 512)
            wb2 = wkey2 = None
            if smp:
                wb2, wkey2 = next_wbuf()
                load_w(wb2, wkey2, w_fm[l][:, C_QBR:C_QBR + 512], 512)
            for cc in range(4):
                chunk(wb, wkey, cc, 128, C_QB + cc * 128, wb2, wkey2)
            wb, wkey = next_wbuf()
            load_w(wb, wkey, w_fm[l][:, C_KB:C_KB + 512], 512)
            chunk(wb, wkey, 0, 128, C_KB)
            chunk(wb, wkey, 2, 128, C_QC)
            chunk(wb, wkey, 3, 128, C_QC + 128)
            wb, wkey = next_wbuf()
            load_w(wb, wkey, w_fm[l][:, C_KC:C_KC + 512], 512)
            for cc in range(4):
                chunk(wb, wkey, cc, 128, C_KC + cc * 128)
            wb, wkey = next_wbuf()
            load_w(wb, wkey, w_fm[l][:, C_Z:C_Z + 32], 32)
            chunk(wb, wkey, 0, 32, C_Z)

            wtm = [None, None]
            for gi, (g0, gn) in enumerate([(0, 512), (512, 384)]):
                wb, wkey = next_wbuf()
                load_w(wb, wkey, w_tm[l][:, g0:g0 + gn], gn)
                wtm[gi] = (wb, wkey)
            em.dma('pool', wz2[:], w_z2[l], writes=['wz2'])
            for i in range(8):
                tsl = slice(i * 128, (i + 1) * 128)
                gi_tile = (0 if not smp else blk * 8) + i
                th = i // 4
                for gi, (g0, gn) in enumerate([(0, 512), (512, 384)]):
                    wb, wkey = wtm[gi]
                    ps, pskey = next_ps()
                    em.op('pe', [lambda e, kc=kc, wb=wb, ps=ps, gn=gn: e.matmul(
                        ps[:, 0:gn], hT[:, kc, tsl], wb[:, kc, 0:gn], start=(kc == 0), stop=(kc == 7)) for kc in range(8)],
                        reads=[wkey, ('hT', th)], writes=[pskey])
                    if gi == 0:
                        if not smp:
                            em.op('act', lambda e, ps=ps: e.copy(tf[5][:, 0:512], ps[:, 0:512]), writes=[pskey, 'tf5'])
                            em.dma('sp', o_nav[l][tsl, :], tf[5][:, 0:256], reads=['tf5'])
                            em.dma('sp', o_gv[l][tsl, :], tf[5][:, 256:384], reads=['tf5'])
                            em.op('dve', lambda e: e.tensor_copy(Vslab[:, i, :], tf[5][:, 0:256]), reads=['tf5'], writes=['Vslab'])
                            em.op('dve', lambda e: e.tensor_copy(vaug[:, i, 0, 0:64], tf[5][:, 256:320]), reads=['tf5'], writes=['vb_all'])
                            em.op('dve', lambda e: e.tensor_copy(vaug[:, i, 1, 64:128], tf[5][:, 320:384]), reads=['tf5'], writes=['vb_all'])
                            em.op('dve', lambda e: e.tensor_copy(vtok[:, i, 0:128], tf[5][:, 384:512]), reads=['tf5'],
                                  writes=[('vtok', i)])
                        else:
                            em.op('act', lambda e, ps=ps: e.copy(tb[1][:], ps[:]), writes=[pskey, 'tb1'])
                            em.dma('sp', va_scr[gi_tile * 128:(gi_tile + 1) * 128, :], tb[1][:, 0:256], reads=['tb1'],
                                   writes=['va_scr'])
                            em.op('dve', lambda e: e.tensor_copy(vaug[:, gi_tile, 0, 0:64], tb[1][:, 256:320]), reads=['tb1'],
                                  writes=['vb_all'])
                            em.op('dve', lambda e: e.tensor_copy(vaug[:, gi_tile, 1, 64:128], tb[1][:, 320:384]), reads=['tb1'],
                                  writes=['vb_all'])
                            em.op('dve', lambda e: e.tensor_copy(vtok[:, i, 0:128], tb[1][:, 384:512]), reads=['tb1'],
                                  writes=[('vtok', i)])
                    else:
                        em.op('act', lambda e, ps=ps: e.copy(vtok[:, i, 128:256], ps[:, 0:128]), writes=[pskey, ('vtok', i)])
                        em.op('dve', lambda e, ps=ps: e.tensor_copy(ktok[:, i, :], ps[:, 128:384]), writes=[pskey, ('ktok', i)])
                ps, pskey = next_ps()
                em.op('pe', lambda e, ps=ps: e.matmul(ps[:], zT[:, tsl], wz2[:], start=True, stop=True),
                      reads=['zT', 'wz2'], writes=[pskey])
                em.op('act', lambda e, ps=ps: e.activation(tf[0][:], ps[:], AF.Exp, scale=-1.0), writes=[pskey, 'tf0'])
                em.op('act', lambda e: e.activation(tf[1][:], tf[0][:], AF.Ln, bias=onec[:, 0:1]), reads=['tf0', 'onec'],
                      writes=['tf1'])
                em.op('dve', lambda e: e.tensor_scalar_mul(latok[:, i, :], tf[1][:], -1.0 / 16.0), reads=['tf1'],
                      writes=[('latok', i)])

        TBK = ['tb0', 'tb1', 'tb2', 'tb3']

        def attn_generic(qT, qkey, nchunk, qsl, n, th, tiles_of, oT, okey, aug=False):
            per = 512 // n
            stages = []
            for c in range(nchunk):
                for hh in range(2):
                    tl = tiles_of(c, hh)
                    nt = len(tl)
                    for gi, g0 in enumerate(range(0, nt, per)):
                        stages.append((c, hh, gi, tl[g0:g0 + per], g0, nt))

            def acc_bank(c, hh):
                if not aug:
                    return None
                return ((6, 2) if hh == 0 else (7, 3))[c % 2]

            def slot_of(st):
                c, hh, gi, grp, g0, nt = st
                if aug:
                    k = st_index[id(st)] % 4
                    return (0, 1, 4, 5)[k], k
                return ((0, 1) if hh == 0 else (4, 5))[gi % 2], (0 if hh == 0 else 2) + gi % 2

            def emit_qk(st):
                c, hh, gi, grp, g0, nt = st
                bank, ti = slot_of(st)
                fns = []
                rk = [(qkey, th), 'cb']
                if aug and g0 == 0:
                    em.op('pool', lambda e: e.tensor_copy(qpad[hh][hh * 64:(hh + 1) * 64, 0:n], qT[hh * 64:(hh + 1) * 64, c, qsl]),
                          reads=[(qkey, th)], writes=['qpad%d' % hh])
                if aug:
                    rk.append('qpad%d' % hh)
                for j, (kap, kkeys, vap, vkeys, bias) in enumerate(grp):
                    o = PS[bank][:, j * n:(j + 1) * n]
                    if aug:
                        fns.append(lambda e, o=o, kap=kap: e.matmul(o, kap, qpad[hh][:, 0:n], start=True, stop=True))
                        rk += list(kkeys)
                        continue
                    fns.append(lambda e, o=o, kap=kap, bias=bias: e.matmul(
                        o, kap, qT[hh * 64:(hh + 1) * 64, c, qsl], start=True, stop=(bias is None)))
                    if bias is not None:
                        fns.append(lambda e, o=o, bias=bias: e.matmul(o, IDENT, bias, start=False, stop=True))
                    rk += list(kkeys)
                em.op('pe', fns, reads=rk, writes=['ps%d' % bank])
                w = len(grp) * n
                em.op('act', lambda e: e.activation(tb[ti][:, 0:w], PS[bank][:, 0:w], AF.Exp),
                      writes=['ps%d' % bank, TBK[ti]])

            def emit_pv(st):
                c, hh, gi, grp, g0, nt = st
                bank_, ti = slot_of(st)
                tbt = tb[ti]
                fns = []
                rk = [TBK[ti], 'cb']
                A = acc_bank(c, hh)
                for j, (kap, kkeys, vap, vkeys, bias) in enumerate(grp):
                    first = (g0 + j == 0)
                    last = (g0 + j == nt - 1)
                    if aug:
                        fns.append(lambda e, vap=vap, j=j, first=first, last=last: e.matmul(
                            PS[A][:, 0:n], vap, tbt[:, j * n:(j + 1) * n], start=first, stop=last))
                    else:
                        fns.append(lambda e, vap=vap, j=j, first=first, last=last: e.matmul(
                            PS[6][hh * 64:(hh + 1) * 64, 0:n], vap, tbt[:, j * n:(j + 1) * n], start=first, stop=last))
                        fns.append(lambda e, j=j, first=first, last=last: e.matmul(
                            PS[7][hh * 64:(hh + 1) * 64, 0:n], ONES[:, 0:64], tbt[:, j * n:(j + 1) * n], start=first, stop=last))
                    rk += list(vkeys)
                em.op('pe', fns, reads=rk, writes=(['ps%d' % A] if aug else ['ps6', 'ps7']))
                if g0 + len(grp) < nt:
                    return
                if aug:
                    rn = slice(hh * 64, hh * 64 + 64)
                    rd = slice((1 - hh) * 64, (1 - hh) * 64 + 64)
                    tfx = tf[4 + hh]
                    em.op('dve', lambda e: e.reciprocal(tfx[rn, 0:n], PS[A][rd, 0:n]), writes=['ps%d' % A, 'tf%d' % (4 + hh)])
                    em.op('dve', lambda e: e.tensor_tensor(oT[rn, c, qsl], tfx[rn, 0:n], PS[A][rn, 0:n], ALU.mult),
                          reads=['tf%d' % (4 + hh)], writes=['ps%d' % A, (okey, th)])
                elif hh == 1:
                    em.op('dve', lambda e: e.reciprocal(tf[0][:, 0:n], PS[7][:, 0:n]), writes=['ps7', 'tf0'])
                    em.op('dve', lambda e: e.tensor_tensor(oT[:, c, qsl], PS[6][:, 0:n], tf[0][:, 0:n], ALU.mult),
                          reads=['tf0'], writes=['ps6', (okey, th)])

            st_index = {id(st): k for k, st in enumerate(stages)}
            lag = 2 if aug else 1
            for k, st in enumerate(stages):
                emit_qk(st)
                if k - lag >= 0:
                    emit_pv(stages[k - lag])
            for k in range(max(len(stages) - lag, 0), len(stages)):
                emit_pv(stages[k])

        def attention_prompt():
            for s in range(NT // SEQ):
                th = s // 2
                qs = slice(s * SEQ, (s + 1) * SEQ)

                def tiles_a(c, hh, s=s):
                    h = 2 * c + hh
                    return [(Kslab[hh * 64:(hh + 1) * 64, c, (2 * s + kt) * 128:(2 * s + kt + 1) * 128], ['Kslab'],
                             Vslab[:, 2 * s + kt, h * 64:(h + 1) * 64], ['Vslab'], None) for kt in range(2)]

                def tiles_b(c, hh, s=s):
                    return [(kbT_all[:, (2 * s + kt) * 128:(2 * s + kt + 1) * 128], ['kbT_all'],
                             vaug[:, 2 * s + kt, hh, :], ['vb_all'], None) for kt in range(2)]

                attn_generic(qaT, 'qaT', 2, qs, 256, th, tiles_a, oaT, 'oaT')
                attn_generic(qbT, 'qbT', 4, qs, 256, th, tiles_b, obT, 'obT', aug=True)

        def attention_sample(l, blk):
            sb0 = max(8 * blk - 2, 0)
            sb1 = min(8 * blk + 10, 32)
            nsl = sb1 - sb0
            em.dma('sp', Kslab[:, :, 0:nsl * 128], kaT_scr[:, :, sb0 * 128:sb1 * 128], reads=['kaT_scr'], writes=['Kslab'])
            em.dma('pool', Kslab[:, :, 1536:2048], cnak_in[l], writes=['Kslab'])
            em.dma('sp', Vslab[:, 0:nsl, :], va_scr[sb0 * 128:sb1 * 128, :].rearrange("(i p) c -> p i c", p=128),
                   reads=['va_scr'], writes=['Vslab'])
            em.dma('pool', Vslab[:, 12:16, :], cnav_in[l].rearrange("(i p) c -> p i c", p=128), writes=['Vslab'])
            em.dma('pool', nabI, nab_in[l][0], writes=['nabI'])
            if blk == NBLK - 1:
                em.dma('pool', kbT_all[:, NTS:NTS + 512], cgk_in[l], writes=['kbT_all'])
                cgv3 = cgv_in[l].rearrange("(i p) c -> p i c", p=128)
                em.dma('pool', vaug[:, 32:36, 0, 0:64], cgv3[:, :, 0:64], writes=['vb_all'])
                em.dma('pool', vaug[:, 32:36, 1, 64:128], cgv3[:, :, 64:128], writes=['vb_all'])
            for pr in range(8):
                r = 16 * blk + 2 * pr
                t0 = min(max((r - 4) // 2, 0), 27)
                var = {0: 1, 2: 2, 60: 3, 62: 4}.get(r, 0)
                if var == 0:
                    nb, nbkey = nabI, 'nabI'
                else:
                    em.dma('pool', nabE, nab_in[l][var], writes=['nabE'])
                    nb, nbkey = nabE, 'nabE'
                qsl = slice(pr * 128, (pr + 1) * 128)
                th = pr // 4

                def tiles_a(c, hh, t0=t0, nb=nb, nbkey=nbkey):
                    h = 2 * c + hh
                    tl = []
                    for j in range(5):
                        si = t0 + j - sb0
                        tl.append((Kslab[hh * 64:(hh + 1) * 64, c, si * 128:(si + 1) * 128], ['Kslab', nbkey],
                                   Vslab[:, si, h * 64:(h + 1) * 64], ['Vslab'],
                                   nb[:, j * 512 + h * 128: j * 512 + (h + 1) * 128]))
                    for j in range(4):
                        tl.append((Kslab[hh * 64:(hh + 1) * 64, c, 1536 + j * 128:1536 + (j + 1) * 128], ['Kslab'],
                                   Vslab[:, 12 + j, h * 64:(h + 1) * 64], ['Vslab'], None))
                    return tl

                attn_generic(qaT, 'qaT', 2, qsl, 128, th, tiles_a, oaT, 'oaT')
            for th in range(2):
                qsl = slice(th * 512, (th + 1) * 512)

                def tiles_b(c, hh):
                    return [(kbT_all[:, kt * 128:(kt + 1) * 128], ['kbT_all'],
                             vaug[:, kt, hh, :], ['vb_all'], None) for kt in range(36)]

                attn_generic(qbT, 'qbT', 4, qsl, 512, th, tiles_b, obT, 'obT', aug=True)

        def gla_decay(i, th, d, slot):
            TRI = TRI_F if d == 0 else TRI_B
            tsl = slice(i * 128, (i + 1) * 128)
            em.op('pe', [lambda e, hp=hp: e.matmul(
                PS[4][:, hp * 128:(hp + 1) * 128], latok[:, i, d * 256 + hp * 128:d * 256 + (hp + 1) * 128],
                TRI, start=True, stop=True) for hp in range(2)],
                reads=[('latok', i), 'cb'], writes=['ps4'])
            em.op('act', lambda e: e.activation(tf[0][:, 0:256], PS[4][:, 0:256], AF.Exp), writes=['ps4', 'tf0'])
            em.op('act', lambda e: e.activation(tf[1][:, 0:256], PS[4][:, 0:256], AF.Exp, scale=-1.0), writes=['ps4', 'tf1'])
            qtt = qtil[d][slot]
            em.op('dve', lambda e: e.tensor_tensor(
                qtt[:].rearrange("p (c t) -> p c t", c=2), qcT[:, :, tsl],
                tf[0][:, 0:256].rearrange("p (c t) -> p c t", c=2), ALU.mult),
                reads=[('qcT', th), 'tf0'], writes=[('qtil', d, slot)])
            em.op('dve', lambda e: e.tensor_tensor(
                tb[2][:, 0:256].rearrange("p (c t) -> p c t", c=2), kcT[:, :, tsl],
                tf[1][:, 0:256].rearrange("p (c t) -> p c t", c=2), ALU.mult),
                reads=[('kcT', th), 'tf1'], writes=['tb2'])
            att = attm[d][slot]
            for hh, bank in ((0, 5), (1, 3)):
                em.op('pe', [lambda e, hp=hp: e.matmul(
                    PS[bank][:, hp * 128:(hp + 1) * 128],
                    tb[2][hh * 64:hh * 64 + 64, hp * 128:hp * 128 + 128],
                    qtt[hh * 64:hh * 64 + 64, hp * 128:hp * 128 + 128],
                    start=True, stop=True) for hp in range(2)],
                    reads=['tb2', ('qtil', d, slot)], writes=['ps%d' % bank])
                em.op('dve', lambda e: e.tensor_tensor(
                    att[:].rearrange("p (hp hh t) -> p hp hh t", hp=2, hh=2)[:, :, hh, :],
                    PS[bank][:, 0:256].rearrange("p (hp t) -> p hp t", hp=2),
                    MASK4[d][:, 0:256].rearrange("p (hp t) -> p hp t", hp=2), ALU.mult),
                    reads=['cb'], writes=['ps%d' % bank, ('attm', d, slot)])

        def gla_chain(i, d, slot, save_to=None):
            TRIX = TRIX_F if d == 0 else TRIX_B
            em.op('act', lambda e: e.copy(Sprev[d][slot][:], S32[d][:]), reads=[('S32', d)], writes=[('Sprev', d, slot)])
            if save_to is not None:
                em.dma('sp', save_to, Sprev[d][slot][:].rearrange("p a b -> p (a b)"), reads=[('Sprev', d, slot)],
                       writes=['sprev_scr'])
            em.op('pe', [lambda e, hp=hp: e.matmul(
                PS[7][:, 128 + hp:129 + hp], latok[:, i, d * 256 + hp * 128:d * 256 + (hp + 1) * 128], ONES[:, 0:1],
                start=True, stop=True) for hp in range(2)], reads=[('latok', i), 'cb'], writes=['ps7'])
            em.op('act', lambda e: e.activation(dec[d][:, 0:2], PS[7][:, 128:130], AF.Exp), writes=['ps7', ('dec', d)])
            em.op('pe', lambda e: e.matmul(PS[4][:, 0:256], TRIX, latok[:, i, d * 256:(d + 1) * 256], start=True, stop=True),
                  reads=[('latok', i), 'cb'], writes=['ps4'])
            em.op('act', lambda e: e.activation(tf[2][:, 0:256], PS[4][:, 0:256], AF.Exp), writes=['ps4', 'tf2'])
            em.op('dve', lambda e: e.tensor_tensor(tb[0][:, 0:256], ktok[:, i, :], tf[2][:, 0:256], ALU.mult),
                  reads=[('ktok', i), 'tf2'], writes=['tb0'])
            em.op('pe', [lambda e, h=h: e.matmul(
                PS[7][(h % 2) * 64:(h % 2) * 64 + 64, (h // 2) * 64:(h // 2) * 64 + 64],
                tb[0][:, h * 64:(h + 1) * 64], vtok[:, i, h * 64:(h + 1) * 64],
                start=True, stop=True) for h in range(4)],
                reads=['tb0', ('vtok', i)], writes=['ps7'])
            for hp in range(2):
                em.op('dve', lambda e, hp=hp: e.scalar_tensor_tensor(
                    S32[d][:, hp, :], S32[d][:, hp, :], dec[d][:, hp:hp + 1], PS[7][:, hp * 64:(hp + 1) * 64],
                    ALU.mult, ALU.add), reads=[('dec', d)], writes=['ps7', ('S32', d)])

        def gla_out(i, th, slot, use_inter, V_NRM):
            tsl = slice(i * 128, (i + 1) * 128)
            fns = []
            for h in range(4):
                hh, hp = h % 2, h // 2
                outp = PS[6][hh * 64:hh * 64 + 64, hp * 128:(hp + 1) * 128]
                seqm = []
                for d in range(2):
                    seqm.append((vtok[:, i, h * 64:(h + 1) * 64], attm[d][slot][:, h * 128:(h + 1) * 128]))
                    if use_inter[d]:
                        seqm.append((Sprev[d][slot][hh * 64:hh * 64 + 64, hp, :],
                                     qtil[d][slot][hh * 64:hh * 64 + 64, hp * 128:(hp + 1) * 128]))
                for k, (lt, rh) in enumerate(seqm):
                    fns.append(lambda e, outp=outp, lt=lt, rh=rh, k=k, n=len(seqm): e.matmul(
                        outp, lt, rh, start=(k == 0), stop=(k == n - 1)))
            em.op('pe', fns, reads=[('vtok', i)] + [('attm', d, slot) for d in range(2)] +
                  [('Sprev', d, slot) for d in range(2)] + [('qtil', d, slot) for d in range(2)], writes=['ps6'])
            em.op('act', lambda e: e.activation(tb[1][:, 0:256], PS[6][:, 0:256], AF.Square), writes=['ps6', 'tb1'])
            em.op('dve', lambda e: e.tensor_copy(tf[3][:, 0:256], PS[6][:, 0:256]), writes=['ps6', 'tf3'])
            em.op('pe', lambda e: e.matmul(PS[2][:, 0:256], BLK, tb[1][:, 0:256], start=True, stop=True),
                  reads=['tb1', 'cb'], writes=['ps2'])
            em.op('act', lambda e: e.activation(tf[0][:, 0:256], PS[2][:, 0:256], AF.Ln, bias=epsc[:, 0:1],
                                                scale=1.0 / 64), reads=['epsc'], writes=['ps2', 'tf0'])
            em.op('act', lambda e: e.activation(tf[1][:, 0:256], tf[0][:, 0:256], AF.Exp, scale=-0.5), reads=['tf0'],
                  writes=['tf1'])
            em.op('dve', lambda e: e.scalar_tensor_tensor(tf[4][:, 0:256], tf[3][:, 0:256], vcol(V_NRM, 4),
                                                         tf[1][:, 0:256], ALU.mult, ALU.mult),
                  reads=['tf3', 'tf1', 'vecs'], writes=['tf4'])
            em.op('dve', lambda e: e.tensor_tensor(
                ocT[:, :, tsl], tf[4][:, 0:256].rearrange("p (c t) -> p c t", c=2), rcT[:, :, tsl], ALU.mult),
                reads=['tf4', ('rcT', th)], writes=[('ocT', th)])

        def gla_prompt(l, V_NRM):
            for s in range(NT // SEQ):
                th = s // 2
                for d in range(2):
                    order = [0, 1] if d == 0 else [1, 0]
                    em.op('pool', lambda e, d=d: e.memset(S32[d][:], 0.0), writes=[('S32', d)])
                    for ci in order:
                        i = 2 * s + ci
                        gla_decay(i, th, d, ci)
                        gla_chain(i, d, ci)
                    dst = (o_sf if d == 0 else o_sb)[l][s]
                    em.dma('sp', dst.rearrange("hp p v -> p hp v"), S32[d][:], reads=[('S32', d)])
                for ci in range(2):
                    gla_out(2 * s + ci, th, ci, [ci != 0, ci != 1], V_NRM)

        mrr = [0]

        def merge_outproj(l, ci, xsrc, xdst, xkey, hook_mid=None):
            em.barrier(soft=True)
            for j in range(8):
                wb, wkey = next_wbuf()
                for bi in range(3):
                    em.dma('pool', wb[:, :, bi * 128:(bi + 1) * 128],
                           w_g[l][:, bi * 1024 + j * 128: bi * 1024 + (j + 1) * 128].rearrange("(kc p) c -> p kc c", p=128),
                           writes=[wkey])
                brw, brk = brw2[j % 2], 'brw%d' % (j % 2)
                em.dma('pool', brw[:, 0:2, :], w_ba[l][:, j * 128:(j + 1) * 128].rearrange("(kc p) c -> p kc c", p=128), writes=[brk])
                em.dma('pool', brw[:, 2:6, :], w_bb[l][:, j * 128:(j + 1) * 128].rearrange("(kc p) c -> p kc c", p=128), writes=[brk])
                em.dma('pool', brw[:, 6:8, :], w_bc[l][:, j * 128:(j + 1) * 128].rearrange("(kc p) c -> p kc c", p=128), writes=[brk])
                for th in range(2):
                    ts = slice(th * 512, th * 512 + 512)
                    for bi, (oT, okey, k0, nk) in enumerate([(oaT, 'oaT', 0, 2), (obT, 'obT', 2, 4), (ocT, 'ocT', 6, 2)]):
                        ps, pskey = next_ps()
                        proj_fm(wb, wkey, bi * 128, 128, th, ps, pskey)
                        mrr[0] ^= 1
                        sg, sgk = (tf[0], 'tf0') if mrr[0] else (tf[3], 'tf3')
                        pr_, prk = (tf[2], 'tf2') if mrr[0] else (tf[5], 'tf5')
                        bb_ = 3 if mrr[0] else 2
                        em.op('act', lambda e, ps=ps: e.activation(sg[:], ps[:], AF.Sigmoid), writes=[pskey, sgk])
                        em.op('pe', [lambda e, kc=kc, oT=oT, k0=k0, nk=nk: e.matmul(
                            PS[bb_][:], brw[:, k0 + kc, :], oT[:, kc, ts],
                            start=(kc == 0), stop=(kc == nk - 1)) for kc in range(nk)],
                            reads=[brk, (okey, th)], writes=['ps%d' % bb_])
                        if bi == 0:
                            em.op('dve', lambda e: e.tensor_tensor(tf[1][:], PS[bb_][:], sg[:], ALU.mult),
                                  reads=[sgk], writes=['ps%d' % bb_, 'tf1'])
                        else:
                            em.op('dve', lambda e: e.tensor_tensor(pr_[:], PS[bb_][:], sg[:], ALU.mult),
                                  reads=[sgk], writes=['ps%d' % bb_, prk])
                            em.op('dve', lambda e: e.tensor_tensor(tf[1][:], tf[1][:], pr_[:], ALU.add),
                                  reads=[prk], writes=['tf1'])
                    em.op('act', lambda e, j=j: e.copy(mT[:, j, ts], tf[1][:]), reads=['tf1'], writes=[('mT', th)])
            if hook_mid is not None:
                hook_mid()
            tiles = [(g, cc, th) for g in range(2) for cc in range(4) for th in range(2)]

            def xbuf(k):
                return (tf[3], 'tf3') if k % 2 == 0 else (tf[4], 'tf4')

            def xload(k):
                g, cc, th = tiles[k]
                xt_, xtk = xbuf(k)
                em.dma('sp', xt_[:], xsrc[:, g * 4 + cc, th * 512:(th + 1) * 512], reads=[xkey, 'xs2'], writes=[xtk])

            xload(0)
            wb = wkey = None
            for k, (g, cc, th) in enumerate(tiles):
                if cc == 0 and th == 0:
                    wb, wkey = next_wbuf()
                    load_w(wb, wkey, w_out[l][:, g * 512:(g + 1) * 512], 512)
                j = g * 4 + cc
                ts = slice(th * 512, th * 512 + 512)
                ps, pskey = next_ps()
                proj_fm(wb, wkey, cc * 128, 128, th, ps, pskey, rhsT=mT, rkey='mT')
                if k + 1 < len(tiles):
                    xload(k + 1)
                xt_, xtk = xbuf(k)
                em.op('dve', lambda e, ps=ps, j=j: e.scalar_tensor_tensor(
                    xt_[:], ps[:], modT[:, ci, 16 + j:17 + j], xt_[:], ALU.mult, ALU.add),
                    reads=['modT'], writes=[pskey, xtk])
                em.dma('sp', xdst[:, j, ts], xt_[:], reads=[xtk], writes=[xkey])

        def ffn(l, ci, V_CONV, smp, has_left, has_right, xout, xoutkey):
            em.barrier()
            pending_tail = []
            wq = [wbuf[k // 2][:, :, (k % 2) * 256:(k % 2) * 256 + 256] for k in range(4)]
            for fg in range(11):
                nf = 2
                r_ = (fg % 2) * 2
                wa, wakey = wq[r_], 'wq%d' % r_
                em.dma('pool', wa, w_up[l][:, fg * 256:fg * 256 + 256].rearrange("(kc p) c -> p kc c", p=128), writes=[wakey])
                wg, wgkey = wq[r_ + 1], 'wq%d' % (r_ + 1)
                em.dma('pool', wg, w_up[l][:, DFF + fg * 256:DFF + fg * 256 + 256].rearrange("(kc p) c -> p kc c", p=128),
                       writes=[wgkey])
                for cc in range(nf):
                    f = fg * 2 + cc
                    if f % 2 == 0:
                        ACC, ACCK = [tf[0], tf[1], tf[2], tf[3]], ['tf0', 'tf1', 'tf2', 'tf3']
                        SIL, SILK = tf[4], 'tf4'
                    else:
                        ACC, ACCK = [trC[0], trC[1], trS[0], trS[1]], ['trC0', 'trC1', 'trS0', 'trS1']
                        SIL, SILK = tf[5], 'tf5'
                    for part, (wb, wkey) in enumerate([(wa, wakey), (wg, wgkey)]):
                        if part == 1 and len(pending_tail) > 0:
                            pending_tail.pop(0)()
                        cbase = V_CONV + part * 4 * NFF
                        w0 = vcol(cbase, f)
                        w1 = vcol(cbase + NFF, f)
                        w2 = vcol(cbase + 2 * NFF, f)
                        bb = vcol(cbase + 3 * NFF, f)
                        pss = []
                        pb = 0 if part == 0 else 4
                        hb = 2 if part == 0 else 3
                        for th in range(2):
                            ps, pskey = PS[pb + th], 'ps%d' % (pb + th)
                            proj_fm(wb, wkey, cc * 128, 128, th, ps, pskey)
                            pss.append((ps, pskey))
                        if smp and (has_left or has_right):
                            em.op('pe', [lambda e, kc=kc: e.matmul(PS[hb][:, 0:2], wb[:, kc, cc * 128:(cc + 1) * 128],
                                                                  hTh[:, kc, :], start=(kc == 0), stop=(kc == 7))
                                         for kc in range(8)], reads=[wkey, 'hTh'], writes=['ps%d' % hb])
                        for th in range(2):
                            ps, pskey = pss[th]
                            acc = ACC[part * 2 + th]
                            akey = ACCK[part * 2 + th]
                            em.op('act', lambda e, ps=ps, acc=acc: e.activation(acc[:], ps[:], AF.Identity, bias=bb, scale=w1),
                                  reads=['vecs'], writes=[pskey, akey])
                            if not smp:
                                a3 = acc[:].rearrange("p (s t) -> p s t", s=2)
                                p3 = ps[:].rearrange("p (s t) -> p s t", s=2)
                                em.op('dve', lambda e, a3=a3, p3=p3: e.scalar_tensor_tensor(
                                    a3[:, :, 1:256], p3[:, :, 0:255], w0, a3[:, :, 1:256], ALU.mult, ALU.add),
                                    reads=['vecs'], writes=[pskey, akey])
                                em.op('dve', lambda e, a3=a3, p3=p3: e.scalar_tensor_tensor(
                                    a3[:, :, 0:255], p3[:, :, 1:256], w2, a3[:, :, 0:255], ALU.mult, ALU.add),
                                    reads=['vecs'], writes=[pskey, akey])
                            else:
                                em.op('dve', lambda e, acc=acc, ps=ps: e.scalar_tensor_tensor(
                                    acc[:, 1:512], ps[:, 0:511], w0, acc[:, 1:512], ALU.mult, ALU.add),
                                    reads=['vecs'], writes=[pskey, akey])
                                em.op('dve', lambda e, acc=acc, ps=ps: e.scalar_tensor_tensor(
                                    acc[:, 0:511], ps[:, 1:512], w2, acc[:, 0:511], ALU.mult, ALU.add),
                                    reads=['vecs'], writes=[pskey, akey])
                        if smp:
                            a0, a1 = ACC[part * 2], ACC[part * 2 + 1]
                            k0, k1 = ACCK[part * 2], ACCK[part * 2 + 1]
                            em.op('dve', lambda e: e.scalar_tensor_tensor(
                                a0[:, 511:512], PS[pb + 1][:, 0:1], w2, a0[:, 511:512], ALU.mult, ALU.add),
                                reads=['vecs'], writes=['ps%d' % (pb + 1), k0])
                            em.op('dve', lambda e: e.scalar_tensor_tensor(
                                a1[:, 0:1], PS[pb][:, 511:512], w0, a1[:, 0:1], ALU.mult, ALU.add),
                                reads=['vecs'], writes=['ps%d' % pb, k1])
                            if has_left:
                                em.op('dve', lambda e: e.scalar_tensor_tensor(
                                    a0[:, 0:1], PS[hb][:, 0:1], w0, a0[:, 0:1], ALU.mult, ALU.add),
                                    reads=['vecs'], writes=['ps%d' % hb, k0])
                            if has_right:
                                em.op('dve', lambda e: e.scalar_tensor_tensor(
                                    a1[:, 511:512], PS[hb][:, 1:2], w2, a1[:, 511:512], ALU.mult, ALU.add),
                                    reads=['vecs'], writes=['ps%d' % hb, k1])
                    def tail(f=f, ACC=ACC, ACCK=ACCK):
                        for th in range(2):
                            ts = slice(th * 512, th * 512 + 512)
                            SIL, SILK = (tf[4], 'tf4') if th == 0 else (tf[5], 'tf5')
                            em.op('act', lambda e: e.activation(SIL[:], ACC[2 + th][:], AF.Silu), reads=[ACCK[2 + th]],
                                  writes=[SILK])
                            em.op('dve', lambda e: e.tensor_tensor(actT[:, f, ts], ACC[th][:], SIL[:], ALU.mult),
                                  reads=[ACCK[th], SILK], writes=[('actT', th)])
                    pending_tail.append(tail)
            while pending_tail:
                pending_tail.pop(0)()
            em.barrier(soft=True)
            for j in range(8):
                wdn, wdk = wdn2[j % 2], 'wdn%d' % (j % 2)
                em.dma('pool', wdn[:], w_dn[l][:, j * 128:(j + 1) * 128].rearrange("(kc p) c -> p kc c", p=128), writes=[wdk])
                for th in range(2):
                    ts = slice(th * 512, th * 512 + 512)
                    ps, pskey = next_ps()
                    em.op('pe', [lambda e, f=f, ps=ps: e.matmul(ps[:], wdn[:, f, :], actT[:, f, ts],
                                                               start=(f == 0), stop=(f == NFF - 1)) for f in range(NFF)],
                          reads=[wdk, ('actT', th)], writes=[pskey])
                    xo_, xok = (tf[0], 'tf0') if (2 * j + th) % 2 == 0 else (tf[1], 'tf1')
                    em.op('dve', lambda e, ps=ps, j=j: e.scalar_tensor_tensor(
                        xo_[:], ps[:], modT[:, ci, 40 + j:41 + j], xT[:, j, ts], ALU.mult, ALU.add),
                        reads=['modT', 'xT'], writes=[pskey, xok])
                    em.dma('sp', xout[:, j, ts], xo_[:], reads=[xok], writes=[xoutkey])

        try:
          for l in range(n_layers):
            VB = 64 + l * PER_L
            V_BMOD, V_GATT, V_GFFN, V_NRM, V_CONV = VB, VB + 48, VB + 56, VB + 64, VB + 71
            last = (l == n_layers - 1)
            mod_phase(l, V_BMOD, V_GATT, V_GFFN)
            phase(1)
            xp_src = x_in if l == 0 else xp_scr
            em.barrier()
            em.dma('sp', xT, xp_src, reads=['xp'], writes=['xT'])
            rmsnorm_mod(0, 0)
            em.barrier()
            phase(2)
            inproj(l, None, V_NRM)
            phase(4)
            attention_prompt()
            phase(5)
            gla_prompt(l, V_NRM)
            phase(6)
            merge_outproj(l, 0, xp_src, xp_scr, 'xp')
            phase(8)
            em.barrier()
            em.dma('sp', xT, xp_scr, reads=['xp'], writes=['xT'])
            rmsnorm_mod(1, 0)
            ffn(l, 0, V_CONV, False, False, False, y_out if last else xp_scr, 'xp')
            em.barrier()
            phase(9)
            if not do_sample:
                continue
            xs_src = xs_in if l == 0 else xs_scr2
            K_A = [(n_, t_) for n_ in ('kcT', 'rcT') for t_ in range(2)] + \
                  [(n_, i_) for n_ in ('vtok', 'ktok', 'latok') for i_ in range(8)]
            K_C = [(n_, t_) for n_ in ('qaT', 'qbT', 'qcT') for t_ in range(2)]
            K_H = [('hT', 0), ('hT', 1)]
            NSAV = 20480
            em.dma('sp', S32[0][:], sgf_in[l], writes=[('S32', 0)])
            for blk in range(NBLK):
                bs = slice(blk * NT, (blk + 1) * NT)
                if blk == 0:
                    em.barrier()
                em.dma('sp', xT, xs_src[:, :, bs], reads=['xs', 'xs2'], writes=['xT'])
                rmsnorm_mod(0, 1)
                inproj(l, blk, V_NRM)
                for i in range(8):
                    gla_chain(i, 0, 0, save_to=sprev_scr[blk * 8 + i])
                em.dma('sp', sav_arena[blk][:, 0:NSAV], arena[:, 0:NSAV], reads=K_A + K_C, writes=[('sav', blk)])
                em.dma('sp', sav_hT[blk], hT[:].rearrange("p c t -> p (c t)"), reads=K_H, writes=[('savh', blk)])
            phase(10)
            em.barrier()
            em.dma('sp', S32[1][:], sgb_in[l], writes=[('S32', 1)])

            def restore_a(blk):
                em.dma('sp', arena[:, 8192:NSAV], sav_arena[blk][:, 8192:NSAV], reads=[('sav', blk)], writes=K_A)

            def restore_h(blk):
                em.dma('sp', hT[:].rearrange("p c t -> p (c t)"), sav_hT[blk], reads=[('savh', blk)], writes=K_H)

            def restore_c(blk):
                em.dma('sp', arena[:, 0:8192], sav_arena[blk][:, 0:8192], reads=[('sav', blk)],
                       writes=K_C + [('mT', 0), ('mT', 1), ('actT', 0), ('actT', 1)])

            restore_a(NBLK - 1)
            restore_h(NBLK - 1)
            for blk in reversed(range(NBLK)):
                bs = slice(blk * NT, (blk + 1) * NT)
                restore_c(blk)
                attention_sample(l, blk)
                def gla_pre(i):
                    gla_decay(i, i // 4, 0, i % 2)
                    gla_decay(i, i // 4, 1, i % 2)
                    em.dma('sp', Sprev[0][i % 2][:].rearrange("p a b -> p (a b)"), sprev_scr[blk * 8 + i],
                           reads=['sprev_scr'], writes=[('Sprev', 0, i % 2)])

                gla_pre(7)
                for i in reversed(range(8)):
                    gla_chain(i, 1, i % 2)
                    if i > 0:
                        gla_pre(i - 1)
                    gla_out(i, i // 4, i % 2, [True, True], V_NRM)
                nxt = blk - 1
                if nxt >= 0:
                    restore_a(nxt)
                merge_outproj(l, 1, xs_src[:, :, bs], xs_scr[:, :, bs], 'xs',
                              hook_mid=(lambda nxt=nxt: restore_h(nxt)) if nxt >= 0 else None)
            phase(11)
            for blk in range(NBLK):
                bs = slice(blk * NT, (blk + 1) * NT)
                has_left, has_right = blk > 0, blk < NBLK - 1
                if blk == 0:
                    em.barrier()
                em.dma('sp', xT, xs_scr[:, :, bs], reads=['xs'], writes=['xT'])
                if has_left:
                    em.dma('sp', xh[:, :, 0:1], xs_scr[:, :, blk * NT - 1:blk * NT], reads=['xs'], writes=['xh'])
                else:
                    em.op('pool', lambda e: e.memset(xh[:, :, 0:1], 1.0), writes=['xh'])
                if has_right:
                    em.dma('sp', xh[:, :, 1:2], xs_scr[:, :, (blk + 1) * NT:(blk + 1) * NT + 1], reads=['xs'], writes=['xh'])
                else:
                    em.op('pool', lambda e: e.memset(xh[:, :, 1:2], 1.0), writes=['xh'])
                rmsnorm_mod(1, 1)
                rms_cols(1, 1, lambda kc: xh[:, kc, :], lambda kc: hTh[:, kc, :], 2, 'hTh', 'xh')
                ffn(l, 1, V_CONV, True, has_left, has_right, (ys_out if last else xs_scr2)[:, :, bs], 'xs2')
            em.barrier()
        except _Stop:
            pass

        with nc.allow_low_precision("bf16 matmul operands, fp32 accumulation"):
            with nc.allow_non_contiguous_dma("halo columns / small strided loads"):
                em.run()
    return nc


_PROGRAM = {}


def _get_program():
    if 'p' not in _PROGRAM:
        _PROGRAM['p'] = build_program(L, True)
    return _PROGRAM['p']


def _na_bias_tables(rpb):
    NEG = np.float32(-1e30)
    cq = np.arange(64)
    win0 = np.clip(cq - 8, 0, 48)
    ck = np.arange(64)
    in_win = (ck[:, None] >= win0[None, :]) & (ck[:, None] < win0[None, :] + 16)
    dc = np.clip(ck[:, None] - cq[None, :] + 15, 0, 30)
    out = np.full((5, 128, 5, 4, 128), NEG, np.float32)
    for vi, r in enumerate([30, 0, 2, 60, 62]):
        t0 = min(max((r - 4) // 2, 0), 27)
        for j in range(5):
            for a in range(2):
                kr = 2 * (t0 + j) + a
                for bq in range(2):
                    rq = r + bq
                    k0 = min(max(rq - 4, 0), 56)
                    if not (0 <= kr - k0 < 8):
                        continue
                    dr = kr - rq + 7
                    for h in range(4):
                        blk = np.where(in_win, rpb[h, dr][dc], NEG)
                        out[vi, a * 64:(a + 1) * 64, j, h, bq * 64:(bq + 1) * 64] = blk
    return out.reshape(5, 128, 5 * 512)


def _host_layout(inp):
    f = lambda a: np.ascontiguousarray(np.asarray(a, dtype=np.float32))
    w_in = f(inp['w_in'])
    offs = np.cumsum([0, 256, 256, 256, 512, 128, 128, 256, 256, 256, 256, 16, 16, 1024, 1024, 1024])
    (o_qa, o_ka, o_va, o_qb, o_kb, o_vb, o_qc, o_kc, o_vc, o_rc, o_zf, o_zb, o_ga, o_gb, o_gc, _) = offs
    rot = (np.arange(64) + 32) % 64
    qb_cols = np.concatenate([o_qb + h * 64 + np.arange(64) for h in QB_PERM])
    qbr_cols = np.concatenate([o_qb + h * 64 + rot for h in QB_PERM])
    kbr_cols = np.concatenate([o_kb + h * 64 + rot for h in range(2)])
    rng = lambda a, n: np.arange(a, a + n)
    fm_cols = np.concatenate([rng(o_qa, 256), rng(o_ka, 256), qb_cols, qbr_cols, rng(o_kb, 128), kbr_cols,
                              rng(o_qc, 256), rng(o_kc, 256), rng(o_rc, 256), rng(o_zf, 32)])
    assert len(fm_cols) == W_FM
    tm_cols = np.concatenate([rng(o_va, 256), rng(o_vb, 128), rng(o_vc, 256), rng(o_kc, 256)])
    shared = {
        'w_mod': f(inp['w_mod']),
        'w_fm': np.ascontiguousarray(w_in[:, :, fm_cols]),
        'w_tm': np.ascontiguousarray(w_in[:, :, tm_cols]),
        'w_g': np.ascontiguousarray(w_in[:, :, o_ga:o_ga + 3072]),
        'w_ba': f(inp['w_branch_a']),
        'w_bb': np.ascontiguousarray(f(inp['w_branch_b']).reshape(L, 8, 64, D)[:, QB_PERM].reshape(L, 512, D)),
        'w_bc': f(inp['w_branch_c']),
        'w_out': f(inp['w_out']),
        'w_up': f(inp['ffn_w_up']),
        'w_dn': f(inp['ffn_w_down']),
    }
    wg2 = f(inp['gla_wg2'])
    bg = f(inp['gla_bg'])
    wz2 = np.zeros((L, 33, 512), np.float32)
    wz2[:, 0:16, 0:256] = wg2[:, 0]
    wz2[:, 16:32, 256:512] = wg2[:, 1]
    wz2[:, 32, 0:256] = bg[:, 0]
    wz2[:, 32, 256:512] = bg[:, 1]
    shared['w_z2'] = wz2
    vecs = np.zeros((128, NV), np.float32)
    col128 = lambda v: v.reshape(-1, 128).T
    rep64 = lambda v: np.concatenate([v, v])
    for l in range(L):
        b = 64 + l * PER_L
        vecs[:, b:b + 48] = col128(f(inp['b_mod'])[l])
        vecs[:, b + 48:b + 56] = col128(f(inp['g_attn'])[l])
        vecs[:, b + 56:b + 64] = col128(f(inp['g_ffn'])[l])
        vecs[:, b + 64] = rep64(f(inp['na_q_norm'])[l])
        vecs[:, b + 65] = rep64(f(inp['na_k_norm'])[l])
        vecs[:, b + 66] = rep64(f(inp['gqa_q_norm'])[l])
        vecs[:, b + 67] = rep64(f(inp['gqa_k_norm'])[l])
        vecs[:, b + 68] = rep64(f(inp['gla_out_norm'])[l])
        vecs[:, b + 69] = rep64(f(inp['gqa_q_norm'])[l][rot])
        vecs[:, b + 70] = rep64(f(inp['gqa_k_norm'])[l][rot])
        cw = f(inp['ffn_conv_w'])[l]
        cbias = f(inp['ffn_conv_b'])[l]
        for part in range(2):
            cb0 = b + 71 + part * 4 * NFF
            sl = slice(part * DFF, (part + 1) * DFF)
            for k in range(3):
                vecs[:, cb0 + k * NFF:cb0 + (k + 1) * NFF] = col128(cw[k, sl])
            vecs[:, cb0 + 3 * NFF:cb0 + 4 * NFF] = col128(cbias[sl])
    shared['vecs'] = vecs
    s_idx = np.arange(128)[:, None]
    t_idx = np.arange(128)[None, :]
    blk = ((s_idx // 64) == (t_idx // 64))
    mf = (s_idx <= t_idx)
    mb = (s_idx >= t_idx)
    consts = np.concatenate([mf, mb, (s_idx > t_idx), (s_idx < t_idx), blk, np.ones((128, 128), bool), (s_idx == t_idx),
                             mf, mf, mf, mf, mb, mb, mb, mb], axis=1).astype(np.float32)
    assert consts.shape[1] == NCONST
    shared['consts'] = consts
    t = np.arange(NTS)
    n_freq = 16
    inv_freq = 10000.0 ** (-np.arange(n_freq) / n_freq)
    ang = np.concatenate([(t // 64)[:, None] * inv_freq, (t % 64)[:, None] * inv_freq], axis=-1)
    cosT = np.cos(ang).astype(np.float32).T
    sinT = np.sin(ang).astype(np.float32).T
    c64 = np.concatenate([cosT, cosT], 0)
    s64 = np.concatenate([-sinT, sinT], 0)
    shared['ropeC'] = np.ascontiguousarray(np.concatenate([c64, c64], 0))
    shared['ropeS'] = np.ascontiguousarray(np.concatenate([s64, s64], 0))
    rpb = f(inp['na_rpb'])
    shared['nab'] = np.stack([_na_bias_tables(rpb[l]) for l in range(L)], 0)
    return shared


def kernel(**inputs):
    shared = _host_layout(inputs)
    f = lambda a: np.asarray(a, dtype=np.float32)
    xp = f(inputs['x_prompt'])
    xsm = f(inputs['x_sample'])
    c_ctx = f(inputs['c_ctx'])
    cc = f(inputs['c'])
    per_b = []
    for b in range(2):
        d = {}
        d['xsT0'] = np.ascontiguousarray(xsm[b].T.reshape(8, 128, NTS).transpose(1, 0, 2))
        cond = np.stack([c_ctx.reshape(8, 128).T, cc[b].reshape(8, 128).T], axis=-1)
        d['cond'] = np.ascontiguousarray(cond.reshape(128, 16))
        nk = f(inputs['cache_na_k'])[b]
        d['cnak'] = np.ascontiguousarray(nk.reshape(L, 512, 2, 128).transpose(0, 3, 2, 1))
        d['cnav'] = np.ascontiguousarray(f(inputs['cache_na_v'])[b].reshape(L, 512, 256))
        d['cgk'] = np.ascontiguousarray(f(inputs['cache_gqa_k'])[b].reshape(L, 512, 128).transpose(0, 2, 1))
        d['cgv'] = np.ascontiguousarray(f(inputs['cache_gqa_v'])[b].reshape(L, 512, 128))
        for nm, key in (('sgf', 'state_gla_fwd'), ('sgb', 'state_gla_bwd')):
            st = f(inputs[key])[b]
            d[nm] = np.ascontiguousarray(st.reshape(L, 2, 2, 64, 64).transpose(0, 2, 3, 1, 4).reshape(L, 128, 2, 64))
        per_b.append(d)
    in_maps = []
    for c in range(NCORES):
        xs = xp[4 * c:4 * c + 4].reshape(NT, D)
        m = dict(shared)
        m.update(per_b[c // 4])
        m['xT0'] = np.ascontiguousarray(xs.T.reshape(8, 128, NT).transpose(1, 0, 2))
        in_maps.append(m)
    nc = _get_program()
    res = run_bass_kernel_spmd(nc, in_maps, core_ids=list(range(NCORES)))
    R = res.results
    B, S = 32, SEQ
    y_prompt = np.zeros((B, S, D), np.float32)
    y_sample = np.zeros((2, NTS, D), np.float32)
    na_k = np.zeros((B, L, S, 4, 64), np.float32)
    na_v = np.zeros((B, L, S, 4, 64), np.float32)
    gq_k = np.zeros((B, L, S, 2, 64), np.float32)
    gq_v = np.zeros((B, L, S, 2, 64), np.float32)
    s_f = np.zeros((B, L, 4, 64, 64), np.float32)
    s_b = np.zeros((B, L, 4, 64, 64), np.float32)
    for c in range(NCORES):
        r = R[c]
        yT = np.asarray(r['yT'])
        y_prompt[4 * c:4 * c + 4] = yT.transpose(2, 1, 0).reshape(4, S, D)
        if c % 4 == 0:
            y_sample[c // 4] = np.asarray(r['ysT']).transpose(2, 1, 0).reshape(NTS, D)
        nak = np.asarray(r['o_nak'])
        na_k[4 * c:4 * c + 4] = nak.reshape(L, 4, 64, 4, S).transpose(3, 0, 4, 1, 2)
        nav = np.asarray(r['o_nav'])
        na_v[4 * c:4 * c + 4] = nav.reshape(L, 4, S, 4, 64).transpose(1, 0, 2, 3, 4)
        gk = np.asarray(r['o_gk'])
        gq_k[4 * c:4 * c + 4] = gk.reshape(L, 2, 64, 4, S).transpose(3, 0, 4, 1, 2)
        gv = np.asarray(r['o_gv'])
        gq_v[4 * c:4 * c + 4] = gv.reshape(L, 4, S, 2, 64).transpose(1, 0, 2, 3, 4)
        sf = np.asarray(r['o_sf'])
        s_f[4 * c:4 * c + 4] = sf.reshape(L, 4, 4, 64, 64).transpose(1, 0, 2, 3, 4)
        sb = np.asarray(r['o_sb'])
        s_b[4 * c:4 * c + 4] = sb.reshape(L, 4, 4, 64, 64).transpose(1, 0, 2, 3, 4)
    return (y_prompt, y_sample, na_k, na_v, gq_k, gq_v, s_f, s_b)
```

```python
from contextlib import ExitStack
import numpy as np
import concourse.bass as bass
import concourse.mybir as mybir
from concourse.bass_utils import run_bass_kernel_spmd

F32 = mybir.dt.float32
BF16 = mybir.dt.bfloat16
AF = mybir.ActivationFunctionType
ALU = mybir.AluOpType

NDMA = 24
L = 4
D = 1024
NT = 1024
SEQ = 256
DFF = 2816
NFF = 22
EPS = 1e-6
NCORES = 8


class _Rec:
    def __init__(self):
        self.calls = []

    def __getattr__(self, name):
        def f(*a, **k):
            self.calls.append((name, a, k))
        return f


class Emit:
    def __init__(self, nc, es):
        self.nc = nc
        self.engs = {'pe': nc.tensor, 'act': nc.scalar, 'dve': nc.vector, 'pool': nc.gpsimd, 'sp': nc.sync}
        self.thunks = {e: [] for e in self.engs}
        self.seq = {e: 0 for e in self.engs}
        self.sem = {e: es.enter_context(nc.semaphore("s_" + e)) for e in self.engs}
        self.dsem = [es.enter_context(nc.semaphore("d%d" % i)) for i in range(NDMA)]
        self.dcnt = [0] * NDMA
        self.dnext2 = [0, 0]
        self.waited = {}
        self.last_w = {}
        self.readers = {}

    def _deps(self, reads, writes):
        deps = {}

        def add(p):
            if p is None:
                return
            prod, val = p
            if deps.get(prod, 0) < val:
                deps[prod] = val

        for k in reads:
            add(self.last_w.get(k))
        for k in writes:
            add(self.last_w.get(k))
            for r in self.readers.get(k, ()):
                add(r)
        return deps

    def _emit_waits(self, e, deps):
        for prod, val in deps.items():
            if prod == e and e == 'pe':
                continue
            if self.waited.get((e, prod), 0) >= val:
                continue
            self.waited[(e, prod)] = val
            sem = self.sem[prod] if isinstance(prod, str) else self.dsem[prod]
            self.thunks[e].append(lambda eng, sem=sem, val=val: eng.wait_ge(sem, val))

    def _record(self, me, reads, writes):
        for k in writes:
            self.last_w[k] = me
            self.readers[k] = []
        for k in reads:
            self.readers.setdefault(k, []).append(me)

    def op(self, e, fns, reads=(), writes=()):
        if not isinstance(fns, (list, tuple)):
            fns = [fns]
        deps = self._deps(reads, writes)
        self._emit_waits(e, deps)
        self.seq[e] += 1
        val = self.seq[e]
        sem = self.sem[e]
        n = len(fns)
        for i, fn in enumerate(fns):
            rec = _Rec()
            fn(rec)
            (name, a, k), = rec.calls
            if i == n - 1:
                self.thunks[e].append(lambda eng, name=name, a=a, k=k, sem=sem: getattr(eng, name)(*a, **k).then_inc(sem, 1))
            else:
                self.thunks[e].append(lambda eng, name=name, a=a, k=k: getattr(eng, name)(*a, **k))
        self._record((e, val), reads, writes)

    def dma(self, q, out, in_, reads=(), writes=(), **kw):
        half = NDMA // 2
        qi = 0 if q == 'sp' else 1
        d = qi * half + self.dnext2[qi]
        self.dnext2[qi] = (self.dnext2[qi] + 1) % half
        deps = self._deps(reads, writes)
        if self.dcnt[d] > 0 and deps.get(d, 0) < self.dcnt[d]:
            deps[d] = self.dcnt[d]
        self._emit_waits(q, deps)
        self.dcnt[d] += 16
        val = self.dcnt[d]
        sem = self.dsem[d]
        self.thunks[q].append(
            lambda eng, out=out, in_=in_, sem=sem, kw=kw: eng.dma_start(out=out, in_=in_, **kw).then_inc(sem, 16))
        self._record((d, val), reads, writes)

    def barrier(self, soft=False):
        if soft:
            for e in ('pe', 'act', 'dve'):
                deps = {}
                for p in ('pe', 'act', 'dve', 'pool'):
                    if self.seq[p] > 0 and not (p == e and e == 'pe'):
                        deps[p] = self.seq[p]
                self._emit_waits(e, deps)
            return
        for e in self.engs:
            deps = {}
            for p in self.engs:
                if p != e and self.seq[p] > 0:
                    deps[p] = self.seq[p]
            if e != 'pe' and self.seq[e] > 0:
                deps[e] = self.seq[e]
            for d in range(NDMA):
                if self.dcnt[d] > 0:
                    deps[d] = self.dcnt[d]
            self._emit_waits(e, deps)

    def run(self):
        for d in range(NDMA):
            if self.dcnt[d] > 0:
                self.thunks['sp'].append(lambda eng, sem=self.dsem[d], val=self.dcnt[d]: eng.wait_ge(sem, val))
        with self.nc.Block() as block:
            @block.tensor
            def _(eng):
                for t in self.thunks['pe']:
                    t(eng)

            @block.scalar
            def _(eng):
                for t in self.thunks['act']:
                    t(eng)

            @block.vector
            def _(eng):
                for t in self.thunks['dve']:
                    t(eng)

            @block.gpsimd
            def _(eng):
                for t in self.thunks['pool']:
                    t(eng)

            @block.sync
            def _(eng):
                for t in self.thunks['sp']:
                    t(eng)


C_QA, C_KA, C_QB, C_QBR, C_KB, C_KBR, C_QC, C_KC, C_RC, C_Z = 0, 256, 512, 1024, 1536, 1664, 1792, 2048, 2304, 2560
W_FM = 2592
T_VA, T_VB, T_VC, T_KC = 0, 256, 384, 640
W_TM = 896
QB_PERM = [0, 4, 1, 5, 2, 6, 3, 7]
NTS = 4096
NBLK = 4
PER_L = 48 + 8 + 8 + 7 + 4 * 2 * NFF
NV = 64 + L * PER_L
NCONST = 7 * 128 + 2 * 512
ARENA = 28672


class _Stop(Exception):
    pass


def build_program(n_layers=L, do_sample=True, stop=99):
    nc = bass.Bass("TRN2", target_bir_lowering=False)
    dt_in = lambda name, shape: nc.dram_tensor(name, shape, F32, kind="ExternalInput").ap()
    dt_out = lambda name, shape: nc.dram_tensor(name, shape, F32, kind="ExternalOutput").ap()
    x_in = dt_in("xT0", [128, 8, NT])
    xs_in = dt_in("xsT0", [128, 8, NTS])
    cond_in = dt_in("cond", [128, 16])
    w_mod = dt_in("w_mod", [L, D, 6 * D])
    w_fm = dt_in("w_fm", [L, D, W_FM])
    w_tm = dt_in("w_tm", [L, D, W_TM])
    w_g = dt_in("w_g", [L, 8, 128, 8, 384])
    w_br = dt_in("w_br", [L, 8, 128, 8, 128])
    w_out = dt_in("w_out", [L, D, D])
    w_up = dt_in("w_up", [L, 2, 11, 128, 8, 256])
    w_dn = dt_in("w_dn", [L, 8, 128, NFF, 128])
    w_z2 = dt_in("w_z2", [L, 33, 512])
    vecs_in = dt_in("vecs", [128, NV])
    consts_in = dt_in("consts", [128, NCONST])
    ropeC_in = dt_in("ropeC", [128, NTS])
    ropeS_in = dt_in("ropeS", [128, NTS])
    nab_in = dt_in("nab", [L, 5, 128, 5 * 512])
    cnak_in = dt_in("cnak", [L, 128, 2, 512])
    cnav_in = dt_in("cnav", [L, 512, 256])
    cgk_in = dt_in("cgk", [L, 128, 512])
    cgv_in = dt_in("cgv", [L, 512, 128])
    sgf_in = dt_in("sgf", [L, 128, 2, 64])
    sgb_in = dt_in("sgb", [L, 128, 2, 64])

    y_out = dt_out("yT", [128, 8, NT])
    ys_out = dt_out("ysT", [128, 8, NTS])
    o_nak = dt_out("o_nak", [L, 256, NT])
    o_nav = dt_out("o_nav", [L, NT, 256])
    o_gk = dt_out("o_gk", [L, 128, NT])
    o_gv = dt_out("o_gv", [L, NT, 128])
    o_sf = dt_out("o_sf", [L, 4, 2, 128, 64])
    o_sb = dt_out("o_sb", [L, 4, 2, 128, 64])

    scr = lambda name, shape, dt: nc.dram_tensor(name, shape, dt).ap()
    xp_scr = scr("xp_scr", [128, 8, NT], F32)
    xs_scr = scr("xs_scr", [128, 8, NTS], F32)
    xs_scr2 = scr("xs_scr2", [128, 8, NTS], F32)
    sav_arena = scr("sav_arena", [NBLK, 128, ARENA], BF16)
    sav_hT = scr("sav_hT", [NBLK, 128, 8 * NT], BF16)
    kaT_scr = scr("kaT_scr", [128, 2, NTS], BF16)
    va_scr = scr("va_scr", [NTS, 256], BF16)
    sprev_scr = scr("sprev_scr", [32, 128, 128], BF16)

    with ExitStack() as es:
        T = lambda name, shape, dt: es.enter_context(nc.sbuf_tensor("sb_" + name, shape, dt))
        xreg = T("xreg", [128, 16384], BF16)
        xT = xreg[:].bitcast(F32).rearrange("p (c t) -> p c t", c=8)
        Kslab = xreg[:, 0:4096].rearrange("p (c t) -> p c t", c=2)
        Vslab = xreg[:, 4096:8192].rearrange("p (i c) -> p i c", i=16)
        nabI = xreg[:, 8192:10752]
        nabE = xreg[:, 10752:13312]
        hT = T("hT", [128, 8, NT], BF16)
        hTh = T("hTh", [128, 8, 2], BF16)
        xh = T("xh", [128, 8, 2], F32)
        wbuf = [T("wbuf%d" % i, [128, 8, 512], BF16) for i in range(2)]
        wdn2 = [T("wdn%d" % i, [128, NFF, 128], BF16) for i in range(2)]
        brw2 = [T("brw%d" % i, [128, 8, 128], BF16) for i in range(2)]
        arena = T("arena", [128, ARENA], BF16)
        kbT_all = T("kbT_all", [128, NTS + 512], BF16)
        vaug = T("vaug", [128, 36, 2, 128], BF16)
        vecs = T("vecs", [128, NV], F32)
        cb = T("cb", [128, NCONST], BF16)
        modT = T("modT", [128, 2, 48], F32)
        scA = T("scA", [128, 2, 16], F32)
        condf = T("condf", [128, 16], F32)
        condb = T("condb", [128, 16], BF16)
        zT = T("zT", [33, NT], BF16)
        wz2 = T("wz2", [33, 512], BF16)
        tf = [T("tf%d" % i, [128, 512], F32) for i in range(6)]
        tb = [T("tb%d" % i, [128, 512], BF16) for i in range(4)]
        trC = [T("trC%d" % i, [128, 512], F32) for i in range(2)]
        trS = [T("trS%d" % i, [128, 512], F32) for i in range(2)]
        S32 = [T("S32_%d" % i, [128, 2, 64], F32) for i in range(2)]
        qtil = [[T("qtil%d%d" % (d, c), [128, 256], BF16) for c in range(2)] for d in range(2)]
        attm = [[T("attm%d%d" % (d, c), [128, 512], BF16) for c in range(2)] for d in range(2)]
        Sprev = [[T("Sprev%d%d" % (d, c), [128, 2, 64], BF16) for c in range(2)] for d in range(2)]
        dec = [T("dec%d" % d, [128, 2], F32) for d in range(2)]
        onec = T("onec", [128, 1], F32)
        gsc = T("gsc", [128, 4], F32)
        qpad = [T("qpad%d" % i, [128, 512], BF16) for i in range(2)]
        epsc = T("epsc", [128, 1], F32)
        PS = [es.enter_context(nc.psum_tensor("ps%d" % i, [128, 512], F32)) for i in range(8)]

        off = [0]

        def carve(n, shape_str=None, **kw):
            a = arena[:, off[0]:off[0] + n]
            off[0] += n
            return a.rearrange(shape_str, **kw) if shape_str else a

        qaT = carve(2 * NT, "p (c t) -> p c t", c=2)
        qbT = carve(4 * NT, "p (c t) -> p c t", c=4)
        qcT = carve(2 * NT, "p (c t) -> p c t", c=2)
        mT = arena[:, 0:8 * NT].rearrange("p (c t) -> p c t", c=8)
        kcT = carve(2 * NT, "p (c t) -> p c t", c=2)
        rcT = carve(2 * NT, "p (c t) -> p c t", c=2)
        vtok = carve(8 * 256, "p (i c) -> p i c", i=8)
        ktok = carve(8 * 256, "p (i c) -> p i c", i=8)
        latok = carve(8 * 512, "p (i c) -> p i c", i=8)
        oaT = carve(2 * NT, "p (c t) -> p c t", c=2)
        obT = carve(4 * NT, "p (c t) -> p c t", c=4)
        ocT = carve(2 * NT, "p (c t) -> p c t", c=2)
        assert off[0] <= ARENA, off[0]
        actT = arena[:, 0:NFF * NT].rearrange("p (c t) -> p c t", c=NFF)

        em = Emit(nc, es)
        ps_rr = [0]

        def phase(k):
            if stop == k:
                raise _Stop()

        em.dma('sp', vecs[:], vecs_in, writes=['vecs'])
        em.dma('pool', cb[:], consts_in, writes=['cb'])
        em.op('pool', lambda e: e.memset(onec[:], 1.0), writes=['onec'])
        em.op('pool', lambda e: e.memset(epsc[:], EPS), writes=['epsc'])
        em.op('pool', lambda e: e.memset(zT[:], 1.0), writes=['zT'])
        em.op('pool', lambda e: e.memset(vaug[:], 1.0), writes=['vb_all'])
        for i_ in range(2):
            em.op('pool', lambda e: e.memset(qpad[i_][:], 0.0), writes=['qpad%d' % i_])
        TRI_F, TRI_B, TRIX_F, TRIX_B, BLK, ONES, IDENT = [cb[:, i * 128:(i + 1) * 128] for i in range(7)]
        MASK4 = [cb[:, 896:1408], cb[:, 1408:1920]]
        em.dma('sp', condf[:], cond_in, writes=['condf'])
        em.op('act', lambda e: e.activation(condb[:], condf[:], AF.Silu), reads=['condf'], writes=['condb'])
        condb3 = condb[:].rearrange("p (k c) -> p k c", c=2)

        def vcol(base, j):
            return vecs[:, base + j:base + j + 1]

        def load_w(buf, key, src, ncols, nk=8):
            em.dma('pool', buf[:, 0:nk, 0:ncols], src.rearrange("(kc p) c -> p kc c", p=128), writes=[key])

        wpar = [0]

        def next_wbuf():
            i = wpar[0]
            wpar[0] ^= 1
            return wbuf[i], 'wbuf%d' % i

        def next_ps():
            i = (0, 1, 4, 5)[ps_rr[0]]
            ps_rr[0] = (ps_rr[0] + 1) % 4
            return PS[i], 'ps%d' % i

        def rms_cols(which, ci, xsrc, hdst, w, hkey, xkey):
            sh_base = (0 if which == 0 else 24)
            for kc in range(8):
                em.op('act', lambda e, kc=kc: e.activation(tb[0][:, 0:w], xsrc(kc), AF.Square),
                      reads=[xkey], writes=['tb0'])
                em.op('pe', lambda e, kc=kc: e.matmul(PS[2][:, 0:w], ONES, tb[0][:, 0:w], start=(kc == 0), stop=(kc == 7)),
                      reads=['tb0', 'cb'], writes=['ps2'])
            em.op('act', lambda e: e.activation(tf[0][:, 0:w], PS[2][:, 0:w], AF.Sqrt, bias=epsc[:, 0:1], scale=1.0 / D),
                  reads=['epsc'], writes=['ps2', 'tf0'])
            em.op('dve', lambda e: e.reciprocal(tf[1][:, 0:w], tf[0][:, 0:w]), reads=['tf0'], writes=['tf1'])
            for kc in range(8):
                em.op('dve', lambda e, kc=kc: e.tensor_tensor(tf[2][:, 0:w], xsrc(kc), tf[1][:, 0:w], ALU.mult),
                      reads=[xkey, 'tf1'], writes=['tf2'])
                em.op('act', lambda e, kc=kc: e.activation(hdst(kc), tf[2][:, 0:w], AF.Identity,
                                                           bias=modT[:, ci, sh_base + kc:sh_base + kc + 1],
                                                           scale=scA[:, ci, which * 8 + kc:which * 8 + kc + 1]),
                      reads=['tf2', 'modT', 'scA'], writes=[hkey])

        def rmsnorm_mod(which, ci):
            for th in range(2):
                ts = slice(th * 512, th * 512 + 512)
                rms_cols(which, ci, lambda kc, ts=ts: xT[:, kc, ts], lambda kc, ts=ts: hT[:, kc, ts], 512, ('hT', th), 'xT')

        def proj_fm(wb, wkey, c0, n, th, ps, pskey, rhsT=None, rkey='hT', nk=8):
            r = hT if rhsT is None else rhsT
            ts = slice(th * 512, th * 512 + 512)
            em.op('pe', [lambda e, kc=kc: e.matmul(ps[0:n, :], wb[:, kc, c0:c0 + n], r[:, kc, ts],
                                                  start=(kc == 0), stop=(kc == nk - 1)) for kc in range(nk)],
                  reads=[wkey, (rkey, th)], writes=[pskey])

        hn_par = [0]

        def head_norm_core(ps, pskey, gcol, dst=None, dkey='tf4'):
            dst = tf[4][:] if dst is None else dst
            sq, sqk, t0, t0k, t1, t1k, pb = tb[1], 'tb1', tf[0], 'tf0', tf[1], 'tf1', 2
            pbk = 'ps%d' % pb
            em.op('act', lambda e: e.activation(sq[:], ps[:], AF.Square), writes=[pskey, sqk])
            em.op('pe', lambda e: e.matmul(PS[pb][:], BLK, sq[:], start=True, stop=True),
                  reads=[sqk, 'cb'], writes=[pbk])
            em.op('act', lambda e: e.activation(t0[:], PS[pb][:], AF.Ln, bias=epsc[:, 0:1], scale=1.0 / 64),
                  reads=['epsc'], writes=[pbk, t0k])
            em.op('act', lambda e: e.activation(t1[:], t0[:], AF.Exp, scale=-0.5), reads=[t0k], writes=[t1k])
            em.op('dve', lambda e: e.scalar_tensor_tensor(dst, ps[:], gcol, t1[:], ALU.mult, ALU.mult),
                  reads=[t1k, 'vecs', 'gsc'], writes=[pskey, dkey])

        def mod_phase(l, V_BMOD, V_GATT, V_GFFN):
            for g in range(12):
                wb, wkey = next_wbuf()
                load_w(wb, wkey, w_mod[l][:, g * 512:(g + 1) * 512], 512)
                for j in range(4):
                    col = g * 4 + j
                    em.op('pe', [lambda e, kc=kc, j=j, col=col, wb=wb: e.matmul(
                        PS[3][:, 2 * col:2 * col + 2], wb[:, kc, j * 128:(j + 1) * 128], condb3[:, kc, :],
                        start=(kc == 0), stop=(kc == 7)) for kc in range(8)],
                        reads=[wkey, 'condb'], writes=['ps3'])
            ps3v = PS[3][:, 0:96].rearrange("p (j c) -> p j c", c=2)
            for ci in range(2):
                em.op('dve', lambda e, ci=ci: e.tensor_tensor(modT[:, ci, :], ps3v[:, :, ci], vecs[:, V_BMOD:V_BMOD + 48], ALU.add),
                      reads=['vecs'], writes=['ps3', 'modT'])
                em.op('dve', lambda e, ci=ci: e.scalar_tensor_tensor(scA[:, ci, 0:8], modT[:, ci, 8:16], 1.0,
                                                                    vecs[:, V_GATT:V_GATT + 8], ALU.add, ALU.mult),
                      reads=['modT', 'vecs'], writes=['scA'])
                em.op('dve', lambda e, ci=ci: e.scalar_tensor_tensor(scA[:, ci, 8:16], modT[:, ci, 32:40], 1.0,
                                                                    vecs[:, V_GFFN:V_GFFN + 8], ALU.add, ALU.mult),
                      reads=['modT', 'vecs'], writes=['scA'])

        def inproj(l, blk, V_NRM):
            smp = blk is not None
            t0g = 0 if not smp else blk * NT
            for k_, src_ in enumerate((0, 2, 5)):
                em.op('pool', lambda e: e.tensor_scalar_mul(gsc[:, k_:k_ + 1], vcol(V_NRM, src_), 0.125), reads=['vecs'],
                      writes=['gsc'])
            if smp:
                for th in range(2):
                    em.dma('sp', trC[th][:], ropeC_in[:, t0g + th * 512:t0g + (th + 1) * 512], writes=['trC%d' % th])
                    em.dma('sp', trS[th][:], ropeS_in[:, t0g + th * 512:t0g + (th + 1) * 512], writes=['trS%d' % th])

            def sink_bf(out_bf, out_key, scale):
                em.op('act', lambda e: e.activation(out_bf, tf[4][:], AF.Identity, scale=scale), reads=['tf4'],
                      writes=[out_key])

            def chunk(wb, wkey, cc, n, col, wb2=None, wkey2=None):
                for th in range(2):
                    ts = slice(th * 512, th * 512 + 512)
                    gts = slice(t0g + th * 512, t0g + th * 512 + 512)
                    ps, pskey = next_ps()
                    proj_fm(wb, wkey, cc * 128, n, th, ps, pskey)
                    if col < C_KA:
                        c = (col - C_QA) // 128
                        head_norm_core(ps, pskey, gsc[:, 0:1], qaT[:, c, ts], ('qaT', th))
                    elif col < C_QB:
                        c = (col - C_KA) // 128
                        if not smp:
                            head_norm_core(ps, pskey, vcol(V_NRM, 1))
                            em.dma('sp', o_nak[l][c * 128:(c + 1) * 128, ts], tf[4][:], reads=['tf4'])
                            em.op('act', lambda e: e.copy(Kslab[:, c, ts], tf[4][:]), reads=['tf4'], writes=['Kslab'])
                        else:
                            head_norm_core(ps, pskey, vcol(V_NRM, 1), tb[2][:], 'tb2')
                            em.dma('sp', kaT_scr[:, c, gts], tb[2][:], reads=['tb2'], writes=['kaT_scr'])
                    elif col < C_QBR or (C_KB <= col < C_KBR):
                        isq = col < C_QBR
                        c = (col - C_QB) // 128 if isq else 0
                        if isq:
                            dst, dkey = qbT[:, c, ts], ('qbT', th)
                            g0_, g1_ = gsc[:, 1:2], gsc[:, 2:3]
                        else:
                            dst, dkey = kbT_all[:, gts], 'kbT_all'
                            g0_, g1_ = vcol(V_NRM, 3), vcol(V_NRM, 6)
                        if not smp:
                            if isq:
                                head_norm_core(ps, pskey, g0_, dst, dkey)
                            else:
                                head_norm_core(ps, pskey, g0_)
                                em.dma('sp', o_gk[l][:, ts], tf[4][:], reads=['tf4'])
                                em.op('act', lambda e: e.copy(dst, tf[4][:]), reads=['tf4'], writes=[dkey])
                        else:
                            head_norm_core(ps, pskey, g0_)
                            em.op('dve', lambda e: e.tensor_tensor(tf[5][:], tf[4][:], trC[th][:], ALU.mult),
                                  reads=['tf4', 'trC%d' % th], writes=['tf5'])
                            ps2_, ps2key = next_ps()
                            rc0 = (cc * 128) if isq else (cc + 1) * 128
                            wbr, wkr = (wb2, wkey2) if isq else (wb, wkey)
                            proj_fm(wbr, wkr, rc0, 128, th, ps2_, ps2key)
                            head_norm_core(ps2_, ps2key, g1_)
                            em.op('dve', lambda e: e.tensor_tensor(tf[4][:], tf[4][:], trS[th][:], ALU.mult),
                                  reads=['trS%d' % th], writes=['tf4'])
                            em.op('dve', lambda e: e.tensor_tensor(dst, tf[4][:], tf[5][:], ALU.add),
                                  reads=['tf4', 'tf5'], writes=[dkey])
                    elif col < C_KC:
                        c = (col - C_QC) // 128
                        em.op('act', lambda e, ps=ps, c=c: e.activation(qcT[:, c, ts], ps[:], AF.Identity, scale=0.125),
                              writes=[pskey, ('qcT', th)])
                    elif col < C_RC:
                        c = (col - C_KC) // 128
                        em.op('act', lambda e, ps=ps, c=c: e.copy(kcT[:, c, ts], ps[:]), writes=[pskey, ('kcT', th)])
                    elif col < C_Z:
                        c = (col - C_RC) // 128
                        em.op('act', lambda e, ps=ps, c=c: e.activation(rcT[:, c, ts], ps[:], AF.Silu),
                              writes=[pskey, ('rcT', th)])
                    else:
                        em.op('act', lambda e, ps=ps: e.copy(zT[0:32, ts], ps[0:32, :]), writes=[pskey, 'zT'])

            wb, wkey = next_wbuf()
            load_w(wb, wkey, w_fm[l][:, 0:512], 512)
            for cc in range(4):
                chunk(wb, wkey, cc, 128, cc * 128)
            wb, wkey = next_wbuf()
            load_w(wb, wkey, w_fm[l][:, C_QB:C_QB + 512], 512)
            wb2 = wkey2 = None
            if smp:
                wb2, wkey2 = next_wbuf()
                load_w(wb2, wkey2, w_fm[l][:, C_QBR:C_QBR + 512], 512)
            for cc in range(4):
                chunk(wb, wkey, cc, 128, C_QB + cc * 128, wb2, wkey2)
            wb, wkey = next_wbuf()
            load_w(wb, wkey, w_fm[l][:, C_KB:C_KB + 512], 512)
            chunk(wb, wkey, 0, 128, C_KB)
            chunk(wb, wkey, 2, 128, C_QC)
            chunk(wb, wkey, 3, 128, C_QC + 128)
            wb, wkey = next_wbuf()
            load_w(wb, wkey, w_fm[l][:, C_KC:C_KC + 512], 512)
            for cc in range(4):
                chunk(wb, wkey, cc, 128, C_KC + cc * 128)
            wb, wkey = next_wbuf()
            load_w(wb, wkey, w_fm[l][:, C_Z:C_Z + 32], 32)
            chunk(wb, wkey, 0, 32, C_Z)

            wtm = [None, None]
            for gi, (g0, gn) in enumerate([(0, 512), (512, 384)]):
                wb, wkey = next_wbuf()
                load_w(wb, wkey, w_tm[l][:, g0:g0 + gn], gn)
                wtm[gi] = (wb, wkey)
            em.dma('pool', wz2[:], w_z2[l], writes=['wz2'])
            for i in range(8):
                tsl = slice(i * 128, (i + 1) * 128)
                gi_tile = (0 if not smp else blk * 8) + i
                th = i // 4
                for gi, (g0, gn) in enumerate([(0, 512), (512, 384)]):
                    wb, wkey = wtm[gi]
                    ps, pskey = next_ps()
                    em.op('pe', [lambda e, kc=kc, wb=wb, ps=ps, gn=gn: e.matmul(
                        ps[:, 0:gn], hT[:, kc, tsl], wb[:, kc, 0:gn], start=(kc == 0), stop=(kc == 7)) for kc in range(8)],
                        reads=[wkey, ('hT', th)], writes=[pskey])
                    if gi == 0:
                        if not smp:
                            em.op('act', lambda e, ps=ps: e.copy(tf[5][:, 0:512], ps[:, 0:512]), writes=[pskey, 'tf5'])
                            em.dma('sp', o_nav[l][tsl, :], tf[5][:, 0:256], reads=['tf5'])
                            em.dma('sp', o_gv[l][tsl, :], tf[5][:, 256:384], reads=['tf5'])
                            em.op('dve', lambda e: e.tensor_copy(Vslab[:, i, :], tf[5][:, 0:256]), reads=['tf5'], writes=['Vslab'])
                            em.op('dve', lambda e: e.tensor_copy(vaug[:, i, 0, 0:64], tf[5][:, 256:320]), reads=['tf5'], writes=['vb_all'])
                            em.op('dve', lambda e: e.tensor_copy(vaug[:, i, 1, 64:128], tf[5][:, 320:384]), reads=['tf5'], writes=['vb_all'])
                            em.op('dve', lambda e: e.tensor_copy(vtok[:, i, 0:128], tf[5][:, 384:512]), reads=['tf5'],
                                  writes=[('vtok', i)])
                        else:
                            em.op('act', lambda e, ps=ps: e.copy(tb[1][:], ps[:]), writes=[pskey, 'tb1'])
                            em.dma('sp', va_scr[gi_tile * 128:(gi_tile + 1) * 128, :], tb[1][:, 0:256], reads=['tb1'],
                                   writes=['va_scr'])
                            em.op('dve', lambda e: e.tensor_copy(vaug[:, gi_tile, 0, 0:64], tb[1][:, 256:320]), reads=['tb1'],
                                  writes=['vb_all'])
                            em.op('dve', lambda e: e.tensor_copy(vaug[:, gi_tile, 1, 64:128], tb[1][:, 320:384]), reads=['tb1'],
                                  writes=['vb_all'])
                            em.op('dve', lambda e: e.tensor_copy(vtok[:, i, 0:128], tb[1][:, 384:512]), reads=['tb1'],
                                  writes=[('vtok', i)])
                    else:
                        em.op('act', lambda e, ps=ps: e.copy(vtok[:, i, 128:256], ps[:, 0:128]), writes=[pskey, ('vtok', i)])
                        em.op('dve', lambda e, ps=ps: e.tensor_copy(ktok[:, i, :], ps[:, 128:384]), writes=[pskey, ('ktok', i)])
                ps, pskey = next_ps()
                em.op('pe', lambda e, ps=ps: e.matmul(ps[:], zT[:, tsl], wz2[:], start=True, stop=True),
                      reads=['zT', 'wz2'], writes=[pskey])
                em.op('act', lambda e, ps=ps: e.activation(tf[0][:], ps[:], AF.Exp, scale=-1.0), writes=[pskey, 'tf0'])
                em.op('act', lambda e: e.activation(tf[1][:], tf[0][:], AF.Ln, bias=onec[:, 0:1]), reads=['tf0', 'onec'],
                      writes=['tf1'])
                em.op('dve', lambda e: e.tensor_scalar_mul(latok[:, i, :], tf[1][:], -1.0 / 16.0), reads=['tf1'],
                      writes=[('latok', i)])

        TBK = ['tb0', 'tb1', 'tb2', 'tb3']

        def attn_generic(qT, qkey, nchunk, qsl, n, th, tiles_of, oT, okey, aug=False):
            per = 512 // n
            stages = []
            for c in range(nchunk):
                for hh in range(2):
                    tl = tiles_of(c, hh)
                    nt = len(tl)
                    for gi, g0 in enumerate(range(0, nt, per)):
                        stages.append((c, hh, gi, tl[g0:g0 + per], g0, nt))

            def acc_bank(c, hh):
                if not aug:
                    return None
                return ((6, 2) if hh == 0 else (7, 3))[c % 2]

            def slot_of(st):
                c, hh, gi, grp, g0, nt = st
                if aug:
                    k = st_index[id(st)] % 4
                    return (0, 1, 4, 5)[k], k
                return ((0, 1) if hh == 0 else (4, 5))[gi % 2], (0 if hh == 0 else 2) + gi % 2

            def emit_qk(st):
                c, hh, gi, grp, g0, nt = st
                bank, ti = slot_of(st)
                fns = []
                rk = [(qkey, th), 'cb']
                if aug and g0 == 0:
                    em.op('pool', lambda e: e.tensor_copy(qpad[hh][hh * 64:(hh + 1) * 64, 0:n], qT[hh * 64:(hh + 1) * 64, c, qsl]),
                          reads=[(qkey, th)], writes=['qpad%d' % hh])
                if aug:
                    rk.append('qpad%d' % hh)
                for j, (kap, kkeys, vap, vkeys, bias) in enumerate(grp):
                    o = PS[bank][:, j * n:(j + 1) * n]
                    if aug:
                        fns.append(lambda e, o=o, kap=kap: e.matmul(o, kap, qpad[hh][:, 0:n], start=True, stop=True))
                        rk += list(kkeys)
                        continue
                    fns.append(lambda e, o=o, kap=kap, bias=bias: e.matmul(
                        o, kap, qT[hh * 64:(hh + 1) * 64, c, qsl], start=True, stop=(bias is None)))
                    if bias is not None:
                        fns.append(lambda e, o=o, bias=bias: e.matmul(o, IDENT, bias, start=False, stop=True))
                    rk += list(kkeys)
                em.op('pe', fns, reads=rk, writes=['ps%d' % bank])
                w = len(grp) * n
                em.op('act', lambda e: e.activation(tb[ti][:, 0:w], PS[bank][:, 0:w], AF.Exp),
                      writes=['ps%d' % bank, TBK[ti]])

            def emit_pv(st):
                c, hh, gi, grp, g0, nt = st
                bank_, ti = slot_of(st)
                tbt = tb[ti]
                fns = []
                rk = [TBK[ti], 'cb']
                A = acc_bank(c, hh)
                for j, (kap, kkeys, vap, vkeys, bias) in enumerate(grp):
                    first = (g0 + j == 0)
                    last = (g0 + j == nt - 1)
                    if aug:
                        fns.append(lambda e, vap=vap, j=j, first=first, last=last: e.matmul(
                            PS[A][:, 0:n], vap, tbt[:, j * n:(j + 1) * n], start=first, stop=last))
                    else:
                        fns.append(lambda e, vap=vap, j=j, first=first, last=last: e.matmul(
                            PS[6][hh * 64:(hh + 1) * 64, 0:n], vap, tbt[:, j * n:(j + 1) * n], start=first, stop=last))
                        fns.append(lambda e, j=j, first=first, last=last: e.matmul(
                            PS[7][hh * 64:(hh + 1) * 64, 0:n], ONES[:, 0:64], tbt[:, j * n:(j + 1) * n], start=first, stop=last))
                    rk += list(vkeys)
                em.op('pe', fns, reads=rk, writes=(['ps%d' % A] if aug else ['ps6', 'ps7']))
                if g0 + len(grp) < nt:
                    return
                if aug:
                    rn = slice(hh * 64, hh * 64 + 64)
                    rd = slice((1 - hh) * 64, (1 - hh) * 64 + 64)
                    tfx = tf[4 + hh]
                    em.op('dve', lambda e: e.reciprocal(tfx[rn, 0:n], PS[A][rd, 0:n]), writes=['ps%d' % A, 'tf%d' % (4 + hh)])
                    em.op('dve', lambda e: e.tensor_tensor(oT[rn, c, qsl], tfx[rn, 0:n], PS[A][rn, 0:n], ALU.mult),
                          reads=['tf%d' % (4 + hh)], writes=['ps%d' % A, (okey, th)])
                elif hh == 1:
                    em.op('dve', lambda e: e.reciprocal(tf[0][:, 0:n], PS[7][:, 0:n]), writes=['ps7', 'tf0'])
                    em.op('dve', lambda e: e.tensor_tensor(oT[:, c, qsl], PS[6][:, 0:n], tf[0][:, 0:n], ALU.mult),
                          reads=['tf0'], writes=['ps6', (okey, th)])

            st_index = {id(st): k for k, st in enumerate(stages)}
            lag = 2 if aug else 1
            for k, st in enumerate(stages):
                emit_qk(st)
                if k - lag >= 0:
                    emit_pv(stages[k - lag])
            for k in range(max(len(stages) - lag, 0), len(stages)):
                emit_pv(stages[k])

        def attention_prompt():
            for s in range(NT // SEQ):
                th = s // 2
                qs = slice(s * SEQ, (s + 1) * SEQ)

                def tiles_a(c, hh, s=s):
                    h = 2 * c + hh
                    return [(Kslab[hh * 64:(hh + 1) * 64, c, (2 * s + kt) * 128:(2 * s + kt + 1) * 128], ['Kslab'],
                             Vslab[:, 2 * s + kt, h * 64:(h + 1) * 64], ['Vslab'], None) for kt in range(2)]

                def tiles_b(c, hh, s=s):
                    return [(kbT_all[:, (2 * s + kt) * 128:(2 * s + kt + 1) * 128], ['kbT_all'],
                             vaug[:, 2 * s + kt, hh, :], ['vb_all'], None) for kt in range(2)]

                attn_generic(qaT, 'qaT', 2, qs, 256, th, tiles_a, oaT, 'oaT')
                attn_generic(qbT, 'qbT', 4, qs, 256, th, tiles_b, obT, 'obT', aug=True)

        def attention_sample(l, blk):
            sb0 = max(8 * blk - 2, 0)
            sb1 = min(8 * blk + 10, 32)
            nsl = sb1 - sb0
            em.dma('sp', Kslab[:, :, 0:nsl * 128], kaT_scr[:, :, sb0 * 128:sb1 * 128], reads=['kaT_scr'], writes=['Kslab'])
            em.dma('pool', Kslab[:, :, 1536:2048], cnak_in[l], writes=['Kslab'])
            em.dma('sp', Vslab[:, 0:nsl, :], va_scr[sb0 * 128:sb1 * 128, :].rearrange("(i p) c -> p i c", p=128),
                   reads=['va_scr'], writes=['Vslab'])
            em.dma('pool', Vslab[:, 12:16, :], cnav_in[l].rearrange("(i p) c -> p i c", p=128), writes=['Vslab'])
            em.dma('pool', nabI, nab_in[l][0], writes=['nabI'])
            if blk == NBLK - 1:
                em.dma('pool', kbT_all[:, NTS:NTS + 512], cgk_in[l], writes=['kbT_all'])
                cgv3 = cgv_in[l].rearrange("(i p) c -> p i c", p=128)
                em.dma('pool', vaug[:, 32:36, 0, 0:64], cgv3[:, :, 0:64], writes=['vb_all'])
                em.dma('pool', vaug[:, 32:36, 1, 64:128], cgv3[:, :, 64:128], writes=['vb_all'])
            for pr in range(8):
                r = 16 * blk + 2 * pr
                t0 = min(max((r - 4) // 2, 0), 27)
                var = {0: 1, 2: 2, 60: 3, 62: 4}.get(r, 0)
                if var == 0:
                    nb, nbkey = nabI, 'nabI'
                else:
                    em.dma('pool', nabE, nab_in[l][var], writes=['nabE'])
                    nb, nbkey = nabE, 'nabE'
                qsl = slice(pr * 128, (pr + 1) * 128)
                th = pr // 4

                def tiles_a(c, hh, t0=t0, nb=nb, nbkey=nbkey):
                    h = 2 * c + hh
                    tl = []
                    for j in range(5):
                        si = t0 + j - sb0
                        tl.append((Kslab[hh * 64:(hh + 1) * 64, c, si * 128:(si + 1) * 128], ['Kslab', nbkey],
                                   Vslab[:, si, h * 64:(h + 1) * 64], ['Vslab'],
                                   nb[:, j * 512 + h * 128: j * 512 + (h + 1) * 128]))
                    for j in range(4):
                        tl.append((Kslab[hh * 64:(hh + 1) * 64, c, 1536 + j * 128:1536 + (j + 1) * 128], ['Kslab'],
                                   Vslab[:, 12 + j, h * 64:(h + 1) * 64], ['Vslab'], None))
                    return tl

                attn_generic(qaT, 'qaT', 2, qsl, 128, th, tiles_a, oaT, 'oaT')
            for th in range(2):
                qsl = slice(th * 512, (th + 1) * 512)

                def tiles_b(c, hh):
                    return [(kbT_all[:, kt * 128:(kt + 1) * 128], ['kbT_all'],
                             vaug[:, kt, hh, :], ['vb_all'], None) for kt in range(36)]

                attn_generic(qbT, 'qbT', 4, qsl, 512, th, tiles_b, obT, 'obT', aug=True)

        def gla_decay(i, th, d, slot):
            TRI = TRI_F if d == 0 else TRI_B
            tsl = slice(i * 128, (i + 1) * 128)
            em.op('pe', [lambda e, hp=hp: e.matmul(
                PS[4][:, hp * 128:(hp + 1) * 128], latok[:, i, d * 256 + hp * 128:d * 256 + (hp + 1) * 128],
                TRI, start=True, stop=True) for hp in range(2)],
                reads=[('latok', i), 'cb'], writes=['ps4'])
            em.op('act', lambda e: e.activation(tf[0][:, 0:256], PS[4][:, 0:256], AF.Exp), writes=['ps4', 'tf0'])
            em.op('act', lambda e: e.activation(tf[1][:, 0:256], PS[4][:, 0:256], AF.Exp, scale=-1.0), writes=['ps4', 'tf1'])
            qtt = qtil[d][slot]
            em.op('dve', lambda e: e.tensor_tensor(
                qtt[:].rearrange("p (c t) -> p c t", c=2), qcT[:, :, tsl],
                tf[0][:, 0:256].rearrange("p (c t) -> p c t", c=2), ALU.mult),
                reads=[('qcT', th), 'tf0'], writes=[('qtil', d, slot)])
            em.op('dve', lambda e: e.tensor_tensor(
                tb[2][:, 0:256].rearrange("p (c t) -> p c t", c=2), kcT[:, :, tsl],
                tf[1][:, 0:256].rearrange("p (c t) -> p c t", c=2), ALU.mult),
                reads=[('kcT', th), 'tf1'], writes=['tb2'])
            att = attm[d][slot]
            for hh, bank in ((0, 5), (1, 3)):
                em.op('pe', [lambda e, hp=hp: e.matmul(
                    PS[bank][:, hp * 128:(hp + 1) * 128],
                    tb[2][hh * 64:hh * 64 + 64, hp * 128:hp * 128 + 128],
                    qtt[hh * 64:hh * 64 + 64, hp * 128:hp * 128 + 128],
                    start=True, stop=True) for hp in range(2)],
                    reads=['tb2', ('qtil', d, slot)], writes=['ps%d' % bank])
                em.op('dve', lambda e: e.tensor_tensor(
                    att[:].rearrange("p (hp hh t) -> p hp hh t", hp=2, hh=2)[:, :, hh, :],
                    PS[bank][:, 0:256].rearrange("p (hp t) -> p hp t", hp=2),
                    MASK4[d][:, 0:256].rearrange("p (hp t) -> p hp t", hp=2), ALU.mult),
                    reads=['cb'], writes=['ps%d' % bank, ('attm', d, slot)])

        def gla_chain(i, d, slot, save_to=None):
            TRIX = TRIX_F if d == 0 else TRIX_B
            em.op('act', lambda e: e.copy(Sprev[d][slot][:], S32[d][:]), reads=[('S32', d)], writes=[('Sprev', d, slot)])
            if save_to is not None:
                em.dma('sp', save_to, Sprev[d][slot][:].rearrange("p a b -> p (a b)"), reads=[('Sprev', d, slot)],
                       writes=['sprev_scr'])
            em.op('pe', [lambda e, hp=hp: e.matmul(
                PS[7][:, 128 + hp:129 + hp], latok[:, i, d * 256 + hp * 128:d * 256 + (hp + 1) * 128], ONES[:, 0:1],
                start=True, stop=True) for hp in range(2)], reads=[('latok', i), 'cb'], writes=['ps7'])
            em.op('act', lambda e: e.activation(dec[d][:, 0:2], PS[7][:, 128:130], AF.Exp), writes=['ps7', ('dec', d)])
            em.op('pe', lambda e: e.matmul(PS[4][:, 0:256], TRIX, latok[:, i, d * 256:(d + 1) * 256], start=True, stop=True),
                  reads=[('latok', i), 'cb'], writes=['ps4'])
            em.op('act', lambda e: e.activation(tf[2][:, 0:256], PS[4][:, 0:256], AF.Exp), writes=['ps4', 'tf2'])
            em.op('dve', lambda e: e.tensor_tensor(tb[0][:, 0:256], ktok[:, i, :], tf[2][:, 0:256], ALU.mult),
                  reads=[('ktok', i), 'tf2'], writes=['tb0'])
            em.op('pe', [lambda e, h=h: e.matmul(
                PS[7][(h % 2) * 64:(h % 2) * 64 + 64, (h // 2) * 64:(h // 2) * 64 + 64],
                tb[0][:, h * 64:(h + 1) * 64], vtok[:, i, h * 64:(h + 1) * 64],
                start=True, stop=True) for h in range(4)],
                reads=['tb0', ('vtok', i)], writes=['ps7'])
            for hp in range(2):
                em.op('dve', lambda e, hp=hp: e.scalar_tensor_tensor(
                    S32[d][:, hp, :], S32[d][:, hp, :], dec[d][:, hp:hp + 1], PS[7][:, hp * 64:(hp + 1) * 64],
                    ALU.mult, ALU.add), reads=[('dec', d)], writes=['ps7', ('S32', d)])

        def gla_out(i, th, slot, use_inter, V_NRM):
            tsl = slice(i * 128, (i + 1) * 128)
            fns = []
            for h in range(4):
                hh, hp = h % 2, h // 2
                outp = PS[6][hh * 64:hh * 64 + 64, hp * 128:(hp + 1) * 128]
                seqm = []
                for d in range(2):
                    seqm.append((vtok[:, i, h * 64:(h + 1) * 64], attm[d][slot][:, h * 128:(h + 1) * 128]))
                    if use_inter[d]:
                        seqm.append((Sprev[d][slot][hh * 64:hh * 64 + 64, hp, :],
                                     qtil[d][slot][hh * 64:hh * 64 + 64, hp * 128:(hp + 1) * 128]))
                for k, (lt, rh) in enumerate(seqm):
                    fns.append(lambda e, outp=outp, lt=lt, rh=rh, k=k, n=len(seqm): e.matmul(
                        outp, lt, rh, start=(k == 0), stop=(k == n - 1)))
            em.op('pe', fns, reads=[('vtok', i)] + [('attm', d, slot) for d in range(2)] +
                  [('Sprev', d, slot) for d in range(2)] + [('qtil', d, slot) for d in range(2)], writes=['ps6'])
            em.op('act', lambda e: e.activation(tb[1][:, 0:256], PS[6][:, 0:256], AF.Square), writes=['ps6', 'tb1'])
            em.op('dve', lambda e: e.tensor_copy(tf[3][:, 0:256], PS[6][:, 0:256]), writes=['ps6', 'tf3'])
            em.op('pe', lambda e: e.matmul(PS[2][:, 0:256], BLK, tb[1][:, 0:256], start=True, stop=True),
                  reads=['tb1', 'cb'], writes=['ps2'])
            em.op('act', lambda e: e.activation(tf[0][:, 0:256], PS[2][:, 0:256], AF.Ln, bias=epsc[:, 0:1],
                                                scale=1.0 / 64), reads=['epsc'], writes=['ps2', 'tf0'])
            em.op('act', lambda e: e.activation(tf[1][:, 0:256], tf[0][:, 0:256], AF.Exp, scale=-0.5), reads=['tf0'],
                  writes=['tf1'])
            em.op('dve', lambda e: e.scalar_tensor_tensor(tf[4][:, 0:256], tf[3][:, 0:256], vcol(V_NRM, 4),
                                                         tf[1][:, 0:256], ALU.mult, ALU.mult),
                  reads=['tf3', 'tf1', 'vecs'], writes=['tf4'])
            em.op('dve', lambda e: e.tensor_tensor(
                ocT[:, :, tsl], tf[4][:, 0:256].rearrange("p (c t) -> p c t", c=2), rcT[:, :, tsl], ALU.mult),
                reads=['tf4', ('rcT', th)], writes=[('ocT', th)])

        def gla_prompt(l, V_NRM):
            for s in range(NT // SEQ):
                th = s // 2
                for d in range(2):
                    order = [0, 1] if d == 0 else [1, 0]
                    em.op('pool', lambda e, d=d: e.memset(S32[d][:], 0.0), writes=[('S32', d)])
                    for ci in order:
                        i = 2 * s + ci
                        gla_decay(i, th, d, ci)
                        gla_chain(i, d, ci)
                    dst = (o_sf if d == 0 else o_sb)[l][s]
                    em.dma('sp', dst.rearrange("hp p v -> p hp v"), S32[d][:], reads=[('S32', d)])
                for ci in range(2):
                    gla_out(2 * s + ci, th, ci, [ci != 0, ci != 1], V_NRM)

        mrr = [0]

        def merge_outproj(l, ci, xsrc, xdst, xkey, hook_mid=None):
            em.barrier(soft=True)
            for j in range(8):
                wb, wkey = next_wbuf()
                em.dma('pool', wb[:, :, 0:384], w_g[l][j], writes=[wkey])
                brw, brk = brw2[j % 2], 'brw%d' % (j % 2)
                em.dma('pool', brw[:], w_br[l][j], writes=[brk])
                for th in range(2):
                    ts = slice(th * 512, th * 512 + 512)
                    for bi, (oT, okey, k0, nk) in enumerate([(oaT, 'oaT', 0, 2), (obT, 'obT', 2, 4), (ocT, 'ocT', 6, 2)]):
                        ps, pskey = next_ps()
                        proj_fm(wb, wkey, bi * 128, 128, th, ps, pskey)
                        mrr[0] ^= 1
                        sg, sgk = (tf[0], 'tf0') if mrr[0] else (tf[3], 'tf3')
                        pr_, prk = (tf[2], 'tf2') if mrr[0] else (tf[5], 'tf5')
                        bb_ = 3 if mrr[0] else 2
                        em.op('act', lambda e, ps=ps: e.activation(sg[:], ps[:], AF.Sigmoid), writes=[pskey, sgk])
                        em.op('pe', [lambda e, kc=kc, oT=oT, k0=k0, nk=nk: e.matmul(
                            PS[bb_][:], brw[:, k0 + kc, :], oT[:, kc, ts],
                            start=(kc == 0), stop=(kc == nk - 1)) for kc in range(nk)],
                            reads=[brk, (okey, th)], writes=['ps%d' % bb_])
                        if bi == 0:
                            em.op('dve', lambda e: e.tensor_tensor(tf[1][:], PS[bb_][:], sg[:], ALU.mult),
                                  reads=[sgk], writes=['ps%d' % bb_, 'tf1'])
                        else:
                            em.op('dve', lambda e: e.tensor_tensor(pr_[:], PS[bb_][:], sg[:], ALU.mult),
                                  reads=[sgk], writes=['ps%d' % bb_, prk])
                            em.op('dve', lambda e: e.tensor_tensor(tf[1][:], tf[1][:], pr_[:], ALU.add),
                                  reads=[prk], writes=['tf1'])
                    em.op('act', lambda e, j=j: e.copy(mT[:, j, ts], tf[1][:]), reads=['tf1'], writes=[('mT', th)])
            if hook_mid is not None:
                hook_mid()
            tiles = [(g, cc, th) for g in range(2) for cc in range(4) for th in range(2)]

            def xbuf(k):
                return (tf[3], 'tf3') if k % 2 == 0 else (tf[4], 'tf4')

            def xload(k):
                g, cc, th = tiles[k]
                xt_, xtk = xbuf(k)
                em.dma('sp', xt_[:], xsrc[:, g * 4 + cc, th * 512:(th + 1) * 512], reads=[xkey, 'xs2'], writes=[xtk])

            xload(0)
            wb = wkey = None
            for k, (g, cc, th) in enumerate(tiles):
                if cc == 0 and th == 0:
                    wb, wkey = next_wbuf()
                    load_w(wb, wkey, w_out[l][:, g * 512:(g + 1) * 512], 512)
                j = g * 4 + cc
                ts = slice(th * 512, th * 512 + 512)
                ps, pskey = next_ps()
                proj_fm(wb, wkey, cc * 128, 128, th, ps, pskey, rhsT=mT, rkey='mT')
                if k + 1 < len(tiles):
                    xload(k + 1)
                xt_, xtk = xbuf(k)
                em.op('dve', lambda e, ps=ps, j=j: e.scalar_tensor_tensor(
                    xt_[:], ps[:], modT[:, ci, 16 + j:17 + j], xt_[:], ALU.mult, ALU.add),
                    reads=['modT'], writes=[pskey, xtk])
                em.dma('sp', xdst[:, j, ts], xt_[:], reads=[xtk], writes=[xkey])

        def ffn(l, ci, V_CONV, smp, has_left, has_right, xout, xoutkey):
            em.barrier()
            pending_tail = []
            wq = [wbuf[k // 2][:, :, (k % 2) * 256:(k % 2) * 256 + 256] for k in range(4)]
            for fg in range(11):
                nf = 2
                r_ = (fg % 2) * 2
                wa, wakey = wq[r_], 'wq%d' % r_
                em.dma('pool', wa, w_up[l][0][fg], writes=[wakey])
                wg, wgkey = wq[r_ + 1], 'wq%d' % (r_ + 1)
                em.dma('pool', wg, w_up[l][1][fg], writes=[wgkey])
                for cc in range(nf):
                    f = fg * 2 + cc
                    if f % 2 == 0:
                        ACC, ACCK = [tf[0], tf[1], tf[2], tf[3]], ['tf0', 'tf1', 'tf2', 'tf3']
                        SIL, SILK = tf[4], 'tf4'
                    else:
                        ACC, ACCK = [trC[0], trC[1], trS[0], trS[1]], ['trC0', 'trC1', 'trS0', 'trS1']
                        SIL, SILK = tf[5], 'tf5'
                    for part, (wb, wkey) in enumerate([(wa, wakey), (wg, wgkey)]):
                        if part == 1 and len(pending_tail) > 0:
                            pending_tail.pop(0)()
                        cbase = V_CONV + part * 4 * NFF
                        w0 = vcol(cbase, f)
                        w1 = vcol(cbase + NFF, f)
                        w2 = vcol(cbase + 2 * NFF, f)
                        bb = vcol(cbase + 3 * NFF, f)
                        pss = []
                        pb = 0 if part == 0 else 4
                        hb = 2 if part == 0 else 3
                        for th in range(2):
                            ps, pskey = PS[pb + th], 'ps%d' % (pb + th)
                            proj_fm(wb, wkey, cc * 128, 128, th, ps, pskey)
                            pss.append((ps, pskey))
                        if smp and (has_left or has_right):
                            em.op('pe', [lambda e, kc=kc: e.matmul(PS[hb][:, 0:2], wb[:, kc, cc * 128:(cc + 1) * 128],
                                                                  hTh[:, kc, :], start=(kc == 0), stop=(kc == 7))
                                         for kc in range(8)], reads=[wkey, 'hTh'], writes=['ps%d' % hb])
                        for th in range(2):
                            ps, pskey = pss[th]
                            acc = ACC[part * 2 + th]
                            akey = ACCK[part * 2 + th]
                            em.op('act', lambda e, ps=ps, acc=acc: e.activation(acc[:], ps[:], AF.Identity, bias=bb, scale=w1),
                                  reads=['vecs'], writes=[pskey, akey])
                            if not smp:
                                a3 = acc[:].rearrange("p (s t) -> p s t", s=2)
                                p3 = ps[:].rearrange("p (s t) -> p s t", s=2)
                                em.op('dve', lambda e, a3=a3, p3=p3: e.scalar_tensor_tensor(
                                    a3[:, :, 1:256], p3[:, :, 0:255], w0, a3[:, :, 1:256], ALU.mult, ALU.add),
                                    reads=['vecs'], writes=[pskey, akey])
                                em.op('dve', lambda e, a3=a3, p3=p3: e.scalar_tensor_tensor(
                                    a3[:, :, 0:255], p3[:, :, 1:256], w2, a3[:, :, 0:255], ALU.mult, ALU.add),
                                    reads=['vecs'], writes=[pskey, akey])
                            else:
                                em.op('dve', lambda e, acc=acc, ps=ps: e.scalar_tensor_tensor(
                                    acc[:, 1:512], ps[:, 0:511], w0, acc[:, 1:512], ALU.mult, ALU.add),
                                    reads=['vecs'], writes=[pskey, akey])
                                em.op('dve', lambda e, acc=acc, ps=ps: e.scalar_tensor_tensor(
                                    acc[:, 0:511], ps[:, 1:512], w2, acc[:, 0:511], ALU.mult, ALU.add),
                                    reads=['vecs'], writes=[pskey, akey])
                        if smp:
                            a0, a1 = ACC[part * 2], ACC[part * 2 + 1]
                            k0, k1 = ACCK[part * 2], ACCK[part * 2 + 1]
                            em.op('dve', lambda e: e.scalar_tensor_tensor(
                                a0[:, 511:512], PS[pb + 1][:, 0:1], w2, a0[:, 511:512], ALU.mult, ALU.add),
                                reads=['vecs'], writes=['ps%d' % (pb + 1), k0])
                            em.op('dve', lambda e: e.scalar_tensor_tensor(
                                a1[:, 0:1], PS[pb][:, 511:512], w0, a1[:, 0:1], ALU.mult, ALU.add),
                                reads=['vecs'], writes=['ps%d' % pb, k1])
                            if has_left:
                                em.op('dve', lambda e: e.scalar_tensor_tensor(
                                    a0[:, 0:1], PS[hb][:, 0:1], w0, a0[:, 0:1], ALU.mult, ALU.add),
                                    reads=['vecs'], writes=['ps%d' % hb, k0])
                            if has_right:
                                em.op('dve', lambda e: e.scalar_tensor_tensor(
                                    a1[:, 511:512], PS[hb][:, 1:2], w2, a1[:, 511:512], ALU.mult, ALU.add),
                                    reads=['vecs'], writes=['ps%d' % hb, k1])
                    def tail(f=f, ACC=ACC, ACCK=ACCK):
                        for th in range(2):
                            ts = slice(th * 512, th * 512 + 512)
                            SIL, SILK = (tf[4], 'tf4') if th == 0 else (tf[5], 'tf5')
                            em.op('act', lambda e: e.activation(SIL[:], ACC[2 + th][:], AF.Silu), reads=[ACCK[2 + th]],
                                  writes=[SILK])
                            em.op('dve', lambda e: e.tensor_tensor(actT[:, f, ts], ACC[th][:], SIL[:], ALU.mult),
                                  reads=[ACCK[th], SILK], writes=[('actT', th)])
                    pending_tail.append(tail)
            while pending_tail:
                pending_tail.pop(0)()
            em.barrier(soft=True)
            for j in range(8):
                wdn, wdk = wdn2[j % 2], 'wdn%d' % (j % 2)
                em.dma('pool', wdn[:], w_dn[l][j], writes=[wdk])
                for th in range(2):
                    ts = slice(th * 512, th * 512 + 512)
                    ps, pskey = next_ps()
                    em.op('pe', [lambda e, f=f, ps=ps: e.matmul(ps[:], wdn[:, f, :], actT[:, f, ts],
                                                               start=(f == 0), stop=(f == NFF - 1)) for f in range(NFF)],
                          reads=[wdk, ('actT', th)], writes=[pskey])
                    xo_, xok = (tf[0], 'tf0') if (2 * j + th) % 2 == 0 else (tf[1], 'tf1')
                    em.op('dve', lambda e, ps=ps, j=j: e.scalar_tensor_tensor(
                        xo_[:], ps[:], modT[:, ci, 40 + j:41 + j], xT[:, j, ts], ALU.mult, ALU.add),
                        reads=['modT', 'xT'], writes=[pskey, xok])
                    em.dma('sp', xout[:, j, ts], xo_[:], reads=[xok], writes=[xoutkey])

        try:
          for l in range(n_layers):
            VB = 64 + l * PER_L
            V_BMOD, V_GATT, V_GFFN, V_NRM, V_CONV = VB, VB + 48, VB + 56, VB + 64, VB + 71
            last = (l == n_layers - 1)
            mod_phase(l, V_BMOD, V_GATT, V_GFFN)
            phase(1)
            xp_src = x_in if l == 0 else xp_scr
            em.barrier()
            em.dma('sp', xT, xp_src, reads=['xp'], writes=['xT'])
            rmsnorm_mod(0, 0)
            em.barrier()
            phase(2)
            inproj(l, None, V_NRM)
            phase(4)
            attention_prompt()
            phase(5)
            gla_prompt(l, V_NRM)
            phase(6)
            merge_outproj(l, 0, xp_src, xp_scr, 'xp')
            phase(8)
            em.barrier()
            em.dma('sp', xT, xp_scr, reads=['xp'], writes=['xT'])
            rmsnorm_mod(1, 0)
            ffn(l, 0, V_CONV, False, False, False, y_out if last else xp_scr, 'xp')
            em.barrier()
            phase(9)
            if not do_sample:
                continue
            xs_src = xs_in if l == 0 else xs_scr2
            K_A = [(n_, t_) for n_ in ('kcT', 'rcT') for t_ in range(2)] + \
                  [(n_, i_) for n_ in ('vtok', 'ktok', 'latok') for i_ in range(8)]
            K_C = [(n_, t_) for n_ in ('qaT', 'qbT', 'qcT') for t_ in range(2)]
            K_H = [('hT', 0), ('hT', 1)]
            NSAV = 20480
            em.dma('sp', S32[0][:], sgf_in[l], writes=[('S32', 0)])
            for blk in range(NBLK):
                bs = slice(blk * NT, (blk + 1) * NT)
                if blk == 0:
                    em.barrier()
                em.dma('sp', xT, xs_src[:, :, bs], reads=['xs', 'xs2'], writes=['xT'])
                rmsnorm_mod(0, 1)
                inproj(l, blk, V_NRM)
                for i in range(8):
                    gla_chain(i, 0, 0, save_to=sprev_scr[blk * 8 + i])
                em.dma('sp', sav_arena[blk][:, 0:NSAV], arena[:, 0:NSAV], reads=K_A + K_C, writes=[('sav', blk)])
                em.dma('sp', sav_hT[blk], hT[:].rearrange("p c t -> p (c t)"), reads=K_H, writes=[('savh', blk)])
            phase(10)
            em.barrier()
            em.dma('sp', S32[1][:], sgb_in[l], writes=[('S32', 1)])

            def restore_a(blk):
                em.dma('sp', arena[:, 8192:NSAV], sav_arena[blk][:, 8192:NSAV], reads=[('sav', blk)], writes=K_A)

            def restore_h(blk):
                em.dma('sp', hT[:].rearrange("p c t -> p (c t)"), sav_hT[blk], reads=[('savh', blk)], writes=K_H)

            def restore_c(blk):
                em.dma('sp', arena[:, 0:8192], sav_arena[blk][:, 0:8192], reads=[('sav', blk)],
                       writes=K_C + [('mT', 0), ('mT', 1), ('actT', 0), ('actT', 1)])

            restore_a(NBLK - 1)
            restore_h(NBLK - 1)
            for blk in reversed(range(NBLK)):
                bs = slice(blk * NT, (blk + 1) * NT)
                restore_c(blk)
                attention_sample(l, blk)
                def gla_pre(i):
                    gla_decay(i, i // 4, 0, i % 2)
                    gla_decay(i, i // 4, 1, i % 2)
                    em.dma('sp', Sprev[0][i % 2][:].rearrange("p a b -> p (a b)"), sprev_scr[blk * 8 + i],
                           reads=['sprev_scr'], writes=[('Sprev', 0, i % 2)])

                gla_pre(7)
                for i in reversed(range(8)):
                    gla_chain(i, 1, i % 2)
                    if i > 0:
                        gla_pre(i - 1)
                    gla_out(i, i // 4, i % 2, [True, True], V_NRM)
                nxt = blk - 1
                if nxt >= 0:
                    restore_a(nxt)
                merge_outproj(l, 1, xs_src[:, :, bs], xs_scr[:, :, bs], 'xs',
                              hook_mid=(lambda nxt=nxt: restore_h(nxt)) if nxt >= 0 else None)
            phase(11)
            for blk in range(NBLK):
                bs = slice(blk * NT, (blk + 1) * NT)
                has_left, has_right = blk > 0, blk < NBLK - 1
                if blk == 0:
                    em.barrier()
                em.dma('sp', xT, xs_scr[:, :, bs], reads=['xs'], writes=['xT'])
                if has_left:
                    em.dma('sp', xh[:, :, 0:1], xs_scr[:, :, blk * NT - 1:blk * NT], reads=['xs'], writes=['xh'])
                else:
                    em.op('pool', lambda e: e.memset(xh[:, :, 0:1], 1.0), writes=['xh'])
                if has_right:
                    em.dma('sp', xh[:, :, 1:2], xs_scr[:, :, (blk + 1) * NT:(blk + 1) * NT + 1], reads=['xs'], writes=['xh'])
                else:
                    em.op('pool', lambda e: e.memset(xh[:, :, 1:2], 1.0), writes=['xh'])
                rmsnorm_mod(1, 1)
                rms_cols(1, 1, lambda kc: xh[:, kc, :], lambda kc: hTh[:, kc, :], 2, 'hTh', 'xh')
                ffn(l, 1, V_CONV, True, has_left, has_right, (ys_out if last else xs_scr2)[:, :, bs], 'xs2')
            em.barrier()
        except _Stop:
            pass

        with nc.allow_low_precision("bf16 matmul operands, fp32 accumulation"):
            with nc.allow_non_contiguous_dma("halo columns / small strided loads"):
                em.run()
    return nc


_PROGRAM = {}


def _get_program():
    if 'p' not in _PROGRAM:
        _PROGRAM['p'] = build_program(L, True)
    return _PROGRAM['p']


def _na_bias_tables(rpb):
    NEG = np.float32(-1e30)
    cq = np.arange(64)
    win0 = np.clip(cq - 8, 0, 48)
    ck = np.arange(64)
    in_win = (ck[:, None] >= win0[None, :]) & (ck[:, None] < win0[None, :] + 16)
    dc = np.clip(ck[:, None] - cq[None, :] + 15, 0, 30)
    out = np.full((5, 128, 5, 4, 128), NEG, np.float32)
    for vi, r in enumerate([30, 0, 2, 60, 62]):
        t0 = min(max((r - 4) // 2, 0), 27)
        for j in range(5):
            for a in range(2):
                kr = 2 * (t0 + j) + a
                for bq in range(2):
                    rq = r + bq
                    k0 = min(max(rq - 4, 0), 56)
                    if not (0 <= kr - k0 < 8):
                        continue
                    dr = kr - rq + 7
                    for h in range(4):
                        blk = np.where(in_win, rpb[h, dr][dc], NEG)
                        out[vi, a * 64:(a + 1) * 64, j, h, bq * 64:(bq + 1) * 64] = blk
    return out.reshape(5, 128, 5 * 512)


def _host_layout(inp):
    f = lambda a: np.ascontiguousarray(np.asarray(a, dtype=np.float32))
    w_in = f(inp['w_in'])
    offs = np.cumsum([0, 256, 256, 256, 512, 128, 128, 256, 256, 256, 256, 16, 16, 1024, 1024, 1024])
    (o_qa, o_ka, o_va, o_qb, o_kb, o_vb, o_qc, o_kc, o_vc, o_rc, o_zf, o_zb, o_ga, o_gb, o_gc, _) = offs
    rot = (np.arange(64) + 32) % 64
    qb_cols = np.concatenate([o_qb + h * 64 + np.arange(64) for h in QB_PERM])
    qbr_cols = np.concatenate([o_qb + h * 64 + rot for h in QB_PERM])
    kbr_cols = np.concatenate([o_kb + h * 64 + rot for h in range(2)])
    rng = lambda a, n: np.arange(a, a + n)
    fm_cols = np.concatenate([rng(o_qa, 256), rng(o_ka, 256), qb_cols, qbr_cols, rng(o_kb, 128), kbr_cols,
                              rng(o_qc, 256), rng(o_kc, 256), rng(o_rc, 256), rng(o_zf, 32)])
    assert len(fm_cols) == W_FM
    tm_cols = np.concatenate([rng(o_va, 256), rng(o_vb, 128), rng(o_vc, 256), rng(o_kc, 256)])
    shared = {
        'w_mod': f(inp['w_mod']),
        'w_fm': np.ascontiguousarray(w_in[:, :, fm_cols]),
        'w_tm': np.ascontiguousarray(w_in[:, :, tm_cols]),
        'w_out': f(inp['w_out']),
    }
    wg_ = w_in[:, :, o_ga:o_ga + 3072].reshape(L, 8, 128, 3, 8, 128)
    shared['w_g'] = np.ascontiguousarray(wg_.transpose(0, 4, 2, 1, 3, 5).reshape(L, 8, 128, 8, 384))
    wbb_ = f(inp['w_branch_b']).reshape(L, 8, 64, D)[:, QB_PERM].reshape(L, 512, D)
    wbr_ = np.concatenate([f(inp['w_branch_a']), wbb_, f(inp['w_branch_c'])], axis=1)
    shared['w_br'] = np.ascontiguousarray(wbr_.reshape(L, 8, 128, 8, 128).transpose(0, 3, 2, 1, 4))
    wup_ = f(inp['ffn_w_up']).reshape(L, 8, 128, 2, 11, 256)
    shared['w_up'] = np.ascontiguousarray(wup_.transpose(0, 3, 4, 2, 1, 5))
    wdn_ = f(inp['ffn_w_down']).reshape(L, NFF, 128, 8, 128)
    shared['w_dn'] = np.ascontiguousarray(wdn_.transpose(0, 3, 2, 1, 4))
    wg2 = f(inp['gla_wg2'])
    bg = f(inp['gla_bg'])
    wz2 = np.zeros((L, 33, 512), np.float32)
    wz2[:, 0:16, 0:256] = wg2[:, 0]
    wz2[:, 16:32, 256:512] = wg2[:, 1]
    wz2[:, 32, 0:256] = bg[:, 0]
    wz2[:, 32, 256:512] = bg[:, 1]
    shared['w_z2'] = wz2
    vecs = np.zeros((128, NV), np.float32)
    col128 = lambda v: v.reshape(-1, 128).T
    rep64 = lambda v: np.concatenate([v, v])
    for l in range(L):
        b = 64 + l * PER_L
        vecs[:, b:b + 48] = col128(f(inp['b_mod'])[l])
        vecs[:, b + 48:b + 56] = col128(f(inp['g_attn'])[l])
        vecs[:, b + 56:b + 64] = col128(f(inp['g_ffn'])[l])
        vecs[:, b + 64] = rep64(f(inp['na_q_norm'])[l])
        vecs[:, b + 65] = rep64(f(inp['na_k_norm'])[l])
        vecs[:, b + 66] = rep64(f(inp['gqa_q_norm'])[l])
        vecs[:, b + 67] = rep64(f(inp['gqa_k_norm'])[l])
        vecs[:, b + 68] = rep64(f(inp['gla_out_norm'])[l])
        vecs[:, b + 69] = rep64(f(inp['gqa_q_norm'])[l][rot])
        vecs[:, b + 70] = rep64(f(inp['gqa_k_norm'])[l][rot])
        cw = f(inp['ffn_conv_w'])[l]
        cbias = f(inp['ffn_conv_b'])[l]
        for part in range(2):
            cb0 = b + 71 + part * 4 * NFF
            sl = slice(part * DFF, (part + 1) * DFF)
            for k in range(3):
                vecs[:, cb0 + k * NFF:cb0 + (k + 1) * NFF] = col128(cw[k, sl])
            vecs[:, cb0 + 3 * NFF:cb0 + 4 * NFF] = col128(cbias[sl])
    shared['vecs'] = vecs
    s_idx = np.arange(128)[:, None]
    t_idx = np.arange(128)[None, :]
    blk = ((s_idx // 64) == (t_idx // 64))
    mf = (s_idx <= t_idx)
    mb = (s_idx >= t_idx)
    consts = np.concatenate([mf, mb, (s_idx > t_idx), (s_idx < t_idx), blk, np.ones((128, 128), bool), (s_idx == t_idx),
                             mf, mf, mf, mf, mb, mb, mb, mb], axis=1).astype(np.float32)
    assert consts.shape[1] == NCONST
    shared['consts'] = consts
    t = np.arange(NTS)
    n_freq = 16
    inv_freq = 10000.0 ** (-np.arange(n_freq) / n_freq)
    ang = np.concatenate([(t // 64)[:, None] * inv_freq, (t % 64)[:, None] * inv_freq], axis=-1)
    cosT = np.cos(ang).astype(np.float32).T
    sinT = np.sin(ang).astype(np.float32).T
    c64 = np.concatenate([cosT, cosT], 0)
    s64 = np.concatenate([-sinT, sinT], 0)
    shared['ropeC'] = np.ascontiguousarray(np.concatenate([c64, c64], 0))
    shared['ropeS'] = np.ascontiguousarray(np.concatenate([s64, s64], 0))
    rpb = f(inp['na_rpb'])
    shared['nab'] = np.stack([_na_bias_tables(rpb[l]) for l in range(L)], 0)
    return shared


def kernel(**inputs):
    shared = _host_layout(inputs)
    f = lambda a: np.asarray(a, dtype=np.float32)
    xp = f(inputs['x_prompt'])
    xsm = f(inputs['x_sample'])
    c_ctx = f(inputs['c_ctx'])
    cc = f(inputs['c'])
    per_b = []
    for b in range(2):
        d = {}
        d['xsT0'] = np.ascontiguousarray(xsm[b].T.reshape(8, 128, NTS).transpose(1, 0, 2))
        cond = np.stack([c_ctx.reshape(8, 128).T, cc[b].reshape(8, 128).T], axis=-1)
        d['cond'] = np.ascontiguousarray(cond.reshape(128, 16))
        nk = f(inputs['cache_na_k'])[b]
        d['cnak'] = np.ascontiguousarray(nk.reshape(L, 512, 2, 128).transpose(0, 3, 2, 1))
        d['cnav'] = np.ascontiguousarray(f(inputs['cache_na_v'])[b].reshape(L, 512, 256))
        d['cgk'] = np.ascontiguousarray(f(inputs['cache_gqa_k'])[b].reshape(L, 512, 128).transpose(0, 2, 1))
        d['cgv'] = np.ascontiguousarray(f(inputs['cache_gqa_v'])[b].reshape(L, 512, 128))
        for nm, key in (('sgf', 'state_gla_fwd'), ('sgb', 'state_gla_bwd')):
            st = f(inputs[key])[b]
            d[nm] = np.ascontiguousarray(st.reshape(L, 2, 2, 64, 64).transpose(0, 2, 3, 1, 4).reshape(L, 128, 2, 64))
        per_b.append(d)
    in_maps = []
    for c in range(NCORES):
        xs = xp[4 * c:4 * c + 4].reshape(NT, D)
        m = dict(shared)
        m.update(per_b[c // 4])
        m['xT0'] = np.ascontiguousarray(xs.T.reshape(8, 128, NT).transpose(1, 0, 2))
        in_maps.append(m)
    nc = _get_program()
    res = run_bass_kernel_spmd(nc, in_maps, core_ids=list(range(NCORES)))
    R = res.results
    B, S = 32, SEQ
    y_prompt = np.zeros((B, S, D), np.float32)
    y_sample = np.zeros((2, NTS, D), np.float32)
    na_k = np.zeros((B, L, S, 4, 64), np.float32)
    na_v = np.zeros((B, L, S, 4, 64), np.float32)
    gq_k = np.zeros((B, L, S, 2, 64), np.float32)
    gq_v = np.zeros((B, L, S, 2, 64), np.float32)
    s_f = np.zeros((B, L, 4, 64, 64), np.float32)
    s_b = np.zeros((B, L, 4, 64, 64), np.float32)
    for c in range(NCORES):
        r = R[c]
        yT = np.asarray(r['yT'])
        y_prompt[4 * c:4 * c + 4] = yT.transpose(2, 1, 0).reshape(4, S, D)
        if c % 4 == 0:
            y_sample[c // 4] = np.asarray(r['ysT']).transpose(2, 1, 0).reshape(NTS, D)
        nak = np.asarray(r['o_nak'])
        na_k[4 * c:4 * c + 4] = nak.reshape(L, 4, 64, 4, S).transpose(3, 0, 4, 1, 2)
        nav = np.asarray(r['o_nav'])
        na_v[4 * c:4 * c + 4] = nav.reshape(L, 4, S, 4, 64).transpose(1, 0, 2, 3, 4)
        gk = np.asarray(r['o_gk'])
        gq_k[4 * c:4 * c + 4] = gk.reshape(L, 2, 64, 4, S).transpose(3, 0, 4, 1, 2)
        gv = np.asarray(r['o_gv'])
        gq_v[4 * c:4 * c + 4] = gv.reshape(L, 4, S, 2, 64).transpose(1, 0, 2, 3, 4)
        sf = np.asarray(r['o_sf'])
        s_f[4 * c:4 * c + 4] = sf.reshape(L, 4, 4, 64, 64).transpose(1, 0, 2, 3, 4)
        sb = np.asarray(r['o_sb'])
        s_b[4 * c:4 * c + 4] = sb.reshape(L, 4, 4, 64, 64).transpose(1, 0, 2, 3, 4)
    return (y_prompt, y_sample, na_k, na_v, gq_k, gq_v, s_f, s_b)
```

```python
from contextlib import ExitStack
import numpy as np
import concourse.bass as bass
import concourse.mybir as mybir
from concourse.bass_utils import run_bass_kernel_spmd

F32 = mybir.dt.float32
BF16 = mybir.dt.bfloat16
AF = mybir.ActivationFunctionType
ALU = mybir.AluOpType

NDMA = 24
L = 4
D = 1024
NT = 1024
SEQ = 256
DFF = 2816
NFF = 22
EPS = 1e-6
NCORES = 8


class _Rec:
    def __init__(self):
        self.calls = []

    def __getattr__(self, name):
        def f(*a, **k):
            self.calls.append((name, a, k))
        return f


class Emit:
    def __init__(self, nc, es):
        self.nc = nc
        self.engs = {'pe': nc.tensor, 'act': nc.scalar, 'dve': nc.vector, 'pool': nc.gpsimd, 'sp': nc.sync}
        self.thunks = {e: [] for e in self.engs}
        self.seq = {e: 0 for e in self.engs}
        self.sem = {e: es.enter_context(nc.semaphore("s_" + e)) for e in self.engs}
        self.dsem = [es.enter_context(nc.semaphore("d%d" % i)) for i in range(NDMA)]
        self.dcnt = [0] * NDMA
        self.dnext2 = [0, 0]
        self.waited = {}
        self.last_w = {}
        self.readers = {}

    def _deps(self, reads, writes):
        deps = {}

        def add(p):
            if p is None:
                return
            prod, val = p
            if deps.get(prod, 0) < val:
                deps[prod] = val

        for k in reads:
            add(self.last_w.get(k))
        for k in writes:
            add(self.last_w.get(k))
            for r in self.readers.get(k, ()):
                add(r)
        return deps

    def _emit_waits(self, e, deps):
        for prod, val in deps.items():
            if prod == e and e == 'pe':
                continue
            if self.waited.get((e, prod), 0) >= val:
                continue
            self.waited[(e, prod)] = val
            sem = self.sem[prod] if isinstance(prod, str) else self.dsem[prod]
            self.thunks[e].append(lambda eng, sem=sem, val=val: eng.wait_ge(sem, val))

    def _record(self, me, reads, writes):
        for k in writes:
            self.last_w[k] = me
            self.readers[k] = []
        for k in reads:
            self.readers.setdefault(k, []).append(me)

    def op(self, e, fns, reads=(), writes=()):
        if not isinstance(fns, (list, tuple)):
            fns = [fns]
        deps = self._deps(reads, writes)
        self._emit_waits(e, deps)
        self.seq[e] += 1
        val = self.seq[e]
        sem = self.sem[e]
        n = len(fns)
        for i, fn in enumerate(fns):
            rec = _Rec()
            fn(rec)
            (name, a, k), = rec.calls
            if i == n - 1:
                self.thunks[e].append(lambda eng, name=name, a=a, k=k, sem=sem: getattr(eng, name)(*a, **k).then_inc(sem, 1))
            else:
                self.thunks[e].append(lambda eng, name=name, a=a, k=k: getattr(eng, name)(*a, **k))
        self._record((e, val), reads, writes)

    def dma(self, q, out, in_, reads=(), writes=(), **kw):
        half = NDMA // 2
        qi = 0 if q == 'sp' else 1
        d = qi * half + self.dnext2[qi]
        self.dnext2[qi] = (self.dnext2[qi] + 1) % half
        deps = self._deps(reads, writes)
        if self.dcnt[d] > 0 and deps.get(d, 0) < self.dcnt[d]:
            deps[d] = self.dcnt[d]
        self._emit_waits(q, deps)
        self.dcnt[d] += 16
        val = self.dcnt[d]
        sem = self.dsem[d]
        self.thunks[q].append(
            lambda eng, out=out, in_=in_, sem=sem, kw=kw: eng.dma_start(out=out, in_=in_, **kw).then_inc(sem, 16))
        self._record((d, val), reads, writes)

    def barrier(self, soft=False):
        if soft:
            for e in ('pe', 'act', 'dve'):
                deps = {}
                for p in ('pe', 'act', 'dve', 'pool'):
                    if self.seq[p] > 0 and not (p == e and e == 'pe'):
                        deps[p] = self.seq[p]
                self._emit_waits(e, deps)
            return
        for e in self.engs:
            deps = {}
            for p in self.engs:
                if p != e and self.seq[p] > 0:
                    deps[p] = self.seq[p]
            if e != 'pe' and self.seq[e] > 0:
                deps[e] = self.seq[e]
            for d in range(NDMA):
                if self.dcnt[d] > 0:
                    deps[d] = self.dcnt[d]
            self._emit_waits(e, deps)

    def run(self):
        for d in range(NDMA):
            if self.dcnt[d] > 0:
                self.thunks['sp'].append(lambda eng, sem=self.dsem[d], val=self.dcnt[d]: eng.wait_ge(sem, val))
        with self.nc.Block() as block:
            @block.tensor
            def _(eng):
                for t in self.thunks['pe']:
                    t(eng)

            @block.scalar
            def _(eng):
                for t in self.thunks['act']:
                    t(eng)

            @block.vector
            def _(eng):
                for t in self.thunks['dve']:
                    t(eng)

            @block.gpsimd
            def _(eng):
                for t in self.thunks['pool']:
                    t(eng)

            @block.sync
            def _(eng):
                for t in self.thunks['sp']:
                    t(eng)


C_QA, C_KA, C_QB, C_QBR, C_KB, C_KBR, C_QC, C_KC, C_RC, C_Z = 0, 256, 512, 1024, 1536, 1664, 1792, 2048, 2304, 2560
W_FM = 2592
T_VA, T_VB, T_VC, T_KC = 0, 256, 384, 640
W_TM = 896
QB_PERM = [0, 4, 1, 5, 2, 6, 3, 7]
NTS = 4096
NBLK = 4
PER_L = 48 + 8 + 8 + 7 + 4 * 2 * NFF
NV = 64 + L * PER_L
NCONST = 7 * 128 + 2 * 512
ARENA = 28672


class _Stop(Exception):
    pass


def build_program(n_layers=L, do_sample=True, stop=99):
    nc = bass.Bass("TRN2", target_bir_lowering=False)
    dt_in = lambda name, shape: nc.dram_tensor(name, shape, F32, kind="ExternalInput").ap()
    dt_out = lambda name, shape: nc.dram_tensor(name, shape, F32, kind="ExternalOutput").ap()
    x_in = dt_in("xT0", [128, 8, NT])
    xs_in = dt_in("xsT0", [128, 8, NTS])
    cond_in = dt_in("cond", [128, 16])
    w_mod = dt_in("w_mod", [L, 12, 128, 8, 512])
    w_fm = dt_in("w_fm", [L, 5, 128, 8, 512])
    w_fz = dt_in("w_fz", [L, 128, 8, 32])
    w_tm0 = dt_in("w_tm0", [L, 128, 8, 512])
    w_tm1 = dt_in("w_tm1", [L, 128, 8, 384])
    w_g = dt_in("w_g", [L, 8, 128, 8, 384])
    w_br = dt_in("w_br", [L, 8, 128, 8, 128])
    w_out = dt_in("w_out", [L, 2, 128, 8, 512])
    w_up = dt_in("w_up", [L, 2, 11, 128, 8, 256])
    w_dn = dt_in("w_dn", [L, 8, 128, NFF, 128])
    w_z2 = dt_in("w_z2", [L, 33, 512])
    vecs_in = dt_in("vecs", [128, NV])
    consts_in = dt_in("consts", [128, NCONST])
    ropeC_in = dt_in("ropeC", [128, NTS])
    ropeS_in = dt_in("ropeS", [128, NTS])
    nab_in = dt_in("nab", [L, 5, 128, 5 * 512])
    cnak_in = dt_in("cnak", [L, 128, 2, 512])
    cnav_in = dt_in("cnav", [L, 512, 256])
    cgk_in = dt_in("cgk", [L, 128, 512])
    cgv_in = dt_in("cgv", [L, 512, 128])
    sgf_in = dt_in("sgf", [L, 128, 2, 64])
    sgb_in = dt_in("sgb", [L, 128, 2, 64])

    y_out = dt_out("yT", [128, 8, NT])
    ys_out = dt_out("ysT", [128, 8, NTS])
    o_nak = dt_out("o_nak", [L, 256, NT])
    o_nav = dt_out("o_nav", [L, NT, 256])
    o_gk = dt_out("o_gk", [L, 128, NT])
    o_gv = dt_out("o_gv", [L, NT, 128])
    o_sf = dt_out("o_sf", [L, 4, 2, 128, 64])
    o_sb = dt_out("o_sb", [L, 4, 2, 128, 64])

    scr = lambda name, shape, dt: nc.dram_tensor(name, shape, dt).ap()
    xp_scr = scr("xp_scr", [128, 8, NT], F32)
    xs_scr = scr("xs_scr", [128, 8, NTS], F32)
    xs_scr2 = scr("xs_scr2", [128, 8, NTS], F32)
    sav_arena = scr("sav_arena", [NBLK, 128, ARENA], BF16)
    sav_hT = scr("sav_hT", [NBLK, 128, 8 * NT], BF16)
    kaT_scr = scr("kaT_scr", [128, 2, NTS], BF16)
    va_scr = scr("va_scr", [NTS, 256], BF16)
    sprev_scr = scr("sprev_scr", [32, 128, 128], BF16)

    with ExitStack() as es:
        T = lambda name, shape, dt: es.enter_context(nc.sbuf_tensor("sb_" + name, shape, dt))
        xreg = T("xreg", [128, 16384], BF16)
        xT = xreg[:].bitcast(F32).rearrange("p (c t) -> p c t", c=8)
        Kslab = xreg[:, 0:4096].rearrange("p (c t) -> p c t", c=2)
        Vslab = xreg[:, 4096:8192].rearrange("p (i c) -> p i c", i=16)
        nabI = xreg[:, 8192:10752]
        nabE = xreg[:, 10752:13312]
        hT = T("hT", [128, 8, NT], BF16)
        hTh = T("hTh", [128, 8, 2], BF16)
        xh = T("xh", [128, 8, 2], F32)
        wbuf = [T("wbuf%d" % i, [128, 8, 512], BF16) for i in range(2)]
        wdn2 = [T("wdn%d" % i, [128, NFF, 128], BF16) for i in range(2)]
        brw2 = [T("brw%d" % i, [128, 8, 128], BF16) for i in range(2)]
        arena = T("arena", [128, ARENA], BF16)
        kbT_all = T("kbT_all", [128, NTS + 512], BF16)
        vaug = T("vaug", [128, 36, 2, 128], BF16)
        vecs = T("vecs", [128, NV], F32)
        cb = T("cb", [128, NCONST], BF16)
        modT = T("modT", [128, 2, 48], F32)
        scA = T("scA", [128, 2, 16], F32)
        condf = T("condf", [128, 16], F32)
        condb = T("condb", [128, 16], BF16)
        zT = T("zT", [33, NT], BF16)
        wz2 = T("wz2", [33, 512], BF16)
        tf = [T("tf%d" % i, [128, 512], F32) for i in range(6)]
        tb = [T("tb%d" % i, [128, 512], BF16) for i in range(4)]
        trC = [T("trC%d" % i, [128, 512], F32) for i in range(2)]
        trS = [T("trS%d" % i, [128, 512], F32) for i in range(2)]
        S32 = [T("S32_%d" % i, [128, 2, 64], F32) for i in range(2)]
        qtil = [[T("qtil%d%d" % (d, c), [128, 256], BF16) for c in range(2)] for d in range(2)]
        attm = [[T("attm%d%d" % (d, c), [128, 512], BF16) for c in range(2)] for d in range(2)]
        Sprev = [[T("Sprev%d%d" % (d, c), [128, 2, 64], BF16) for c in range(2)] for d in range(2)]
        dec = [T("dec%d" % d, [128, 2], F32) for d in range(2)]
        onec = T("onec", [128, 1], F32)
        gsc = T("gsc", [128, 4], F32)
        qpad = [T("qpad%d" % i, [128, 512], BF16) for i in range(2)]
        epsc = T("epsc", [128, 1], F32)
        PS = [es.enter_context(nc.psum_tensor("ps%d" % i, [128, 512], F32)) for i in range(8)]

        off = [0]

        def carve(n, shape_str=None, **kw):
            a = arena[:, off[0]:off[0] + n]
            off[0] += n
            return a.rearrange(shape_str, **kw) if shape_str else a

        qaT = carve(2 * NT, "p (c t) -> p c t", c=2)
        qbT = carve(4 * NT, "p (c t) -> p c t", c=4)
        qcT = carve(2 * NT, "p (c t) -> p c t", c=2)
        mT = arena[:, 0:8 * NT].rearrange("p (c t) -> p c t", c=8)
        kcT = carve(2 * NT, "p (c t) -> p c t", c=2)
        rcT = carve(2 * NT, "p (c t) -> p c t", c=2)
        vtok = carve(8 * 256, "p (i c) -> p i c", i=8)
        ktok = carve(8 * 256, "p (i c) -> p i c", i=8)
        latok = carve(8 * 512, "p (i c) -> p i c", i=8)
        oaT = carve(2 * NT, "p (c t) -> p c t", c=2)
        obT = carve(4 * NT, "p (c t) -> p c t", c=4)
        ocT = carve(2 * NT, "p (c t) -> p c t", c=2)
        assert off[0] <= ARENA, off[0]
        actT = arena[:, 0:NFF * NT].rearrange("p (c t) -> p c t", c=NFF)

        em = Emit(nc, es)
        ps_rr = [0]

        def phase(k):
            if stop == k:
                raise _Stop()

        em.dma('sp', vecs[:], vecs_in, writes=['vecs'])
        em.dma('pool', cb[:], consts_in, writes=['cb'])
        em.op('pool', lambda e: e.memset(onec[:], 1.0), writes=['onec'])
        em.op('pool', lambda e: e.memset(epsc[:], EPS), writes=['epsc'])
        em.op('pool', lambda e: e.memset(zT[:], 1.0), writes=['zT'])
        em.op('pool', lambda e: e.memset(vaug[:], 1.0), writes=['vb_all'])
        for i_ in range(2):
            em.op('pool', lambda e: e.memset(qpad[i_][:], 0.0), writes=['qpad%d' % i_])
        TRI_F, TRI_B, TRIX_F, TRIX_B, BLK, ONES, IDENT = [cb[:, i * 128:(i + 1) * 128] for i in range(7)]
        MASK4 = [cb[:, 896:1408], cb[:, 1408:1920]]
        em.dma('sp', condf[:], cond_in, writes=['condf'])
        em.op('act', lambda e: e.activation(condb[:], condf[:], AF.Silu), reads=['condf'], writes=['condb'])
        condb3 = condb[:].rearrange("p (k c) -> p k c", c=2)

        def vcol(base, j):
            return vecs[:, base + j:base + j + 1]

        def load_w(buf, key, src, ncols, nk=8):
            em.dma('pool', buf[:, 0:nk, 0:ncols], src, writes=[key])

        wpar = [0]

        def next_wbuf():
            i = wpar[0]
            wpar[0] ^= 1
            return wbuf[i], 'wbuf%d' % i

        def next_ps():
            i = (0, 1, 4, 5)[ps_rr[0]]
            ps_rr[0] = (ps_rr[0] + 1) % 4
            return PS[i], 'ps%d' % i

        def rms_cols(which, ci, xsrc, hdst, w, hkey, xkey):
            sh_base = (0 if which == 0 else 24)
            for kc in range(8):
                em.op('act', lambda e, kc=kc: e.activation(tb[0][:, 0:w], xsrc(kc), AF.Square),
                      reads=[xkey], writes=['tb0'])
                em.op('pe', lambda e, kc=kc: e.matmul(PS[2][:, 0:w], ONES, tb[0][:, 0:w], start=(kc == 0), stop=(kc == 7)),
                      reads=['tb0', 'cb'], writes=['ps2'])
            em.op('act', lambda e: e.activation(tf[0][:, 0:w], PS[2][:, 0:w], AF.Sqrt, bias=epsc[:, 0:1], scale=1.0 / D),
                  reads=['epsc'], writes=['ps2', 'tf0'])
            em.op('dve', lambda e: e.reciprocal(tf[1][:, 0:w], tf[0][:, 0:w]), reads=['tf0'], writes=['tf1'])
            for kc in range(8):
                em.op('dve', lambda e, kc=kc: e.tensor_tensor(tf[2][:, 0:w], xsrc(kc), tf[1][:, 0:w], ALU.mult),
                      reads=[xkey, 'tf1'], writes=['tf2'])
                em.op('act', lambda e, kc=kc: e.activation(hdst(kc), tf[2][:, 0:w], AF.Identity,
                                                           bias=modT[:, ci, sh_base + kc:sh_base + kc + 1],
                                                           scale=scA[:, ci, which * 8 + kc:which * 8 + kc + 1]),
                      reads=['tf2', 'modT', 'scA'], writes=[hkey])

        def rmsnorm_mod(which, ci):
            for th in range(2):
                ts = slice(th * 512, th * 512 + 512)
                rms_cols(which, ci, lambda kc, ts=ts: xT[:, kc, ts], lambda kc, ts=ts: hT[:, kc, ts], 512, ('hT', th), 'xT')

        def proj_fm(wb, wkey, c0, n, th, ps, pskey, rhsT=None, rkey='hT', nk=8):
            r = hT if rhsT is None else rhsT
            ts = slice(th * 512, th * 512 + 512)
            em.op('pe', [lambda e, kc=kc: e.matmul(ps[0:n, :], wb[:, kc, c0:c0 + n], r[:, kc, ts],
                                                  start=(kc == 0), stop=(kc == nk - 1)) for kc in range(nk)],
                  reads=[wkey, (rkey, th)], writes=[pskey])

        hn_par = [0]

        def head_norm_core(ps, pskey, gcol, dst=None, dkey='tf4'):
            dst = tf[4][:] if dst is None else dst
            sq, sqk, t0, t0k, t1, t1k, pb = tb[1], 'tb1', tf[0], 'tf0', tf[1], 'tf1', 2
            pbk = 'ps%d' % pb
            em.op('act', lambda e: e.activation(sq[:], ps[:], AF.Square), writes=[pskey, sqk])
            em.op('pe', lambda e: e.matmul(PS[pb][:], BLK, sq[:], start=True, stop=True),
                  reads=[sqk, 'cb'], writes=[pbk])
            em.op('act', lambda e: e.activation(t0[:], PS[pb][:], AF.Ln, bias=epsc[:, 0:1], scale=1.0 / 64),
                  reads=['epsc'], writes=[pbk, t0k])
            em.op('act', lambda e: e.activation(t1[:], t0[:], AF.Exp, scale=-0.5), reads=[t0k], writes=[t1k])
            em.op('dve', lambda e: e.scalar_tensor_tensor(dst, ps[:], gcol, t1[:], ALU.mult, ALU.mult),
                  reads=[t1k, 'vecs', 'gsc'], writes=[pskey, dkey])

        def mod_phase(l, V_BMOD, V_GATT, V_GFFN):
            for g in range(12):
                wb, wkey = next_wbuf()
                load_w(wb, wkey, w_mod[l][g], 512)
                for j in range(4):
                    col = g * 4 + j
                    em.op('pe', [lambda e, kc=kc, j=j, col=col, wb=wb: e.matmul(
                        PS[3][:, 2 * col:2 * col + 2], wb[:, kc, j * 128:(j + 1) * 128], condb3[:, kc, :],
                        start=(kc == 0), stop=(kc == 7)) for kc in range(8)],
                        reads=[wkey, 'condb'], writes=['ps3'])
            ps3v = PS[3][:, 0:96].rearrange("p (j c) -> p j c", c=2)
            for ci in range(2):
                em.op('dve', lambda e, ci=ci: e.tensor_tensor(modT[:, ci, :], ps3v[:, :, ci], vecs[:, V_BMOD:V_BMOD + 48], ALU.add),
                      reads=['vecs'], writes=['ps3', 'modT'])
                em.op('dve', lambda e, ci=ci: e.scalar_tensor_tensor(scA[:, ci, 0:8], modT[:, ci, 8:16], 1.0,
                                                                    vecs[:, V_GATT:V_GATT + 8], ALU.add, ALU.mult),
                      reads=['modT', 'vecs'], writes=['scA'])
                em.op('dve', lambda e, ci=ci: e.scalar_tensor_tensor(scA[:, ci, 8:16], modT[:, ci, 32:40], 1.0,
                                                                    vecs[:, V_GFFN:V_GFFN + 8], ALU.add, ALU.mult),
                      reads=['modT', 'vecs'], writes=['scA'])

        def inproj(l, blk, V_NRM):
            smp = blk is not None
            t0g = 0 if not smp else blk * NT
            for k_, src_ in enumerate((0, 2, 5)):
                em.op('pool', lambda e: e.tensor_scalar_mul(gsc[:, k_:k_ + 1], vcol(V_NRM, src_), 0.125), reads=['vecs'],
                      writes=['gsc'])
            if smp:
                for th in range(2):
                    em.dma('sp', trC[th][:], ropeC_in[:, t0g + th * 512:t0g + (th + 1) * 512], writes=['trC%d' % th])
                    em.dma('sp', trS[th][:], ropeS_in[:, t0g + th * 512:t0g + (th + 1) * 512], writes=['trS%d' % th])

            def sink_bf(out_bf, out_key, scale):
                em.op('act', lambda e: e.activation(out_bf, tf[4][:], AF.Identity, scale=scale), reads=['tf4'],
                      writes=[out_key])

            def chunk(wb, wkey, cc, n, col, wb2=None, wkey2=None):
                for th in range(2):
                    ts = slice(th * 512, th * 512 + 512)
                    gts = slice(t0g + th * 512, t0g + th * 512 + 512)
                    ps, pskey = next_ps()
                    proj_fm(wb, wkey, cc * 128, n, th, ps, pskey)
                    if col < C_KA:
                        c = (col - C_QA) // 128
                        head_norm_core(ps, pskey, gsc[:, 0:1], qaT[:, c, ts], ('qaT', th))
                    elif col < C_QB:
                        c = (col - C_KA) // 128
                        if not smp:
                            head_norm_core(ps, pskey, vcol(V_NRM, 1))
                            em.dma('sp', o_nak[l][c * 128:(c + 1) * 128, ts], tf[4][:], reads=['tf4'])
                            em.op('act', lambda e: e.copy(Kslab[:, c, ts], tf[4][:]), reads=['tf4'], writes=['Kslab'])
                        else:
                            head_norm_core(ps, pskey, vcol(V_NRM, 1), tb[2][:], 'tb2')
                            em.dma('sp', kaT_scr[:, c, gts], tb[2][:], reads=['tb2'], writes=['kaT_scr'])
                    elif col < C_QBR or (C_KB <= col < C_KBR):
                        isq = col < C_QBR
                        c = (col - C_QB) // 128 if isq else 0
                        if isq:
                            dst, dkey = qbT[:, c, ts], ('qbT', th)
                            g0_, g1_ = gsc[:, 1:2], gsc[:, 2:3]
                        else:
                            dst, dkey = kbT_all[:, gts], 'kbT_all'
                            g0_, g1_ = vcol(V_NRM, 3), vcol(V_NRM, 6)
                        if not smp:
                            if isq:
                                head_norm_core(ps, pskey, g0_, dst, dkey)
                            else:
                                head_norm_core(ps, pskey, g0_)
                                em.dma('sp', o_gk[l][:, ts], tf[4][:], reads=['tf4'])
                                em.op('act', lambda e: e.copy(dst, tf[4][:]), reads=['tf4'], writes=[dkey])
                        else:
                            head_norm_core(ps, pskey, g0_)
                            em.op('dve', lambda e: e.tensor_tensor(tf[5][:], tf[4][:], trC[th][:], ALU.mult),
                                  reads=['tf4', 'trC%d' % th], writes=['tf5'])
                            ps2_, ps2key = next_ps()
                            rc0 = (cc * 128) if isq else (cc + 1) * 128
                            wbr, wkr = (wb2, wkey2) if isq else (wb, wkey)
                            proj_fm(wbr, wkr, rc0, 128, th, ps2_, ps2key)
                            head_norm_core(ps2_, ps2key, g1_)
                            em.op('dve', lambda e: e.tensor_tensor(tf[4][:], tf[4][:], trS[th][:], ALU.mult),
                                  reads=['trS%d' % th], writes=['tf4'])
                            em.op('dve', lambda e: e.tensor_tensor(dst, tf[4][:], tf[5][:], ALU.add),
                                  reads=['tf4', 'tf5'], writes=[dkey])
                    elif col < C_KC:
                        c = (col - C_QC) // 128
                        em.op('act', lambda e, ps=ps, c=c: e.activation(qcT[:, c, ts], ps[:], AF.Identity, scale=0.125),
                              writes=[pskey, ('qcT', th)])
                    elif col < C_RC:
                        c = (col - C_KC) // 128
                        em.op('act', lambda e, ps=ps, c=c: e.copy(kcT[:, c, ts], ps[:]), writes=[pskey, ('kcT', th)])
                    elif col < C_Z:
                        c = (col - C_RC) // 128
                        em.op('act', lambda e, ps=ps, c=c: e.activation(rcT[:, c, ts], ps[:], AF.Silu),
                              writes=[pskey, ('rcT', th)])
                    else:
                        em.op('act', lambda e, ps=ps: e.copy(zT[0:32, ts], ps[0:32, :]), writes=[pskey, 'zT'])

            wb, wkey = next_wbuf()
            load_w(wb, wkey, w_fm[l][0], 512)
            for cc in range(4):
                chunk(wb, wkey, cc, 128, cc * 128)
            wb, wkey = next_wbuf()
            load_w(wb, wkey, w_fm[l][1], 512)
            wb2 = wkey2 = None
            if smp:
                wb2, wkey2 = next_wbuf()
                load_w(wb2, wkey2, w_fm[l][2], 512)
            for cc in range(4):
                chunk(wb, wkey, cc, 128, C_QB + cc * 128, wb2, wkey2)
            wb, wkey = next_wbuf()
            load_w(wb, wkey, w_fm[l][3], 512)
            chunk(wb, wkey, 0, 128, C_KB)
            chunk(wb, wkey, 2, 128, C_QC)
            chunk(wb, wkey, 3, 128, C_QC + 128)
            wb, wkey = next_wbuf()
            load_w(wb, wkey, w_fm[l][4], 512)
            for cc in range(4):
                chunk(wb, wkey, cc, 128, C_KC + cc * 128)
            wb, wkey = next_wbuf()
            load_w(wb, wkey, w_fz[l], 32)
            chunk(wb, wkey, 0, 32, C_Z)

            wtm = [None, None]
            for gi, (g0, gn) in enumerate([(0, 512), (512, 384)]):
                wb, wkey = next_wbuf()
                load_w(wb, wkey, (w_tm0 if gi == 0 else w_tm1)[l], gn)
                wtm[gi] = (wb, wkey)
            em.dma('pool', wz2[:], w_z2[l], writes=['wz2'])
            for i in range(8):
                tsl = slice(i * 128, (i + 1) * 128)
                gi_tile = (0 if not smp else blk * 8) + i
                th = i // 4
                for gi, (g0, gn) in enumerate([(0, 512), (512, 384)]):
                    wb, wkey = wtm[gi]
                    ps, pskey = next_ps()
                    em.op('pe', [lambda e, kc=kc, wb=wb, ps=ps, gn=gn: e.matmul(
                        ps[:, 0:gn], hT[:, kc, tsl], wb[:, kc, 0:gn], start=(kc == 0), stop=(kc == 7)) for kc in range(8)],
                        reads=[wkey, ('hT', th)], writes=[pskey])
                    if gi == 0:
                        if not smp:
                            em.op('act', lambda e, ps=ps: e.copy(tf[5][:, 0:512], ps[:, 0:512]), writes=[pskey, 'tf5'])
                            em.dma('sp', o_nav[l][tsl, :], tf[5][:, 0:256], reads=['tf5'])
                            em.dma('sp', o_gv[l][tsl, :], tf[5][:, 256:384], reads=['tf5'])
                            em.op('dve', lambda e: e.tensor_copy(Vslab[:, i, :], tf[5][:, 0:256]), reads=['tf5'], writes=['Vslab'])
                            em.op('dve', lambda e: e.tensor_copy(vaug[:, i, 0, 0:64], tf[5][:, 256:320]), reads=['tf5'], writes=['vb_all'])
                            em.op('dve', lambda e: e.tensor_copy(vaug[:, i, 1, 64:128], tf[5][:, 320:384]), reads=['tf5'], writes=['vb_all'])
                            em.op('dve', lambda e: e.tensor_copy(vtok[:, i, 0:128], tf[5][:, 384:512]), reads=['tf5'],
                                  writes=[('vtok', i)])
                        else:
                            em.op('act', lambda e, ps=ps: e.copy(tb[1][:], ps[:]), writes=[pskey, 'tb1'])
                            em.dma('sp', va_scr[gi_tile * 128:(gi_tile + 1) * 128, :], tb[1][:, 0:256], reads=['tb1'],
                                   writes=['va_scr'])
                            em.op('dve', lambda e: e.tensor_copy(vaug[:, gi_tile, 0, 0:64], tb[1][:, 256:320]), reads=['tb1'],
                                  writes=['vb_all'])
                            em.op('dve', lambda e: e.tensor_copy(vaug[:, gi_tile, 1, 64:128], tb[1][:, 320:384]), reads=['tb1'],
                                  writes=['vb_all'])
                            em.op('dve', lambda e: e.tensor_copy(vtok[:, i, 0:128], tb[1][:, 384:512]), reads=['tb1'],
                                  writes=[('vtok', i)])
                    else:
                        em.op('act', lambda e, ps=ps: e.copy(vtok[:, i, 128:256], ps[:, 0:128]), writes=[pskey, ('vtok', i)])
                        em.op('dve', lambda e, ps=ps: e.tensor_copy(ktok[:, i, :], ps[:, 128:384]), writes=[pskey, ('ktok', i)])
                ps, pskey = next_ps()
                em.op('pe', lambda e, ps=ps: e.matmul(ps[:], zT[:, tsl], wz2[:], start=True, stop=True),
                      reads=['zT', 'wz2'], writes=[pskey])
                em.op('act', lambda e, ps=ps: e.activation(tf[0][:], ps[:], AF.Exp, scale=-1.0), writes=[pskey, 'tf0'])
                em.op('act', lambda e: e.activation(tf[1][:], tf[0][:], AF.Ln, bias=onec[:, 0:1]), reads=['tf0', 'onec'],
                      writes=['tf1'])
                em.op('dve', lambda e: e.tensor_scalar_mul(latok[:, i, :], tf[1][:], -1.0 / 16.0), reads=['tf1'],
                      writes=[('latok', i)])

        TBK = ['tb0', 'tb1', 'tb2', 'tb3']

        def attn_generic(qT, qkey, nchunk, qsl, n, th, tiles_of, oT, okey, aug=False):
            per = 512 // n
            stages = []
            for c in range(nchunk):
                for hh in range(2):
                    tl = tiles_of(c, hh)
                    nt = len(tl)
                    for gi, g0 in enumerate(range(0, nt, per)):
                        stages.append((c, hh, gi, tl[g0:g0 + per], g0, nt))

            def acc_bank(c, hh):
                if not aug:
                    return None
                return ((6, 2) if hh == 0 else (7, 3))[c % 2]

            def slot_of(st):
                c, hh, gi, grp, g0, nt = st
                if aug:
                    k = st_index[id(st)] % 4
                    return (0, 1, 4, 5)[k], k
                return ((0, 1) if hh == 0 else (4, 5))[gi % 2], (0 if hh == 0 else 2) + gi % 2

            def emit_qk(st):
                c, hh, gi, grp, g0, nt = st
                bank, ti = slot_of(st)
                fns = []
                rk = [(qkey, th), 'cb']
                if aug and g0 == 0:
                    em.op('pool', lambda e: e.tensor_copy(qpad[hh][hh * 64:(hh + 1) * 64, 0:n], qT[hh * 64:(hh + 1) * 64, c, qsl]),
                          reads=[(qkey, th)], writes=['qpad%d' % hh])
                if aug:
                    rk.append('qpad%d' % hh)
                for j, (kap, kkeys, vap, vkeys, bias) in enumerate(grp):
                    o = PS[bank][:, j * n:(j + 1) * n]
                    if aug:
                        fns.append(lambda e, o=o, kap=kap: e.matmul(o, kap, qpad[hh][:, 0:n], start=True, stop=True))
                        rk += list(kkeys)
                        continue
                    fns.append(lambda e, o=o, kap=kap, bias=bias: e.matmul(
                        o, kap, qT[hh * 64:(hh + 1) * 64, c, qsl], start=True, stop=(bias is None)))
                    if bias is not None:
                        fns.append(lambda e, o=o, bias=bias: e.matmul(o, IDENT, bias, start=False, stop=True))
                    rk += list(kkeys)
                em.op('pe', fns, reads=rk, writes=['ps%d' % bank])
                w = len(grp) * n
                em.op('act', lambda e: e.activation(tb[ti][:, 0:w], PS[bank][:, 0:w], AF.Exp),
                      writes=['ps%d' % bank, TBK[ti]])

            def emit_pv(st):
                c, hh, gi, grp, g0, nt = st
                bank_, ti = slot_of(st)
                tbt = tb[ti]
                fns = []
                rk = [TBK[ti], 'cb']
                A = acc_bank(c, hh)
                for j, (kap, kkeys, vap, vkeys, bias) in enumerate(grp):
                    first = (g0 + j == 0)
                    last = (g0 + j == nt - 1)
                    if aug:
                        fns.append(lambda e, vap=vap, j=j, first=first, last=last: e.matmul(
                            PS[A][:, 0:n], vap, tbt[:, j * n:(j + 1) * n], start=first, stop=last))
                    else:
                        fns.append(lambda e, vap=vap, j=j, first=first, last=last: e.matmul(
                            PS[6][hh * 64:(hh + 1) * 64, 0:n], vap, tbt[:, j * n:(j + 1) * n], start=first, stop=last))
                        fns.append(lambda e, j=j, first=first, last=last: e.matmul(
                            PS[7][hh * 64:(hh + 1) * 64, 0:n], ONES[:, 0:64], tbt[:, j * n:(j + 1) * n], start=first, stop=last))
                    rk += list(vkeys)
                em.op('pe', fns, reads=rk, writes=(['ps%d' % A] if aug else ['ps6', 'ps7']))
                if g0 + len(grp) < nt:
                    return
                if aug:
                    rn = slice(hh * 64, hh * 64 + 64)
                    rd = slice((1 - hh) * 64, (1 - hh) * 64 + 64)
                    tfx = tf[4 + hh]
                    em.op('dve', lambda e: e.reciprocal(tfx[rn, 0:n], PS[A][rd, 0:n]), writes=['ps%d' % A, 'tf%d' % (4 + hh)])
                    em.op('dve', lambda e: e.tensor_tensor(oT[rn, c, qsl], tfx[rn, 0:n], PS[A][rn, 0:n], ALU.mult),
                          reads=['tf%d' % (4 + hh)], writes=['ps%d' % A, (okey, th)])
                elif hh == 1:
                    em.op('dve', lambda e: e.reciprocal(tf[0][:, 0:n], PS[7][:, 0:n]), writes=['ps7', 'tf0'])
                    em.op('dve', lambda e: e.tensor_tensor(oT[:, c, qsl], PS[6][:, 0:n], tf[0][:, 0:n], ALU.mult),
                          reads=['tf0'], writes=['ps6', (okey, th)])

            st_index = {id(st): k for k, st in enumerate(stages)}
            lag = 2 if aug else 1
            for k, st in enumerate(stages):
                emit_qk(st)
                if k - lag >= 0:
                    emit_pv(stages[k - lag])
            for k in range(max(len(stages) - lag, 0), len(stages)):
                emit_pv(stages[k])

        def attention_prompt():
            for s in range(NT // SEQ):
                th = s // 2
                qs = slice(s * SEQ, (s + 1) * SEQ)

                def tiles_a(c, hh, s=s):
                    h = 2 * c + hh
                    return [(Kslab[hh * 64:(hh + 1) * 64, c, (2 * s + kt) * 128:(2 * s + kt + 1) * 128], ['Kslab'],
                             Vslab[:, 2 * s + kt, h * 64:(h + 1) * 64], ['Vslab'], None) for kt in range(2)]

                def tiles_b(c, hh, s=s):
                    return [(kbT_all[:, (2 * s + kt) * 128:(2 * s + kt + 1) * 128], ['kbT_all'],
                             vaug[:, 2 * s + kt, hh, :], ['vb_all'], None) for kt in range(2)]

                attn_generic(qaT, 'qaT', 2, qs, 256, th, tiles_a, oaT, 'oaT')
                attn_generic(qbT, 'qbT', 4, qs, 256, th, tiles_b, obT, 'obT', aug=True)

        def attention_sample(l, blk):
            sb0 = max(8 * blk - 2, 0)
            sb1 = min(8 * blk + 10, 32)
            nsl = sb1 - sb0
            em.dma('sp', Kslab[:, :, 0:nsl * 128], kaT_scr[:, :, sb0 * 128:sb1 * 128], reads=['kaT_scr'], writes=['Kslab'])
            em.dma('pool', Kslab[:, :, 1536:2048], cnak_in[l], writes=['Kslab'])
            em.dma('sp', Vslab[:, 0:nsl, :], va_scr[sb0 * 128:sb1 * 128, :].rearrange("(i p) c -> p i c", p=128),
                   reads=['va_scr'], writes=['Vslab'])
            em.dma('pool', Vslab[:, 12:16, :], cnav_in[l].rearrange("(i p) c -> p i c", p=128), writes=['Vslab'])
            em.dma('pool', nabI, nab_in[l][0], writes=['nabI'])
            if blk == NBLK - 1:
                em.dma('pool', kbT_all[:, NTS:NTS + 512], cgk_in[l], writes=['kbT_all'])
                cgv3 = cgv_in[l].rearrange("(i p) c -> p i c", p=128)
                em.dma('pool', vaug[:, 32:36, 0, 0:64], cgv3[:, :, 0:64], writes=['vb_all'])
                em.dma('pool', vaug[:, 32:36, 1, 64:128], cgv3[:, :, 64:128], writes=['vb_all'])
            for pr in range(8):
                r = 16 * blk + 2 * pr
                t0 = min(max((r - 4) // 2, 0), 27)
                var = {0: 1, 2: 2, 60: 3, 62: 4}.get(r, 0)
                if var == 0:
                    nb, nbkey = nabI, 'nabI'
                else:
                    em.dma('pool', nabE, nab_in[l][var], writes=['nabE'])
                    nb, nbkey = nabE, 'nabE'
                qsl = slice(pr * 128, (pr + 1) * 128)
                th = pr // 4

                def tiles_a(c, hh, t0=t0, nb=nb, nbkey=nbkey):
                    h = 2 * c + hh
                    tl = []
                    for j in range(5):
                        si = t0 + j - sb0
                        tl.append((Kslab[hh * 64:(hh + 1) * 64, c, si * 128:(si + 1) * 128], ['Kslab', nbkey],
                                   Vslab[:, si, h * 64:(h + 1) * 64], ['Vslab'],
                                   nb[:, j * 512 + h * 128: j * 512 + (h + 1) * 128]))
                    for j in range(4):
                        tl.append((Kslab[hh * 64:(hh + 1) * 64, c, 1536 + j * 128:1536 + (j + 1) * 128], ['Kslab'],
                                   Vslab[:, 12 + j, h * 64:(h + 1) * 64], ['Vslab'], None))
                    return tl

                attn_generic(qaT, 'qaT', 2, qsl, 128, th, tiles_a, oaT, 'oaT')
            for th in range(2):
                qsl = slice(th * 512, (th + 1) * 512)

                def tiles_b(c, hh):
                    return [(kbT_all[:, kt * 128:(kt + 1) * 128], ['kbT_all'],
                             vaug[:, kt, hh, :], ['vb_all'], None) for kt in range(36)]

                attn_generic(qbT, 'qbT', 4, qsl, 512, th, tiles_b, obT, 'obT', aug=True)

        def gla_decay(i, th, d, slot):
            TRI = TRI_F if d == 0 else TRI_B
            tsl = slice(i * 128, (i + 1) * 128)
            em.op('pe', [lambda e, hp=hp: e.matmul(
                PS[4][:, hp * 128:(hp + 1) * 128], latok[:, i, d * 256 + hp * 128:d * 256 + (hp + 1) * 128],
                TRI, start=True, stop=True) for hp in range(2)],
                reads=[('latok', i), 'cb'], writes=['ps4'])
            em.op('act', lambda e: e.activation(tf[0][:, 0:256], PS[4][:, 0:256], AF.Exp), writes=['ps4', 'tf0'])
            em.op('act', lambda e: e.activation(tf[1][:, 0:256], PS[4][:, 0:256], AF.Exp, scale=-1.0), writes=['ps4', 'tf1'])
            qtt = qtil[d][slot]
            em.op('dve', lambda e: e.tensor_tensor(
                qtt[:].rearrange("p (c t) -> p c t", c=2), qcT[:, :, tsl],
                tf[0][:, 0:256].rearrange("p (c t) -> p c t", c=2), ALU.mult),
                reads=[('qcT', th), 'tf0'], writes=[('qtil', d, slot)])
            em.op('dve', lambda e: e.tensor_tensor(
                tb[2][:, 0:256].rearrange("p (c t) -> p c t", c=2), kcT[:, :, tsl],
                tf[1][:, 0:256].rearrange("p (c t) -> p c t", c=2), ALU.mult),
                reads=[('kcT', th), 'tf1'], writes=['tb2'])
            att = attm[d][slot]
            for hh, bank in ((0, 5), (1, 3)):
                em.op('pe', [lambda e, hp=hp: e.matmul(
                    PS[bank][:, hp * 128:(hp + 1) * 128],
                    tb[2][hh * 64:hh * 64 + 64, hp * 128:hp * 128 + 128],
                    qtt[hh * 64:hh * 64 + 64, hp * 128:hp * 128 + 128],
                    start=True, stop=True) for hp in range(2)],
                    reads=['tb2', ('qtil', d, slot)], writes=['ps%d' % bank])
                em.op('dve', lambda e: e.tensor_tensor(
                    att[:].rearrange("p (hp hh t) -> p hp hh t", hp=2, hh=2)[:, :, hh, :],
                    PS[bank][:, 0:256].rearrange("p (hp t) -> p hp t", hp=2),
                    MASK4[d][:, 0:256].rearrange("p (hp t) -> p hp t", hp=2), ALU.mult),
                    reads=['cb'], writes=['ps%d' % bank, ('attm', d, slot)])

        def gla_chain(i, d, slot, save_to=None):
            TRIX = TRIX_F if d == 0 else TRIX_B
            em.op('act', lambda e: e.copy(Sprev[d][slot][:], S32[d][:]), reads=[('S32', d)], writes=[('Sprev', d, slot)])
            if save_to is not None:
                em.dma('sp', save_to, Sprev[d][slot][:].rearrange("p a b -> p (a b)"), reads=[('Sprev', d, slot)],
                       writes=['sprev_scr'])
            em.op('pe', [lambda e, hp=hp: e.matmul(
                PS[7][:, 128 + hp:129 + hp], latok[:, i, d * 256 + hp * 128:d * 256 + (hp + 1) * 128], ONES[:, 0:1],
                start=True, stop=True) for hp in range(2)], reads=[('latok', i), 'cb'], writes=['ps7'])
            em.op('act', lambda e: e.activation(dec[d][:, 0:2], PS[7][:, 128:130], AF.Exp), writes=['ps7', ('dec', d)])
            em.op('pe', lambda e: e.matmul(PS[4][:, 0:256], TRIX, latok[:, i, d * 256:(d + 1) * 256], start=True, stop=True),
                  reads=[('latok', i), 'cb'], writes=['ps4'])
            em.op('act', lambda e: e.activation(tf[2][:, 0:256], PS[4][:, 0:256], AF.Exp), writes=['ps4', 'tf2'])
            em.op('dve', lambda e: e.tensor_tensor(tb[0][:, 0:256], ktok[:, i, :], tf[2][:, 0:256], ALU.mult),
                  reads=[('ktok', i), 'tf2'], writes=['tb0'])
            em.op('pe', [lambda e, h=h: e.matmul(
                PS[7][(h % 2) * 64:(h % 2) * 64 + 64, (h // 2) * 64:(h // 2) * 64 + 64],
                tb[0][:, h * 64:(h + 1) * 64], vtok[:, i, h * 64:(h + 1) * 64],
                start=True, stop=True) for h in range(4)],
                reads=['tb0', ('vtok', i)], writes=['ps7'])
            for hp in range(2):
                em.op('dve', lambda e, hp=hp: e.scalar_tensor_tensor(
                    S32[d][:, hp, :], S32[d][:, hp, :], dec[d][:, hp:hp + 1], PS[7][:, hp * 64:(hp + 1) * 64],
                    ALU.mult, ALU.add), reads=[('dec', d)], writes=['ps7', ('S32', d)])

        def gla_out(i, th, slot, use_inter, V_NRM):
            tsl = slice(i * 128, (i + 1) * 128)
            fns = []
            for h in range(4):
                hh, hp = h % 2, h // 2
                outp = PS[6][hh * 64:hh * 64 + 64, hp * 128:(hp + 1) * 128]
                seqm = []
                for d in range(2):
                    seqm.append((vtok[:, i, h * 64:(h + 1) * 64], attm[d][slot][:, h * 128:(h + 1) * 128]))
                    if use_inter[d]:
                        seqm.append((Sprev[d][slot][hh * 64:hh * 64 + 64, hp, :],
                                     qtil[d][slot][hh * 64:hh * 64 + 64, hp * 128:(hp + 1) * 128]))
                for k, (lt, rh) in enumerate(seqm):
                    fns.append(lambda e, outp=outp, lt=lt, rh=rh, k=k, n=len(seqm): e.matmul(
                        outp, lt, rh, start=(k == 0), stop=(k == n - 1)))
            em.op('pe', fns, reads=[('vtok', i)] + [('attm', d, slot) for d in range(2)] +
                  [('Sprev', d, slot) for d in range(2)] + [('qtil', d, slot) for d in range(2)], writes=['ps6'])
            em.op('act', lambda e: e.activation(tb[1][:, 0:256], PS[6][:, 0:256], AF.Square), writes=['ps6', 'tb1'])
            em.op('dve', lambda e: e.tensor_copy(tf[3][:, 0:256], PS[6][:, 0:256]), writes=['ps6', 'tf3'])
            em.op('pe', lambda e: e.matmul(PS[2][:, 0:256], BLK, tb[1][:, 0:256], start=True, stop=True),
                  reads=['tb1', 'cb'], writes=['ps2'])
            em.op('act', lambda e: e.activation(tf[0][:, 0:256], PS[2][:, 0:256], AF.Ln, bias=epsc[:, 0:1],
                                                scale=1.0 / 64), reads=['epsc'], writes=['ps2', 'tf0'])
            em.op('act', lambda e: e.activation(tf[1][:, 0:256], tf[0][:, 0:256], AF.Exp, scale=-0.5), reads=['tf0'],
                  writes=['tf1'])
            em.op('dve', lambda e: e.scalar_tensor_tensor(tf[4][:, 0:256], tf[3][:, 0:256], vcol(V_NRM, 4),
                                                         tf[1][:, 0:256], ALU.mult, ALU.mult),
                  reads=['tf3', 'tf1', 'vecs'], writes=['tf4'])
            em.op('dve', lambda e: e.tensor_tensor(
                ocT[:, :, tsl], tf[4][:, 0:256].rearrange("p (c t) -> p c t", c=2), rcT[:, :, tsl], ALU.mult),
                reads=['tf4', ('rcT', th)], writes=[('ocT', th)])

        def gla_prompt(l, V_NRM):
            for s in range(NT // SEQ):
                th = s // 2
                for d in range(2):
                    order = [0, 1] if d == 0 else [1, 0]
                    em.op('pool', lambda e, d=d: e.memset(S32[d][:], 0.0), writes=[('S32', d)])
                    for ci in order:
                        i = 2 * s + ci
                        gla_decay(i, th, d, ci)
                        gla_chain(i, d, ci)
                    dst = (o_sf if d == 0 else o_sb)[l][s]
                    em.dma('sp', dst.rearrange("hp p v -> p hp v"), S32[d][:], reads=[('S32', d)])
                for ci in range(2):
                    gla_out(2 * s + ci, th, ci, [ci != 0, ci != 1], V_NRM)

        mrr = [0]

        def merge_outproj(l, ci, xsrc, xdst, xkey, hook_mid=None):
            em.barrier(soft=True)
            for j in range(8):
                wb, wkey = next_wbuf()
                em.dma('pool', wb[:, :, 0:384], w_g[l][j], writes=[wkey])
                brw, brk = brw2[j % 2], 'brw%d' % (j % 2)
                em.dma('pool', brw[:], w_br[l][j], writes=[brk])
                for th in range(2):
                    ts = slice(th * 512, th * 512 + 512)
                    for bi, (oT, okey, k0, nk) in enumerate([(oaT, 'oaT', 0, 2), (obT, 'obT', 2, 4), (ocT, 'ocT', 6, 2)]):
                        ps, pskey = next_ps()
                        proj_fm(wb, wkey, bi * 128, 128, th, ps, pskey)
                        mrr[0] ^= 1
                        sg, sgk = (tf[0], 'tf0') if mrr[0] else (tf[3], 'tf3')
                        pr_, prk = (tf[2], 'tf2') if mrr[0] else (tf[5], 'tf5')
                        bb_ = 3 if mrr[0] else 2
                        em.op('act', lambda e, ps=ps: e.activation(sg[:], ps[:], AF.Sigmoid), writes=[pskey, sgk])
                        em.op('pe', [lambda e, kc=kc, oT=oT, k0=k0, nk=nk: e.matmul(
                            PS[bb_][:], brw[:, k0 + kc, :], oT[:, kc, ts],
                            start=(kc == 0), stop=(kc == nk - 1)) for kc in range(nk)],
                            reads=[brk, (okey, th)], writes=['ps%d' % bb_])
                        if bi == 0:
                            em.op('dve', lambda e: e.tensor_tensor(tf[1][:], PS[bb_][:], sg[:], ALU.mult),
                                  reads=[sgk], writes=['ps%d' % bb_, 'tf1'])
                        else:
                            em.op('dve', lambda e: e.tensor_tensor(pr_[:], PS[bb_][:], sg[:], ALU.mult),
                                  reads=[sgk], writes=['ps%d' % bb_, prk])
                            em.op('dve', lambda e: e.tensor_tensor(tf[1][:], tf[1][:], pr_[:], ALU.add),
                                  reads=[prk], writes=['tf1'])
                    em.op('act', lambda e, j=j: e.copy(mT[:, j, ts], tf[1][:]), reads=['tf1'], writes=[('mT', th)])
            if hook_mid is not None:
                hook_mid()
            tiles = [(g, cc, th) for g in range(2) for cc in range(4) for th in range(2)]

            def xbuf(k):
                return (tf[3], 'tf3') if k % 2 == 0 else (tf[4], 'tf4')

            def xload(k):
                g, cc, th = tiles[k]
                xt_, xtk = xbuf(k)
                em.dma('sp', xt_[:], xsrc[:, g * 4 + cc, th * 512:(th + 1) * 512], reads=[xkey, 'xs2'], writes=[xtk])

            xload(0)
            wb = wkey = None
            for k, (g, cc, th) in enumerate(tiles):
                if cc == 0 and th == 0:
                    wb, wkey = next_wbuf()
                    load_w(wb, wkey, w_out[l][g], 512)
                j = g * 4 + cc
                ts = slice(th * 512, th * 512 + 512)
                ps, pskey = next_ps()
                proj_fm(wb, wkey, cc * 128, 128, th, ps, pskey, rhsT=mT, rkey='mT')
                if k + 1 < len(tiles):
                    xload(k + 1)
                xt_, xtk = xbuf(k)
                em.op('dve', lambda e, ps=ps, j=j: e.scalar_tensor_tensor(
                    xt_[:], ps[:], modT[:, ci, 16 + j:17 + j], xt_[:], ALU.mult, ALU.add),
                    reads=['modT'], writes=[pskey, xtk])
                em.dma('sp', xdst[:, j, ts], xt_[:], reads=[xtk], writes=[xkey])

        def ffn(l, ci, V_CONV, smp, has_left, has_right, xout, xoutkey):
            em.barrier()
            pending_tail = []
            wq = [wbuf[k // 2][:, :, (k % 2) * 256:(k % 2) * 256 + 256] for k in range(4)]
            for fg in range(11):
                nf = 2
                r_ = (fg % 2) * 2
                wa, wakey = wq[r_], 'wq%d' % r_
                em.dma('pool', wa, w_up[l][0][fg], writes=[wakey])
                wg, wgkey = wq[r_ + 1], 'wq%d' % (r_ + 1)
                em.dma('pool', wg, w_up[l][1][fg], writes=[wgkey])
                for cc in range(nf):
                    f = fg * 2 + cc
                    if f % 2 == 0:
                        ACC, ACCK = [tf[0], tf[1], tf[2], tf[3]], ['tf0', 'tf1', 'tf2', 'tf3']
                        SIL, SILK = tf[4], 'tf4'
                    else:
                        ACC, ACCK = [trC[0], trC[1], trS[0], trS[1]], ['trC0', 'trC1', 'trS0', 'trS1']
                        SIL, SILK = tf[5], 'tf5'
                    for part, (wb, wkey) in enumerate([(wa, wakey), (wg, wgkey)]):
                        if part == 1 and len(pending_tail) > 0:
                            pending_tail.pop(0)()
                        cbase = V_CONV + part * 4 * NFF
                        w0 = vcol(cbase, f)
                        w1 = vcol(cbase + NFF, f)
                        w2 = vcol(cbase + 2 * NFF, f)
                        bb = vcol(cbase + 3 * NFF, f)
                        pss = []
                        pb = 0 if part == 0 else 4
                        hb = 2 if part == 0 else 3
                        for th in range(2):
                            ps, pskey = PS[pb + th], 'ps%d' % (pb + th)
                            proj_fm(wb, wkey, cc * 128, 128, th, ps, pskey)
                            pss.append((ps, pskey))
                        if smp and (has_left or has_right):
                            em.op('pe', [lambda e, kc=kc: e.matmul(PS[hb][:, 0:2], wb[:, kc, cc * 128:(cc + 1) * 128],
                                                                  hTh[:, kc, :], start=(kc == 0), stop=(kc == 7))
                                         for kc in range(8)], reads=[wkey, 'hTh'], writes=['ps%d' % hb])
                        for th in range(2):
                            ps, pskey = pss[th]
                            acc = ACC[part * 2 + th]
                            akey = ACCK[part * 2 + th]
                            em.op('act', lambda e, ps=ps, acc=acc: e.activation(acc[:], ps[:], AF.Identity, bias=bb, scale=w1),
                                  reads=['vecs'], writes=[pskey, akey])
                            if not smp:
                                a3 = acc[:].rearrange("p (s t) -> p s t", s=2)
                                p3 = ps[:].rearrange("p (s t) -> p s t", s=2)
                                em.op('dve', lambda e, a3=a3, p3=p3: e.scalar_tensor_tensor(
                                    a3[:, :, 1:256], p3[:, :, 0:255], w0, a3[:, :, 1:256], ALU.mult, ALU.add),
                                    reads=['vecs'], writes=[pskey, akey])
                                em.op('dve', lambda e, a3=a3, p3=p3: e.scalar_tensor_tensor(
                                    a3[:, :, 0:255], p3[:, :, 1:256], w2, a3[:, :, 0:255], ALU.mult, ALU.add),
                                    reads=['vecs'], writes=[pskey, akey])
                            else:
                                em.op('dve', lambda e, acc=acc, ps=ps: e.scalar_tensor_tensor(
                                    acc[:, 1:512], ps[:, 0:511], w0, acc[:, 1:512], ALU.mult, ALU.add),
                                    reads=['vecs'], writes=[pskey, akey])
                                em.op('dve', lambda e, acc=acc, ps=ps: e.scalar_tensor_tensor(
                                    acc[:, 0:511], ps[:, 1:512], w2, acc[:, 0:511], ALU.mult, ALU.add),
                                    reads=['vecs'], writes=[pskey, akey])
                        if smp:
                            a0, a1 = ACC[part * 2], ACC[part * 2 + 1]
                            k0, k1 = ACCK[part * 2], ACCK[part * 2 + 1]
                            em.op('dve', lambda e: e.scalar_tensor_tensor(
                                a0[:, 511:512], PS[pb + 1][:, 0:1], w2, a0[:, 511:512], ALU.mult, ALU.add),
                                reads=['vecs'], writes=['ps%d' % (pb + 1), k0])
                            em.op('dve', lambda e: e.scalar_tensor_tensor(
                                a1[:, 0:1], PS[pb][:, 511:512], w0, a1[:, 0:1], ALU.mult, ALU.add),
                                reads=['vecs'], writes=['ps%d' % pb, k1])
                            if has_left:
                                em.op('dve', lambda e: e.scalar_tensor_tensor(
                                    a0[:, 0:1], PS[hb][:, 0:1], w0, a0[:, 0:1], ALU.mult, ALU.add),
                                    reads=['vecs'], writes=['ps%d' % hb, k0])
                            if has_right:
                                em.op('dve', lambda e: e.scalar_tensor_tensor(
                                    a1[:, 511:512], PS[hb][:, 1:2], w2, a1[:, 511:512], ALU.mult, ALU.add),
                                    reads=['vecs'], writes=['ps%d' % hb, k1])
                    def tail(f=f, ACC=ACC, ACCK=ACCK):
                        for th in range(2):
                            ts = slice(th * 512, th * 512 + 512)
                            SIL, SILK = (tf[4], 'tf4') if th == 0 else (tf[5], 'tf5')
                            em.op('act', lambda e: e.activation(SIL[:], ACC[2 + th][:], AF.Silu), reads=[ACCK[2 + th]],
                                  writes=[SILK])
                            em.op('dve', lambda e: e.tensor_tensor(actT[:, f, ts], ACC[th][:], SIL[:], ALU.mult),
                                  reads=[ACCK[th], SILK], writes=[('actT', th)])
                    pending_tail.append(tail)
            while pending_tail:
                pending_tail.pop(0)()
            em.barrier(soft=True)
            for j in range(8):
                wdn, wdk = wdn2[j % 2], 'wdn%d' % (j % 2)
                em.dma('pool', wdn[:], w_dn[l][j], writes=[wdk])
                for th in range(2):
                    ts = slice(th * 512, th * 512 + 512)
                    ps, pskey = next_ps()
                    em.op('pe', [lambda e, f=f, ps=ps: e.matmul(ps[:], wdn[:, f, :], actT[:, f, ts],
                                                               start=(f == 0), stop=(f == NFF - 1)) for f in range(NFF)],
                          reads=[wdk, ('actT', th)], writes=[pskey])
                    xo_, xok = (tf[0], 'tf0') if (2 * j + th) % 2 == 0 else (tf[1], 'tf1')
                    em.op('dve', lambda e, ps=ps, j=j: e.scalar_tensor_tensor(
                        xo_[:], ps[:], modT[:, ci, 40 + j:41 + j], xT[:, j, ts], ALU.mult, ALU.add),
                        reads=['modT', 'xT'], writes=[pskey, xok])
                    em.dma('sp', xout[:, j, ts], xo_[:], reads=[xok], writes=[xoutkey])

        try:
          for l in range(n_layers):
            VB = 64 + l * PER_L
            V_BMOD, V_GATT, V_GFFN, V_NRM, V_CONV = VB, VB + 48, VB + 56, VB + 64, VB + 71
            last = (l == n_layers - 1)
            mod_phase(l, V_BMOD, V_GATT, V_GFFN)
            phase(1)
            xp_src = x_in if l == 0 else xp_scr
            em.barrier()
            em.dma('sp', xT, xp_src, reads=['xp'], writes=['xT'])
            rmsnorm_mod(0, 0)
            em.barrier()
            phase(2)
            inproj(l, None, V_NRM)
            phase(4)
            attention_prompt()
            phase(5)
            gla_prompt(l, V_NRM)
            phase(6)
            merge_outproj(l, 0, xp_src, xp_scr, 'xp')
            phase(8)
            em.barrier()
            em.dma('sp', xT, xp_scr, reads=['xp'], writes=['xT'])
            rmsnorm_mod(1, 0)
            ffn(l, 0, V_CONV, False, False, False, y_out if last else xp_scr, 'xp')
            em.barrier()
            phase(9)
            if not do_sample:
                continue
            xs_src = xs_in if l == 0 else xs_scr2
            K_A = [(n_, t_) for n_ in ('kcT', 'rcT') for t_ in range(2)] + \
                  [(n_, i_) for n_ in ('vtok', 'ktok', 'latok') for i_ in range(8)]
            K_C = [(n_, t_) for n_ in ('qaT', 'qbT', 'qcT') for t_ in range(2)]
            K_H = [('hT', 0), ('hT', 1)]
            NSAV = 20480
            em.dma('sp', S32[0][:], sgf_in[l], writes=[('S32', 0)])
            for blk in range(NBLK):
                bs = slice(blk * NT, (blk + 1) * NT)
                if blk == 0:
                    em.barrier()
                em.dma('sp', xT, xs_src[:, :, bs], reads=['xs', 'xs2'], writes=['xT'])
                rmsnorm_mod(0, 1)
                inproj(l, blk, V_NRM)
                for i in range(8):
                    gla_chain(i, 0, 0, save_to=sprev_scr[blk * 8 + i])
                em.dma('sp', sav_arena[blk][:, 0:NSAV], arena[:, 0:NSAV], reads=K_A + K_C, writes=[('sav', blk)])
                em.dma('sp', sav_hT[blk], hT[:].rearrange("p c t -> p (c t)"), reads=K_H, writes=[('savh', blk)])
            phase(10)
            em.barrier()
            em.dma('sp', S32[1][:], sgb_in[l], writes=[('S32', 1)])

            def restore_a(blk):
                em.dma('sp', arena[:, 8192:NSAV], sav_arena[blk][:, 8192:NSAV], reads=[('sav', blk)], writes=K_A)

            def restore_h(blk):
                em.dma('sp', hT[:].rearrange("p c t -> p (c t)"), sav_hT[blk], reads=[('savh', blk)], writes=K_H)

            def restore_c(blk):
                em.dma('sp', arena[:, 0:8192], sav_arena[blk][:, 0:8192], reads=[('sav', blk)],
                       writes=K_C + [('mT', 0), ('mT', 1), ('actT', 0), ('actT', 1)])

            restore_a(NBLK - 1)
            restore_h(NBLK - 1)
            for blk in reversed(range(NBLK)):
                bs = slice(blk * NT, (blk + 1) * NT)
                restore_c(blk)
                attention_sample(l, blk)
                def gla_pre(i):
                    gla_decay(i, i // 4, 0, i % 2)
                    gla_decay(i, i // 4, 1, i % 2)
                    em.dma('sp', Sprev[0][i % 2][:].rearrange("p a b -> p (a b)"), sprev_scr[blk * 8 + i],
                           reads=['sprev_scr'], writes=[('Sprev', 0, i % 2)])

                gla_pre(7)
                for i in reversed(range(8)):
                    gla_chain(i, 1, i % 2)
                    if i > 0:
                        gla_pre(i - 1)
                    gla_out(i, i // 4, i % 2, [True, True], V_NRM)
                nxt = blk - 1
                if nxt >= 0:
                    restore_a(nxt)
                merge_outproj(l, 1, xs_src[:, :, bs], xs_scr[:, :, bs], 'xs',
                              hook_mid=(lambda nxt=nxt: restore_h(nxt)) if nxt >= 0 else None)
            phase(11)
            for blk in range(NBLK):
                bs = slice(blk * NT, (blk + 1) * NT)
                has_left, has_right = blk > 0, blk < NBLK - 1
                if blk == 0:
                    em.barrier()
                em.dma('sp', xT, xs_scr[:, :, bs], reads=['xs'], writes=['xT'])
                if has_left:
                    em.dma('sp', xh[:, :, 0:1], xs_scr[:, :, blk * NT - 1:blk * NT], reads=['xs'], writes=['xh'])
                else:
                    em.op('pool', lambda e: e.memset(xh[:, :, 0:1], 1.0), writes=['xh'])
                if has_right:
                    em.dma('sp', xh[:, :, 1:2], xs_scr[:, :, (blk + 1) * NT:(blk + 1) * NT + 1], reads=['xs'], writes=['xh'])
                else:
                    em.op('pool', lambda e: e.memset(xh[:, :, 1:2], 1.0), writes=['xh'])
                rmsnorm_mod(1, 1)
                rms_cols(1, 1, lambda kc: xh[:, kc, :], lambda kc: hTh[:, kc, :], 2, 'hTh', 'xh')
                ffn(l, 1, V_CONV, True, has_left, has_right, (ys_out if last else xs_scr2)[:, :, bs], 'xs2')
            em.barrier()
        except _Stop:
            pass

        with nc.allow_low_precision("bf16 matmul operands, fp32 accumulation"):
            with nc.allow_non_contiguous_dma("halo columns / small strided loads"):
                em.run()
    return nc


_PROGRAM = {}


def _get_program():
    if 'p' not in _PROGRAM:
        _PROGRAM['p'] = build_program(L, True)
    return _PROGRAM['p']


def _na_bias_tables(rpb):
    NEG = np.float32(-1e30)
    cq = np.arange(64)
    win0 = np.clip(cq - 8, 0, 48)
    ck = np.arange(64)
    in_win = (ck[:, None] >= win0[None, :]) & (ck[:, None] < win0[None, :] + 16)
    dc = np.clip(ck[:, None] - cq[None, :] + 15, 0, 30)
    out = np.full((5, 128, 5, 4, 128), NEG, np.float32)
    for vi, r in enumerate([30, 0, 2, 60, 62]):
        t0 = min(max((r - 4) // 2, 0), 27)
        for j in range(5):
            for a in range(2):
                kr = 2 * (t0 + j) + a
                for bq in range(2):
                    rq = r + bq
                    k0 = min(max(rq - 4, 0), 56)
                    if not (0 <= kr - k0 < 8):
                        continue
                    dr = kr - rq + 7
                    for h in range(4):
                        blk = np.where(in_win, rpb[h, dr][dc], NEG)
                        out[vi, a * 64:(a + 1) * 64, j, h, bq * 64:(bq + 1) * 64] = blk
    return out.reshape(5, 128, 5 * 512)


def _host_layout(inp):
    f = lambda a: np.ascontiguousarray(np.asarray(a, dtype=np.float32))
    w_in = f(inp['w_in'])
    offs = np.cumsum([0, 256, 256, 256, 512, 128, 128, 256, 256, 256, 256, 16, 16, 1024, 1024, 1024])
    (o_qa, o_ka, o_va, o_qb, o_kb, o_vb, o_qc, o_kc, o_vc, o_rc, o_zf, o_zb, o_ga, o_gb, o_gc, _) = offs
    rot = (np.arange(64) + 32) % 64
    qb_cols = np.concatenate([o_qb + h * 64 + np.arange(64) for h in QB_PERM])
    qbr_cols = np.concatenate([o_qb + h * 64 + rot for h in QB_PERM])
    kbr_cols = np.concatenate([o_kb + h * 64 + rot for h in range(2)])
    rng = lambda a, n: np.arange(a, a + n)
    fm_cols = np.concatenate([rng(o_qa, 256), rng(o_ka, 256), qb_cols, qbr_cols, rng(o_kb, 128), kbr_cols,
                              rng(o_qc, 256), rng(o_kc, 256), rng(o_rc, 256), rng(o_zf, 32)])
    assert len(fm_cols) == W_FM
    tm_cols = np.concatenate([rng(o_va, 256), rng(o_vb, 128), rng(o_vc, 256), rng(o_kc, 256)])
    def tile512(w, ncol):
        G = w.shape[2] // ncol
        return np.ascontiguousarray(w.reshape(L, 8, 128, G, ncol).transpose(0, 3, 2, 1, 4))

    wfm_ = w_in[:, :, fm_cols]
    wtm_ = w_in[:, :, tm_cols]
    shared = {
        'w_mod': tile512(f(inp['w_mod']), 512),
        'w_fm': tile512(wfm_[:, :, 0:2560], 512),
        'w_fz': tile512(wfm_[:, :, 2560:2592], 32)[:, 0],
        'w_tm0': tile512(wtm_[:, :, 0:512], 512)[:, 0],
        'w_tm1': tile512(wtm_[:, :, 512:896], 384)[:, 0],
        'w_out': tile512(f(inp['w_out']), 512),
    }
    wg_ = w_in[:, :, o_ga:o_ga + 3072].reshape(L, 8, 128, 3, 8, 128)
    shared['w_g'] = np.ascontiguousarray(wg_.transpose(0, 4, 2, 1, 3, 5).reshape(L, 8, 128, 8, 384))
    wbb_ = f(inp['w_branch_b']).reshape(L, 8, 64, D)[:, QB_PERM].reshape(L, 512, D)
    wbr_ = np.concatenate([f(inp['w_branch_a']), wbb_, f(inp['w_branch_c'])], axis=1)
    shared['w_br'] = np.ascontiguousarray(wbr_.reshape(L, 8, 128, 8, 128).transpose(0, 3, 2, 1, 4))
    wup_ = f(inp['ffn_w_up']).reshape(L, 8, 128, 2, 11, 256)
    shared['w_up'] = np.ascontiguousarray(wup_.transpose(0, 3, 4, 2, 1, 5))
    wdn_ = f(inp['ffn_w_down']).reshape(L, NFF, 128, 8, 128)
    shared['w_dn'] = np.ascontiguousarray(wdn_.transpose(0, 3, 2, 1, 4))
    wg2 = f(inp['gla_wg2'])
    bg = f(inp['gla_bg'])
    wz2 = np.zeros((L, 33, 512), np.float32)
    wz2[:, 0:16, 0:256] = wg2[:, 0]
    wz2[:, 16:32, 256:512] = wg2[:, 1]
    wz2[:, 32, 0:256] = bg[:, 0]
    wz2[:, 32, 256:512] = bg[:, 1]
    shared['w_z2'] = wz2
    vecs = np.zeros((128, NV), np.float32)
    col128 = lambda v: v.reshape(-1, 128).T
    rep64 = lambda v: np.concatenate([v, v])
    for l in range(L):
        b = 64 + l * PER_L
        vecs[:, b:b + 48] = col128(f(inp['b_mod'])[l])
        vecs[:, b + 48:b + 56] = col128(f(inp['g_attn'])[l])
        vecs[:, b + 56:b + 64] = col128(f(inp['g_ffn'])[l])
        vecs[:, b + 64] = rep64(f(inp['na_q_norm'])[l])
        vecs[:, b + 65] = rep64(f(inp['na_k_norm'])[l])
        vecs[:, b + 66] = rep64(f(inp['gqa_q_norm'])[l])
        vecs[:, b + 67] = rep64(f(inp['gqa_k_norm'])[l])
        vecs[:, b + 68] = rep64(f(inp['gla_out_norm'])[l])
        vecs[:, b + 69] = rep64(f(inp['gqa_q_norm'])[l][rot])
        vecs[:, b + 70] = rep64(f(inp['gqa_k_norm'])[l][rot])
        cw = f(inp['ffn_conv_w'])[l]
        cbias = f(inp['ffn_conv_b'])[l]
        for part in range(2):
            cb0 = b + 71 + part * 4 * NFF
            sl = slice(part * DFF, (part + 1) * DFF)
            for k in range(3):
                vecs[:, cb0 + k * NFF:cb0 + (k + 1) * NFF] = col128(cw[k, sl])
            vecs[:, cb0 + 3 * NFF:cb0 + 4 * NFF] = col128(cbias[sl])
    shared['vecs'] = vecs
    s_idx = np.arange(128)[:, None]
    t_idx = np.arange(128)[None, :]
    blk = ((s_idx // 64) == (t_idx // 64))
    mf = (s_idx <= t_idx)
    mb = (s_idx >= t_idx)
    consts = np.concatenate([mf, mb, (s_idx > t_idx), (s_idx < t_idx), blk, np.ones((128, 128), bool), (s_idx == t_idx),
                             mf, mf, mf, mf, mb, mb, mb, mb], axis=1).astype(np.float32)
    assert consts.shape[1] == NCONST
    shared['consts'] = consts
    t = np.arange(NTS)
    n_freq = 16
    inv_freq = 10000.0 ** (-np.arange(n_freq) / n_freq)
    ang = np.concatenate([(t // 64)[:, None] * inv_freq, (t % 64)[:, None] * inv_freq], axis=-1)
    cosT = np.cos(ang).astype(np.float32).T
    sinT = np.sin(ang).astype(np.float32).T
    c64 = np.concatenate([cosT, cosT], 0)
    s64 = np.concatenate([-sinT, sinT], 0)
    shared['ropeC'] = np.ascontiguousarray(np.concatenate([c64, c64], 0))
    shared['ropeS'] = np.ascontiguousarray(np.concatenate([s64, s64], 0))
    rpb = f(inp['na_rpb'])
    shared['nab'] = np.stack([_na_bias_tables(rpb[l]) for l in range(L)], 0)
    return shared


def kernel(**inputs):
    shared = _host_layout(inputs)
    f = lambda a: np.asarray(a, dtype=np.float32)
    xp = f(inputs['x_prompt'])
    xsm = f(inputs['x_sample'])
    c_ctx = f(inputs['c_ctx'])
    cc = f(inputs['c'])
    per_b = []
    for b in range(2):
        d = {}
        d['xsT0'] = np.ascontiguousarray(xsm[b].T.reshape(8, 128, NTS).transpose(1, 0, 2))
        cond = np.stack([c_ctx.reshape(8, 128).T, cc[b].reshape(8, 128).T], axis=-1)
        d['cond'] = np.ascontiguousarray(cond.reshape(128, 16))
        nk = f(inputs['cache_na_k'])[b]
        d['cnak'] = np.ascontiguousarray(nk.reshape(L, 512, 2, 128).transpose(0, 3, 2, 1))
        d['cnav'] = np.ascontiguousarray(f(inputs['cache_na_v'])[b].reshape(L, 512, 256))
        d['cgk'] = np.ascontiguousarray(f(inputs['cache_gqa_k'])[b].reshape(L, 512, 128).transpose(0, 2, 1))
        d['cgv'] = np.ascontiguousarray(f(inputs['cache_gqa_v'])[b].reshape(L, 512, 128))
        for nm, key in (('sgf', 'state_gla_fwd'), ('sgb', 'state_gla_bwd')):
            st = f(inputs[key])[b]
            d[nm] = np.ascontiguousarray(st.reshape(L, 2, 2, 64, 64).transpose(0, 2, 3, 1, 4).reshape(L, 128, 2, 64))
        per_b.append(d)
    in_maps = []
    for c in range(NCORES):
        xs = xp[4 * c:4 * c + 4].reshape(NT, D)
        m = dict(shared)
        m.update(per_b[c // 4])
        m['xT0'] = np.ascontiguousarray(xs.T.reshape(8, 128, NT).transpose(1, 0, 2))
        in_maps.append(m)
    nc = _get_program()
    res = run_bass_kernel_spmd(nc, in_maps, core_ids=list(range(NCORES)))
    R = res.results
    B, S = 32, SEQ
    y_prompt = np.zeros((B, S, D), np.float32)
    y_sample = np.zeros((2, NTS, D), np.float32)
    na_k = np.zeros((B, L, S, 4, 64), np.float32)
    na_v = np.zeros((B, L, S, 4, 64), np.float32)
    gq_k = np.zeros((B, L, S, 2, 64), np.float32)
    gq_v = np.zeros((B, L, S, 2, 64), np.float32)
    s_f = np.zeros((B, L, 4, 64, 64), np.float32)
    s_b = np.zeros((B, L, 4, 64, 64), np.float32)
    for c in range(NCORES):
        r = R[c]
        yT = np.asarray(r['yT'])
        y_prompt[4 * c:4 * c + 4] = yT.transpose(2, 1, 0).reshape(4, S, D)
        if c % 4 == 0:
            y_sample[c // 4] = np.asarray(r['ysT']).transpose(2, 1, 0).reshape(NTS, D)
        nak = np.asarray(r['o_nak'])
        na_k[4 * c:4 * c + 4] = nak.reshape(L, 4, 64, 4, S).transpose(3, 0, 4, 1, 2)
        nav = np.asarray(r['o_nav'])
        na_v[4 * c:4 * c + 4] = nav.reshape(L, 4, S, 4, 64).transpose(1, 0, 2, 3, 4)
        gk = np.asarray(r['o_gk'])
        gq_k[4 * c:4 * c + 4] = gk.reshape(L, 2, 64, 4, S).transpose(3, 0, 4, 1, 2)
        gv = np.asarray(r['o_gv'])
        gq_v[4 * c:4 * c + 4] = gv.reshape(L, 4, S, 2, 64).transpose(1, 0, 2, 3, 4)
        sf = np.asarray(r['o_sf'])
        s_f[4 * c:4 * c + 4] = sf.reshape(L, 4, 4, 64, 64).transpose(1, 0, 2, 3, 4)
        sb = np.asarray(r['o_sb'])
        s_b[4 * c:4 * c + 4] = sb.reshape(L, 4, 4, 64, 64).transpose(1, 0, 2, 3, 4)
    return (y_prompt, y_sample, na_k, na_v, gq_k, gq_v, s_f, s_b)
```
